# Optimizing a Trainium2 kernel written in Bass

```python
import jax, jax.numpy as jnp
from jax import lax
import numpy as np

D_MODEL = 1024
BATCH = 8
SEQ = 2048
DEPTH = 2

CTX_LEN = 256
GRID_W = 64
EPS = 1e-6
NEG_INF = -1e30
ROPE_THETA = 10000.0
D_FF = ((8 * D_MODEL // 3) + 127) // 128 * 128
D_MIX = D_MODEL

A_W = D_MIX // 4
A_VDIM = 64
A_HEADS = A_W // A_VDIM
A_DIM = A_VDIM // 2
A_QK = A_HEADS * 2 * A_DIM
A_QBLOCK = 128
B_W = 3 * D_MIX // 8
B_DIM = 64
B_HEADS = B_W // B_DIM
B_KV_HEADS = B_HEADS // 3
B_WINDOW = 128
B_BLOCK = 128
C_W = D_MIX - A_W - B_W
C_DK = 64
C_DV = 64
C_HEADS = C_W // C_DV
C_CONV = 5
C_CHUNK = 64

IN_SIZES = (A_QK, A_QK, A_W, B_HEADS * B_DIM, B_KV_HEADS * B_DIM, B_KV_HEADS * B_DIM,
            3 * C_W, C_W, 2 * C_HEADS, 2 * C_HEADS)
IN_OFFSETS = tuple(sum(IN_SIZES[:i + 1]) for i in range(len(IN_SIZES) - 1))
IN_COLS = sum(IN_SIZES)

kernel_name = 'hybrid_diffusion_headgroup_trunk'

f32 = jnp.float32


def rms_norm(x, w=None):
    xf = x.astype(f32)
    y = xf * lax.rsqrt(jnp.mean(xf * xf, axis=-1, keepdims=True) + EPS)
    if w is not None:
        y = y * w.astype(f32)
    return y.astype(x.dtype)


def l2norm(x):
    return x * lax.rsqrt(jnp.sum(x * x, axis=-1, keepdims=True) + EPS)


def modulate(h, mod, i):
    return rms_norm(h) * (1.0 + mod[:, :, i + 1]) + mod[:, :, i]


def swiglu(x, w1, w2):
    gate, up = jnp.split(x @ w1, 2, axis=-1)
    return (jax.nn.silu(gate) * up) @ w2


def ffn_half_step(h, mod, i, w1, w2):
    return h + 0.5 * mod[:, :, i + 2] * swiglu(modulate(h, mod, i), w1, w2)


def apply_rope_2d(x):
    n, d = x.shape[1], x.shape[-1]
    rows = n // GRID_W
    row = jnp.repeat(jnp.arange(rows, dtype=f32), GRID_W)
    col = jnp.tile(jnp.arange(GRID_W, dtype=f32), rows)
    half = d // 2
    inv = ROPE_THETA ** (-jnp.arange(0, half, 2, dtype=f32) / half)
    bshape = (n,) + (1,) * (x.ndim - 3) + (half // 2,)

    def rot(xa, pos):
        ang = (pos[:, None] * inv).reshape(bshape)
        cs, sn = jnp.cos(ang), jnp.sin(ang)
        x1, x2 = xa[..., :half // 2], xa[..., half // 2:]
        return jnp.concatenate([x1 * cs - x2 * sn, x2 * cs + x1 * sn], axis=-1)

    xf = x.astype(f32)
    out = jnp.concatenate([rot(xf[..., :half], row), rot(xf[..., half:], col)], axis=-1)
    return out.astype(x.dtype)


def diff_attention(q, k, v, lam):
    s = jnp.einsum('bqhmd,bkhmd->bhmqk', q, k).astype(f32) * (A_DIM ** -0.5)
    p = jax.nn.softmax(s, axis=-1)
    w = (p[:, :, 0] - lam * p[:, :, 1]).astype(v.dtype)
    return jnp.einsum('bhqk,bkhd->bqhd', w, v)


def diff_attention_blocked(q, k, v, lam):
    bn, n = q.shape[:2]
    nb = n // A_QBLOCK
    qb = jnp.moveaxis(q.reshape((bn, nb, A_QBLOCK) + q.shape[2:]), 1, 0)
    ob = lax.map(lambda qq: diff_attention(qq, k, v, lam), qb)
    return jnp.moveaxis(ob, 0, 1).reshape((bn, n) + ob.shape[3:])


def window_sink_attention_latent(q, k, v, k_ctx, v_ctx, sink):
    bn, n = q.shape[:2]
    nb = n // B_BLOCK
    G = B_HEADS // B_KV_HEADS
    L = B_BLOCK
    qb = q.reshape(bn, nb, L, B_KV_HEADS, G, B_DIM)

    def band(t):
        tb = t.reshape(bn, nb, L, B_KV_HEADS, B_DIM)
        tp = jnp.pad(tb, ((0, 0), (1, 1), (0, 0), (0, 0), (0, 0)))
        return jnp.concatenate([tp[:, :-2], tp[:, 1:-1], tp[:, 2:]], axis=2)

    kb, vb = band(k), band(v)
    qpos = jnp.arange(n).reshape(nb, L)
    kpos = (jnp.arange(nb)[:, None] - 1) * L + jnp.arange(3 * L)[None, :]
    valid = ((kpos >= 0) & (kpos < n))[:, None, :] & (jnp.abs(qpos[:, :, None] - kpos[:, None, :]) <= B_WINDOW)
    scale = B_DIM ** -0.5
    s_band = jnp.einsum('bnqhgd,bnkhd->bnhgqk', qb, kb).astype(f32) * scale
    s_band = jnp.where(valid[None, :, None, None], s_band, NEG_INF)
    s_ctx = jnp.einsum('bnqhgd,bkhd->bnhgqk', qb, k_ctx).astype(f32) * scale
    s_sink = jnp.broadcast_to(sink.astype(f32).reshape(1, 1, B_KV_HEADS, G, 1, 1), s_ctx.shape[:-1] + (1,))
    p = jax.nn.softmax(jnp.concatenate([s_sink, s_ctx, s_band], axis=-1), axis=-1)
    m = k_ctx.shape[1]
    p_ctx = p[..., 1:1 + m].astype(v.dtype)
    p_band = p[..., 1 + m:].astype(v.dtype)
    o = (jnp.einsum('bnhgqk,bkhd->bnqhgd', p_ctx, v_ctx)
         + jnp.einsum('bnhgqk,bnkhd->bnqhgd', p_band, vb))
    return o.reshape(bn, n, B_HEADS * B_DIM)


def sink_attention_context(q, k, v, sink):
    bn, m = q.shape[:2]
    G = B_HEADS // B_KV_HEADS
    qg = q.reshape(bn, m, B_KV_HEADS, G, B_DIM)
    s = jnp.einsum('bqhgd,bkhd->bhgqk', qg, k).astype(f32) * (B_DIM ** -0.5)
    s_sink = jnp.broadcast_to(sink.astype(f32).reshape(1, B_KV_HEADS, G, 1, 1), s.shape[:-1] + (1,))
    p = jax.nn.softmax(jnp.concatenate([s_sink, s], axis=-1), axis=-1)[..., 1:]
    o = jnp.einsum('bhgqk,bkhd->bqhgd', p.astype(v.dtype), v)
    return o.reshape(bn, m, B_HEADS * B_DIM)


def short_conv(x, w):
    return lax.conv_general_dilated(x, w[:, None, :].astype(x.dtype), window_strides=(1,), padding='SAME',
                                    dimension_numbers=('NWC', 'WIO', 'NWC'), feature_group_count=x.shape[-1])


def gdn_chunk_scan(q, k, v, g, beta, s0):
    bn, T, H, _ = q.shape
    dv = v.shape[-1]
    L = C_CHUNK
    nc = T // L

    def chunk(t):
        return jnp.swapaxes(t.reshape((bn, nc, L) + t.shape[2:]), 2, 3)

    q, k, v, g, beta = chunk(q), chunk(k), chunk(v), chunk(g), chunk(beta)
    Gc = jnp.cumsum(g, axis=-1)
    incl = jnp.tril(jnp.ones((L, L), bool))
    strict = jnp.tril(jnp.ones((L, L), bool), -1)
    decay = jnp.exp(jnp.where(incl, Gc[..., :, None] - Gc[..., None, :], NEG_INF))
    kk = jnp.einsum('bchid,bchjd->bchij', k, k)
    A = jnp.where(strict, beta[..., :, None] * kk * decay, 0.0)
    eye = jnp.eye(L, dtype=f32)
    Tm = lax.linalg.triangular_solve(A + eye, jnp.broadcast_to(eye, A.shape), left_side=True, lower=True)
    eG = jnp.exp(Gc)[..., None]
    u = Tm @ (beta[..., None] * v)
    w = Tm @ (beta[..., None] * eG * k)
    qk = jnp.where(incl, jnp.einsum('bchid,bchjd->bchij', q, k) * decay, 0.0)
    q_dec = q * eG
    k_dec = k * jnp.exp(Gc[..., -1:] - Gc)[..., None]
    g_last = jnp.exp(Gc[..., -1])
    xs = tuple(jnp.moveaxis(t, 1, 0) for t in (u, w, qk, q_dec, k_dec, g_last))

    def step(S, inp):
        u_c, w_c, qk_c, qd_c, kd_c, gl_c = inp
        v_new = u_c - w_c @ S
        o_c = qd_c @ S + qk_c @ v_new
        S = S * gl_c[..., None, None] + jnp.einsum('bhld,bhle->bhde', kd_c, v_new)
        return S, o_c

    s_fin, o = lax.scan(step, s0, xs)
    o = jnp.transpose(o, (1, 0, 3, 2, 4)).reshape(bn, T, H, dv)
    return o, s_fin


def gdn_inputs(p, conv_w):
    bn, T = p[6].shape[:2]
    qkv = jax.nn.silu(short_conv(p[6], conv_w)).astype(f32)
    q, k, v = jnp.split(qkv, 3, axis=-1)
    q = l2norm(q.reshape(bn, T, C_HEADS, C_DK)) * (C_DK ** -0.5)
    k = l2norm(k.reshape(bn, T, C_HEADS, C_DK))
    v = v.reshape(bn, T, C_HEADS, C_DV)
    a = p[8].astype(f32).reshape(bn, T, 2, C_HEADS)
    b = p[9].astype(f32).reshape(bn, T, 2, C_HEADS)
    return q, k, v, a, b


def gdn_bidirectional(ctx_in, lat_in, A_log, dt_bias):
    qc, kc, vc, ac, bc = ctx_in
    ql, kl, vl, al, bl = lat_in
    bn = ql.shape[0]
    o_c = 0.0
    o_l = 0.0
    for d in range(2):
        fl = (lambda t: t[:, ::-1]) if d == 1 else (lambda t: t)
        a_rate = jnp.exp(A_log[d].astype(f32))
        dtb = dt_bias[d].astype(f32)
        gc = -a_rate * jax.nn.softplus(ac[:, :, d] + dtb)
        gl = -a_rate * jax.nn.softplus(al[:, :, d] + dtb)
        s0 = jnp.zeros((bn, C_HEADS, C_DK, C_DV), f32)
        oc, s_ctx = gdn_chunk_scan(fl(qc), fl(kc), fl(vc), fl(gc), fl(jax.nn.sigmoid(bc[:, :, d])), s0)
        ol, _ = gdn_chunk_scan(fl(ql), fl(kl), fl(vl), fl(gl), fl(jax.nn.sigmoid(bl[:, :, d])), s_ctx)
        o_c = o_c + fl(oc)
        o_l = o_l + fl(ol)
    return o_c, o_l


def hybrid_mixer(hn_ctx, hn_lat, w_in, w_out, a_qnorm, a_knorm, a_lambda, a_subln, lam_init,
                 b_qnorm, b_knorm, b_sink, c_conv, c_A_log, c_dt_bias, c_onorm, need_ctx):
    dt = hn_lat.dtype
    bn, n = hn_lat.shape[:2]
    m = hn_ctx.shape[1]
    pl = jnp.split(hn_lat @ w_in, IN_OFFSETS, axis=-1)
    pc = jnp.split(hn_ctx @ w_in, IN_OFFSETS, axis=-1)

    def a_heads(t, w):
        return rms_norm(t.reshape(t.shape[:2] + (A_HEADS, 2, A_DIM)), w)

    qa_l = apply_rope_2d(a_heads(pl[0], a_qnorm))
    ka_l = apply_rope_2d(a_heads(pl[1], a_knorm))
    qa_c = a_heads(pc[0], a_qnorm)
    ka_c = a_heads(pc[1], a_knorm)
    va_l = pl[2].reshape(bn, n, A_HEADS, A_VDIM)
    va_c = pc[2].reshape(bn, m, A_HEADS, A_VDIM)
    lf = a_lambda.astype(f32)
    lam = jnp.exp(jnp.sum(lf[0] * lf[1])) - jnp.exp(jnp.sum(lf[2] * lf[3])) + lam_init

    def a_out(o):
        return (rms_norm(o, a_subln) * (1.0 - lam_init)).reshape(o.shape[:2] + (A_W,)).astype(dt)

    ya_l = a_out(diff_attention_blocked(qa_l, jnp.concatenate([ka_c, ka_l], axis=1),
                                        jnp.concatenate([va_c, va_l], axis=1), lam))

    qb_l = apply_rope_2d(rms_norm(pl[3].reshape(bn, n, B_HEADS, B_DIM), b_qnorm))
    kb_l = apply_rope_2d(rms_norm(pl[4].reshape(bn, n, B_KV_HEADS, B_DIM), b_knorm))
    vb_l = pl[5].reshape(bn, n, B_KV_HEADS, B_DIM)
    kb_c = rms_norm(pc[4].reshape(bn, m, B_KV_HEADS, B_DIM), b_knorm)
    vb_c = pc[5].reshape(bn, m, B_KV_HEADS, B_DIM)
    yb_l = window_sink_attention_latent(qb_l, kb_l, vb_l, kb_c, vb_c, b_sink).astype(dt)

    oc_c, oc_l = gdn_bidirectional(gdn_inputs(pc, c_conv), gdn_inputs(pl, c_conv), c_A_log, c_dt_bias)

    def c_out(o, gate):
        gz = jax.nn.silu(gate.astype(f32).reshape(o.shape))
        return (rms_norm(o, c_onorm) * gz).reshape(o.shape[:2] + (C_W,)).astype(dt)

    yc_l = c_out(oc_l, pl[7])
    y_lat = jnp.concatenate([ya_l, yb_l, yc_l], axis=-1) @ w_out

    if not need_ctx:
        return None, y_lat
    ya_c = a_out(diff_attention(qa_c, ka_c, va_c, lam))
    qb_c = rms_norm(pc[3].reshape(bn, m, B_HEADS, B_DIM), b_qnorm)
    yb_c = sink_attention_context(qb_c, kb_c, vb_c, b_sink).astype(dt)
    yc_c = c_out(oc_c, pc[7])
    y_ctx = jnp.concatenate([ya_c, yb_c, yc_c], axis=-1) @ w_out
    return y_ctx, y_lat


def setup_inputs(seed: int = 0) -> dict:
    key = jax.random.key(seed)
    ks = jax.random.split(key, 24)
    D = D_MODEL

    def nrm(k, shape, scale):
        return jax.random.normal(k, shape, f32) * scale

    dt = jnp.exp(jax.random.uniform(ks[22], (DEPTH, 2, C_HEADS), f32, np.log(1e-3), np.log(1e-1)))
    return {
        'x': nrm(ks[0], (BATCH, SEQ, D), 1.0),
        'c': nrm(ks[1], (BATCH, D), 1.0),
        'ctx': nrm(ks[2], (BATCH, CTX_LEN, D), 1.0),
        'c_ctx': nrm(ks[3], (D,), 1.0),
        'w_mod': nrm(ks[4], (DEPTH, D, 9 * D), 0.5 * D ** -0.5),
        'b_mod': nrm(ks[5], (DEPTH, 9 * D), 0.02),
        'ffn1_w1': nrm(ks[6], (DEPTH, D, 2 * D_FF), D ** -0.5),
        'ffn1_w2': nrm(ks[7], (DEPTH, D_FF, D), D_FF ** -0.5),
        'ffn2_w1': nrm(ks[8], (DEPTH, D, 2 * D_FF), D ** -0.5),
        'ffn2_w2': nrm(ks[9], (DEPTH, D_FF, D), D_FF ** -0.5),
        'w_in': nrm(ks[10], (DEPTH, D, IN_COLS), D ** -0.5),
        'w_out': nrm(ks[11], (DEPTH, D_MIX, D), D_MIX ** -0.5),
        'a_qnorm': 1.0 + nrm(ks[12], (DEPTH, A_DIM), 0.02),
        'a_knorm': 1.0 + nrm(ks[13], (DEPTH, A_DIM), 0.02),
        'a_lambda': nrm(ks[14], (DEPTH, 4, A_DIM), 0.1),
        'a_subln': 1.0 + nrm(ks[15], (DEPTH, A_VDIM), 0.02),
        'b_qnorm': 1.0 + nrm(ks[16], (DEPTH, B_DIM), 0.02),
        'b_knorm': 1.0 + nrm(ks[17], (DEPTH, B_DIM), 0.02),
        'b_sink': nrm(ks[18], (DEPTH, B_HEADS), 0.5),
        'c_conv': nrm(ks[19], (DEPTH, C_CONV, 3 * C_W), C_CONV ** -0.5),
        'c_A_log': jnp.log(jax.random.uniform(ks[20], (DEPTH, 2, C_HEADS), f32, 1.0, 16.0)),
        'c_dt_bias': dt + jnp.log(-jnp.expm1(-dt)),
        'c_onorm': 1.0 + nrm(ks[21], (DEPTH, C_DV), 0.02),
    }


def reference(x, c, ctx, c_ctx, w_mod, b_mod, ffn1_w1, ffn1_w2, ffn2_w1, ffn2_w2, w_in, w_out,
              a_qnorm, a_knorm, a_lambda, a_subln, b_qnorm, b_knorm, b_sink,
              c_conv, c_A_log, c_dt_bias, c_onorm):
    h_lat, h_ctx = x, ctx
    silu_c = jax.nn.silu(c)
    silu_cc = jax.nn.silu(c_ctx)
    for l in range(DEPTH):
        last = l == DEPTH - 1
        lam_init = 0.8 - 0.6 * float(np.exp(-0.3 * l))
        mod_lat = (silu_c @ w_mod[l] + b_mod[l]).reshape(c.shape[0], 1, 9, D_MODEL)
        mod_ctx = (silu_cc @ w_mod[l] + b_mod[l]).reshape(1, 1, 9, D_MODEL)
        h_lat = ffn_half_step(h_lat, mod_lat, 0, ffn1_w1[l], ffn1_w2[l])
        h_ctx = ffn_half_step(h_ctx, mod_ctx, 0, ffn1_w1[l], ffn1_w2[l])
        y_ctx, y_lat = hybrid_mixer(modulate(h_ctx, mod_ctx, 3), modulate(h_lat, mod_lat, 3),
                                    w_in[l], w_out[l], a_qnorm[l], a_knorm[l], a_lambda[l], a_subln[l], lam_init,
                                    b_qnorm[l], b_knorm[l], b_sink[l], c_conv[l], c_A_log[l], c_dt_bias[l],
                                    c_onorm[l], not last)
        h_lat = h_lat + mod_lat[:, :, 5] * y_lat
        h_lat = ffn_half_step(h_lat, mod_lat, 6, ffn2_w1[l], ffn2_w2[l])
        if not last:
            h_ctx = h_ctx + mod_ctx[:, :, 5] * y_ctx
            h_ctx = ffn_half_step(h_ctx, mod_ctx, 6, ffn2_w1[l], ffn2_w2[l])
    return h_lat
```

```python
import contextlib
import os
import numpy as np
import ml_dtypes
import concourse.bass as bass
import concourse.mybir as mybir
from concourse.bass_utils import run_bass_kernel_spmd

F32 = mybir.dt.float32
BF16 = mybir.dt.bfloat16
AF = mybir.ActivationFunctionType
ALU = mybir.AluOpType
AX = mybir.AxisListType

D = 1024; T = 2304; NCTX = 256; NLAT = 2048; NT = 18; DFF = 2816; NF = 22
DEPTH = 2
INC = 2968
OFF_QA, OFF_KA, OFF_VA, OFF_QB, OFF_KB, OFF_VB, OFF_C, OFF_G, OFF_A, OFF_B = 0, 256, 512, 768, 1152, 1280, 1408, 2560, 2944, 2956
BT = [(0, 256), (256, 512), (768, 512), (1280, 512), (1792, 512)]
EPS = 1e-6
SAME_ENG_SYNC = True


class Op:
    __slots__ = ("eng", "fn", "rk", "wk", "dma", "deps", "signal", "count", "semkey")


class Prog:
    def __init__(self, nc):
        self.nc = nc
        self.ops = []
        self.lastw = {}
        self.readers = {}
        self.last_eng = {}

    def add(self, eng, fn, reads=(), writes=(), dma=None):
        op = Op()
        op.eng = eng; op.fn = fn; op.dma = dma
        op.rk = set(reads); op.wk = set(writes)
        if dma is not None:
            op.rk.add(("slot", dma)); op.wk.add(("slot", dma))
        op.signal = dma is not None; op.count = 0
        op.semkey = ("dma", dma) if dma is not None else eng
        deps = []
        seen = set()

        def consider(d, raw):
            if d is None or id(d) in seen:
                return
            if d.dma is None and dma is None and d.eng == eng:
                if eng == "pe" or not raw or not SAME_ENG_SYNC:
                    return
            seen.add(id(d)); deps.append(d); d.signal = True

        for k in op.rk:
            consider(self.lastw.get(k), True)
        for k in op.wk:
            consider(self.lastw.get(k), False)
            for r in self.readers.get(k, ()):
                consider(r, False)
        op.deps = deps
        for k in op.wk:
            self.lastw[k] = op
            self.readers[k] = []
        for k in op.rk:
            if k not in op.wk:
                self.readers.setdefault(k, []).append(op)
        self.ops.append(op)
        if dma is None:
            self.last_eng[eng] = op
        return op

    def barrier(self, engs=("pe", "act", "dve")):
        lasts = [self.last_eng[e] for e in engs if e in self.last_eng]
        for e in tuple(engs) + ("sp",):
            op = Op()
            op.eng = e; op.fn = None; op.dma = None; op.rk = set(); op.wk = set()
            op.signal = False; op.count = 0; op.semkey = e
            op.deps = [d for d in lasts if d.eng != e]
            for d in op.deps:
                d.signal = True
            self.ops.append(op)

    def emit(self, stack):
        nc = self.nc
        semkeys = []
        for op in self.ops:
            if op.signal and op.semkey not in semkeys:
                semkeys.append(op.semkey)
        sems = {}
        for i, k in enumerate(semkeys):
            sems[k] = stack.enter_context(nc.semaphore("s%d" % i))
        cnt = {}
        for op in self.ops:
            if op.signal:
                inc = 16 if op.dma is not None else 1
                cnt[op.semkey] = cnt.get(op.semkey, 0) + inc
                op.count = cnt[op.semkey]
        per = {"pe": [], "act": [], "dve": [], "pool": [], "sp": []}
        for op in self.ops:
            per[op.eng].append(op)
        block = stack.enter_context(nc.Block())

        def run(e, lst):
            waited = {}
            for op in lst:
                need = {}
                for d in op.deps:
                    if d.count > need.get(d.semkey, 0):
                        need[d.semkey] = d.count
                for k, v in need.items():
                    if waited.get(k, 0) < v:
                        e.wait_ge(sems[k], v)
                        waited[k] = v
                if op.fn is None:
                    continue
                try:
                    inst = op.fn(e)
                except BaseException:
                    print('FAILED OP', op.eng, sorted(map(str, op.rk)), sorted(map(str, op.wk)))
                    raise
                if op.signal:
                    inst.then_inc(sems[op.semkey], 16 if op.dma is not None else 1)

        @block.tensor
        def _(e):
            run(e, per["pe"])

        @block.scalar
        def _(e):
            run(e, per["act"])

        @block.vector
        def _(e):
            run(e, per["dve"])

        @block.gpsimd
        def _(e):
            run(e, per["pool"])

        @block.sync
        def _(e):
            run(e, per["sp"])


def ks(name, *ranges):
    out = [(name,)]
    for r in ranges:
        if isinstance(r, int):
            r = [r]
        out = [o + (i,) for o in out for i in r]
    return out


def _rope_tables(dim):
    half = dim // 2
    nf = half // 2
    inv = 10000.0 ** (-np.arange(0, half, 2, dtype=np.float32) / half)
    pos_row = np.repeat(np.arange(32, dtype=np.float32), 64)
    pos_col = np.tile(np.arange(64, dtype=np.float32), 32)
    cos = np.zeros((128, NLAT), np.float32); sin = np.zeros((128, NLAT), np.float32)
    perm = np.zeros((128, 128), np.float32)
    for p in range(128):
        e = p % dim
        hid = e // half
        w = e % half
        f = w % nf
        second = w // nf
        pos = pos_row if hid == 0 else pos_col
        ang = (pos * inv[f]).astype(np.float32)
        cos[p] = np.cos(ang)
        sin[p] = np.sin(ang) * (1.0 if second else -1.0)
        partner = p - nf if second else p + nf
        perm[partner, p] = 1.0
    return cos, sin, perm


def _consts():
    c = {}
    i = np.arange(128)
    I = (i[:, None] == i[None, :]).astype(np.float32)
    c["ident"] = I
    bd32 = ((i[:, None] // 32) == (i[None, :] // 32)).astype(np.float32)
    bd64 = ((i[:, None] // 64) == (i[None, :] // 64)).astype(np.float32)
    c["bd32"] = bd32; c["bd64"] = bd64; c["ones"] = np.ones((128, 128), np.float32)
    cosA, sinA, permA = _rope_tables(32)
    cosB, sinB, permB = _rope_tables(64)
    c["permA"] = permA; c["permB"] = permB
    c["cosA"] = cosA; c["sinA"] = sinA; c["cosB"] = cosB; c["sinB"] = sinB
    c["mprev"] = (i[:, None] >= i[None, :]).astype(np.float32)
    c["mnext"] = (i[:, None] <= i[None, :]).astype(np.float32)
    le = (i[:, None] <= i[None, :]).astype(np.float32)
    ge = (i[:, None] >= i[None, :]).astype(np.float32)
    lt = (i[:, None] < i[None, :]).astype(np.float32)
    gt = (i[:, None] > i[None, :]).astype(np.float32)
    c["tri_f"] = le
    c["tri_b"] = ge
    c["msk4"] = np.concatenate([gt, lt, gt, lt], axis=1)
    c["strict4"] = np.concatenate([gt, lt, gt, lt], axis=1)
    c["incl4"] = np.concatenate([ge, le, ge, le], axis=1)
    c["bd32_4"] = np.tile(bd32, (1, 4))
    c["m1_4"] = np.tile(bd64 - bd32, (1, 4))
    c["m2_4"] = np.tile(1.0 - bd64, (1, 4))
    c["ident4"] = np.tile(I, (1, 4))
    return c


CONST_F32 = ["ident", "permA", "permB", "tri_f", "tri_b", "msk4", "strict4", "incl4", "bd32_4", "m1_4", "m2_4", "ident4",
             "cosA", "sinA", "cosB", "sinB"]
CONST_BF = ["ident", "bd32", "bd64", "ones", "mprev", "mnext"]


def _pack(names, cdict):
    offs = {}
    cols = []
    o = 0
    for n in names:
        a = cdict[n]
        offs[n] = (o, a.shape[1])
        o += a.shape[1]
        cols.append(a)
    return np.concatenate(cols, axis=1), offs


PB = {}
_o = 0
for _n, _s in [("a_qnorm", 32), ("a_knorm", 32), ("a_lambda", 128), ("a_subln", 64), ("b_qnorm", 64), ("b_knorm", 64),
               ("b_sink", 6), ("c_A_log", 12), ("c_dt_bias", 12), ("c_onorm", 64)]:
    PB[_n] = (_o, _s)
    _o += _s
PBN = _o
PPN = 4 + 45


def build(upto="all", dbg=False):
    nc = bass.Bass("TRN2", target_bir_lowering=False)
    cd = _consts()
    cF, offF = _pack(CONST_F32, cd)
    cB, offB = _pack(CONST_BF, cd)
    NCF = cF.shape[1]; NCB = cB.shape[1]
    NCF_RES = offF["cosA"][0]

    dt = nc.dram_tensor
    xT = dt("xT", [D, T], F32, kind="ExternalInput").ap()
    cT = dt("cT", [128, 16], F32, kind="ExternalInput").ap()
    w_mod = dt("w_mod", [DEPTH, D, 9 * D], F32, kind="ExternalInput").ap()
    b_modT = dt("b_modT", [128, DEPTH * 72], F32, kind="ExternalInput").ap()
    f1w1 = dt("ffn1_w1", [DEPTH, D, 2 * DFF], F32, kind="ExternalInput").ap()
    f1w2 = dt("ffn1_w2", [DEPTH, DFF, D], F32, kind="ExternalInput").ap()
    f2w1 = dt("ffn2_w1", [DEPTH, D, 2 * DFF], F32, kind="ExternalInput").ap()
    f2w2 = dt("ffn2_w2", [DEPTH, DFF, D], F32, kind="ExternalInput").ap()
    w_in = dt("w_in", [DEPTH, D, INC], F32, kind="ExternalInput").ap()
    w_out = dt("w_out", [DEPTH, D, D], F32, kind="ExternalInput").ap()
    pbc = dt("pbc", [128, DEPTH * PBN], F32, kind="ExternalInput").ap()
    ppar = dt("ppar", [128, DEPTH * PPN], F32, kind="ExternalInput").ap()
    constF = dt("constF", [128, NCF], F32, kind="ExternalInput").ap()
    constB = dt("constB", [128, NCB], BF16, kind="ExternalInput").ap()
    outT = dt("outT", [D, NLAT], F32, kind="ExternalOutput").ap()
    if dbg:
        dbgT = dt("dbgT", [D, T], F32, kind="ExternalOutput").ap()
        dbg2 = dt("dbg2", [128, 12800], F32, kind="ExternalOutput").ap()

    stack = contextlib.ExitStack()
    with stack:
        sb = lambda n, s, d: stack.enter_context(nc.sbuf_tensor(n, s, d))
        hT = sb("hT", [128, 8, T], F32)
        hn = sb("hn", [128, 8, T], BF16)
        modT = sb("modT", [128, DEPTH, 2, 72], F32)
        bmod = sb("bmod", [128, DEPTH * 72], F32)
        cTs = sb("cTs", [128, 16], F32)
        pb = sb("pb", [128, DEPTH * PBN], F32)
        pp = sb("pp", [128, DEPTH * PPN], F32)
        kF = sb("kF", [128, NCF_RES], F32)
        kB = sb("kB", [128, NCB], BF16)
        w1b = [sb("w1b%d" % i, [128, 8, 512], BF16) for i in range(2)]
        w2b = [sb("w2b%d" % i, [128, 2, 1024], BF16) for i in range(2)]
        ARENA_W = 12800
        arena = sb("arena", [128, ARENA_W], F32)
        psl = [stack.enter_context(nc.psum_tensor("ps%d" % i, [128, 512], F32)) for i in range(8)]

        def cF_(n):
            o, w = offF[n]
            return kF[:, o:o + w]

        def cB_(n):
            o, w = offB[n]
            return kB[:, o:o + w]

        def carve(off_words, nwords, dtype=F32):
            a = arena[:, off_words:off_words + nwords]
            return a.bitcast(dtype) if dtype != F32 else a

        P = Prog(nc)
        A = P.add

        A("sp", lambda e: e.dma_start(out=cTs[:], in_=cT), writes=ks("cTs"), dma="ld_c")
        A("sp", lambda e: e.dma_start(out=bmod[:], in_=b_modT), writes=ks("bmod"), dma="ld_b")
        A("sp", lambda e: e.dma_start(out=pb[:], in_=pbc), writes=ks("pb"), dma="ld_pb")
        A("sp", lambda e: e.dma_start(out=pp[:], in_=ppar), writes=ks("pp"), dma="ld_pp")
        A("sp", lambda e: e.dma_start(out=kF[:], in_=constF[:, 0:NCF_RES]), writes=ks("kF"), dma="ld_kF")
        A("sp", lambda e: e.dma_start(out=kB[:], in_=constB), writes=ks("kB"), dma="ld_kB")
        for c in range(8):
            A("sp", lambda e, c=c: e.dma_start(out=hT[:, c, :], in_=xT[c * 128:(c + 1) * 128, :]),
              writes=ks("hT", c, range(5)), dma="ld_x%d" % c)

        sc = sb("silu_c", [128, 16], F32)
        A("act", lambda e: e.activation(out=sc[:], in_=cTs[:], func=AF.Silu), reads=ks("cTs"), writes=ks("sc"))
        wm = [carve(i * 4096, 4096).rearrange("p (k n) -> p k n", k=8) for i in range(2)]
        gi = 0
        for l in range(DEPTH):
            for g in range(18):
                buf = wm[gi % 2]; bk = ("wm", gi % 2)
                A("sp", lambda e, buf=buf, l=l, g=g: e.dma_start(
                    out=buf, in_=w_mod[l, :, g * 512:(g + 1) * 512].rearrange("(k p) n -> p k n", p=128)),
                  writes=[bk], dma="wm%d" % (gi % 2))
                ps = psl[gi % 2]
                for n4 in range(4):
                    for k in range(8):
                        A("pe", lambda e, ps=ps, buf=buf, n4=n4, k=k: e.matmul(
                            ps[:, n4 * 2:n4 * 2 + 2], buf[:, k, n4 * 128:(n4 + 1) * 128],
                            sc[:].rearrange("p (w k) -> p k w", w=2)[:, k, :], start=(k == 0), stop=(k == 7)),
                          reads=[bk] + ks("sc"), writes=[("ps", gi % 2)])
                for w in range(2):
                    A("dve", lambda e, ps=ps, l=l, g=g, w=w: e.tensor_tensor(
                        modT[:, l, w, g * 4:(g + 1) * 4], ps[:, 0:8].rearrange("p (n w) -> p n w", w=2)[:, :, w],
                        bmod[:, l * 72 + g * 4: l * 72 + (g + 1) * 4], ALU.add),
                      reads=[("ps", gi % 2)] + ks("bmod"), writes=ks("modT", l))
                gi += 1
            for j in (1, 4, 7):
                A("dve", lambda e, l=l, j=j: e.tensor_scalar(modT[:, l, :, j * 8:(j + 1) * 8], modT[:, l, :, j * 8:(j + 1) * 8],
                                                            1.0, None, ALU.add), reads=ks("modT", l), writes=ks("modT", l))
            for j in (2, 8):
                A("dve", lambda e, l=l, j=j: e.tensor_scalar(modT[:, l, :, j * 8:(j + 1) * 8], modT[:, l, :, j * 8:(j + 1) * 8],
                                                            0.5, None, ALU.mult), reads=ks("modT", l), writes=ks("modT", l))
        P.barrier()

        def mod_sc(l, w, j, c):
            return modT[:, l, w, j * 8 + c: j * 8 + c + 1]

        def norm_mod(l, j, tiles):
            for ti in tiles:
                t0, tl = BT[ti]
                w = 1 if ti == 0 else 0
                sq = carve((ti % 2) * 2048, 2048, BF16).rearrange("p (k n) -> p k n", k=8)
                rs = carve(4096 + (ti % 2) * 512, 512)
                ps = psl[ti % 2]
                for k in range(8):
                    A("act", lambda e, sq=sq, k=k, t0=t0, tl=tl: e.activation(out=sq[:, k, 0:tl], in_=hT[:, k, t0:t0 + tl], func=AF.Square),
                      reads=ks("hT", k, ti), writes=[("sq", ti % 2, k)])
                for k in range(8):
                    A("pe", lambda e, ps=ps, sq=sq, k=k, tl=tl: e.matmul(ps[:, 0:tl], cB_("ones"), sq[:, k, 0:tl], start=(k == 0), stop=(k == 7)),
                      reads=[("sq", ti % 2, k)] + ks("kB"), writes=[("ps", ti % 2)])
                A("act", lambda e, ps=ps, rs=rs, tl=tl: e.activation(out=rs[:, 0:tl], in_=ps[:, 0:tl], func=AF.Sqrt, scale=1.0 / D, bias=epsb[:, 0:1]),
                  reads=[("ps", ti % 2)] + ks("epsb"), writes=[("rs", ti % 2)])
                A("dve", lambda e, rs=rs, tl=tl: e.reciprocal(rs[:, 0:tl], rs[:, 0:tl]), reads=[("rs", ti % 2)], writes=[("rs", ti % 2)])
                for k in range(8):
                    tmp = carve(5120 + (k % 2) * 512, 512)
                    A("dve", lambda e, tmp=tmp, k=k, rs=rs, t0=t0, tl=tl: e.tensor_tensor(tmp[:, 0:tl], hT[:, k, t0:t0 + tl], rs[:, 0:tl], ALU.mult),
                      reads=ks("hT", k, ti) + [("rs", ti % 2)], writes=[("ntmp", k % 2)])
                    A("act", lambda e, tmp=tmp, k=k, t0=t0, tl=tl, w=w: e.activation(
                        out=hn[:, k, t0:t0 + tl], in_=tmp[:, 0:tl], func=AF.Identity, scale=mod_sc(l, w, j + 1, k), bias=mod_sc(l, w, j, k)),
                      reads=[("ntmp", k % 2)] + ks("modT", l), writes=ks("hn", k, ti))

        epsb = sb("epsb", [128, 1], F32)
        oneb = sb("oneb", [128, 1], F32)
        A("dve", lambda e: e.memset(oneb[:], 1.0), writes=["oneb"])
        A("dve", lambda e: e.memset(epsb[:], EPS), writes=ks("epsb"))

        def ffn(l, j, w1d, w2d, tiles):
            norm_mod(l, j, tiles)
            P.barrier()
            for part in range(11):
                wb = w1b[part % 2]; w2 = w2b[part % 2]
                wbv = wb[:].rearrange("p k (g n) -> p k g n", g=2)
                for g in range(2):
                    A("pool", lambda e, wbv=wbv, g=g, part=part: e.dma_start(
                        out=wbv[:, :, g, :], in_=w1d[l, :, g * DFF + part * 256: g * DFF + (part + 1) * 256].rearrange("(k p) n -> p k n", p=128)),
                      writes=[("w1b", part % 2, g)], dma="w1b%d%d" % (part % 2, g))
                A("pool", lambda e, w2=w2, part=part: e.dma_start(
                    out=w2[:], in_=w2d[l, part * 256:(part + 1) * 256, :].rearrange("(c p) n -> p c n", p=128)),
                  writes=[("w2b", part % 2)], dma="w2b%d" % (part % 2))
                act = carve((part % 2) * 2304, 2304, BF16).rearrange("p (c n) -> p c n", c=2)
                for fc in range(2):
                    for ti in tiles:
                        t0, tl = BT[ti]
                        pg = psl[(2 * (fc * 5 + ti)) % 4]; pu = psl[(2 * (fc * 5 + ti)) % 4 + 1]
                        kg = ("ps", (2 * (fc * 5 + ti)) % 4); ku = ("ps", (2 * (fc * 5 + ti)) % 4 + 1)
                        for k in range(8):
                            A("pe", lambda e, pg=pg, wbv=wbv, k=k, fc=fc, t0=t0, tl=tl: e.matmul(
                                pg[:, 0:tl], wbv[:, k, 0, fc * 128:(fc + 1) * 128], hn[:, k, t0:t0 + tl], start=(k == 0), stop=(k == 7)),
                              reads=[("w1b", part % 2, 0)] + ks("hn", k, ti), writes=[kg])
                        for k in range(8):
                            A("pe", lambda e, pu=pu, wbv=wbv, k=k, fc=fc, t0=t0, tl=tl: e.matmul(
                                pu[:, 0:tl], wbv[:, k, 1, fc * 128:(fc + 1) * 128], hn[:, k, t0:t0 + tl], start=(k == 0), stop=(k == 7)),
                              reads=[("w1b", part % 2, 1)] + ks("hn", k, ti), writes=[ku])
                        st = carve(4608 + ((fc * 5 + ti) % 2) * 512, 512)
                        skey = ("silut", (fc * 5 + ti) % 2)
                        A("act", lambda e, st=st, pg=pg, tl=tl: e.activation(out=st[:, 0:tl], in_=pg[:, 0:tl], func=AF.Silu),
                          reads=[kg], writes=[skey])
                        A("dve", lambda e, st=st, pu=pu, act=act, fc=fc, t0=t0, tl=tl: e.tensor_tensor(
                            act[:, fc, t0:t0 + tl], pu[:, 0:tl], st[:, 0:tl], ALU.mult),
                          reads=[ku, skey], writes=[("act", part % 2, fc, ti)])
                for ti in tiles:
                    t0, tl = BT[ti]
                    w = 1 if ti == 0 else 0
                    for dc in range(8):
                        po = psl[4 + (ti * 8 + dc) % 4]; ko = ("ps", 4 + (ti * 8 + dc) % 4)
                        for fc in range(2):
                            A("pe", lambda e, po=po, w2=w2, fc=fc, dc=dc, act=act, t0=t0, tl=tl: e.matmul(
                                po[:, 0:tl], w2[:, fc, dc * 128:(dc + 1) * 128], act[:, fc, t0:t0 + tl], start=(fc == 0), stop=(fc == 1)),
                              reads=[("w2b", part % 2), ("act", part % 2, fc, ti)], writes=[ko])
                        A("dve", lambda e, po=po, dc=dc, t0=t0, tl=tl, w=w: e.scalar_tensor_tensor(
                            hT[:, dc, t0:t0 + tl], po[:, 0:tl], mod_sc(l, w, j + 2, dc), hT[:, dc, t0:t0 + tl], ALU.mult, ALU.add),
                          reads=[ko] + ks("modT", l) + ks("hT", dc, ti), writes=ks("hT", dc, ti))
            P.barrier()

        def ACTF(out, in_, func, r, w, **kw):
            A("act", lambda e: e.activation(out=out, in_=in_, func=func, **kw), reads=r, writes=w)

        def TT(eng, out, a, b, op, r, w):
            A(eng, lambda e: e.tensor_tensor(out, a, b, op), reads=r, writes=w)

        def TS(eng, out, a, s1, s2, op0, op1, r, w):
            if op1 is None:
                A(eng, lambda e: e.tensor_scalar(out, a, s1, None, op0), reads=r, writes=w)
            else:
                A(eng, lambda e: e.tensor_scalar(out, a, s1, s2, op0, op1), reads=r, writes=w)

        def STT(out, a, s, b, op0, op1, r, w):
            A("dve", lambda e: e.scalar_tensor_tensor(out, a, s, b, op0, op1), reads=r, writes=w)

        def MM(out, lhsT, rhs, r, w, start=True, stop=True, tp=None):
            if tp is None:
                A("pe", lambda e: e.matmul(out, lhsT, rhs, start=start, stop=stop), reads=r, writes=w)
            else:
                A("pe", lambda e: e.matmul(out, lhsT, rhs, start=start, stop=stop, tile_position=tp), reads=r, writes=w)

        def TR(out, in_, ident, r, w):
            A("pe", lambda e: e.transpose(out, in_, ident), reads=r, writes=w)

        def load_win(l, c0, n, slot):
            A("pool", lambda e: e.dma_start(out=w1b[slot][:, :, 0:n], in_=w_in[l, :, c0:c0 + n].rearrange("(k p) n -> p k n", p=128)),
              writes=[("w1b", slot, 0), ("w1b", slot, 1)], dma="w1b%d0" % slot)

        def pbc_(l, name):
            o, n = PB[name]
            return pb[:, l * PBN + o: l * PBN + o + n]

        TMP0 = 6948

        def normrope(l, c0, nchunks, dim, lhs_fn, ppcols, dsts, tiles, bdname, permname, cosname, sinname, loader=None):
            if loader is None:
                load_win(l, c0, 512, 0)
            else:
                loader()
            WK = [("w1b", 0, 0), ("w1b", 0, 1)]
            tabc = carve(TMP0 + 0, 512); tabs = carve(TMP0 + 512, 512)
            t1 = carve(TMP0 + 1024, 512); t2 = carve(TMP0 + 1536, 512)
            for ti in tiles:
                t0, tl = BT[ti]
                if ti > 0:
                    oc = offF[cosname][0] + (ti - 1) * 512; os_ = offF[sinname][0] + (ti - 1) * 512
                    A("sp", lambda e, oc=oc: e.dma_start(out=tabc, in_=constF[:, oc:oc + 512]), writes=["tabc"], dma="tabc")
                    A("sp", lambda e, os_=os_: e.dma_start(out=tabs, in_=constF[:, os_:os_ + 512]), writes=["tabs"], dma="tabs")
                for cq in range(nchunks):
                    b = cq % 2
                    raw = carve(TMP0 + 2048 + b * 512, 512); sq = carve(TMP0 + 3072 + b * 256, 256, BF16)
                    rinv = carve(TMP0 + 3584 + b * 512, 512); xn = carve(TMP0 + 4608 + b * 512, 512)
                    ps = psl[b]; ps2 = psl[2 + b]; ps3 = psl[4 + b]
                    for k in range(8):
                        MM(ps[:, 0:tl], lhs_fn(cq, k), hn[:, k, t0:t0 + tl], WK + ks("hn", k, ti), [("ps", b)], start=(k == 0), stop=(k == 7))
                    ACTF(raw[:, 0:tl], ps[:, 0:tl], AF.Copy, [("ps", b)], [("raw", b)])
                    ACTF(sq[:, 0:tl], ps[:, 0:tl], AF.Square, [("ps", b)], [("sq", b)])
                    MM(ps2[:, 0:tl], cB_(bdname), sq[:, 0:tl], [("sq", b), "kB"], [("ps", 2 + b)])
                    ACTF(rinv[:, 0:tl], ps2[:, 0:tl], AF.Sqrt, [("ps", 2 + b), "epsb"], [("rinv", b)], scale=1.0 / dim, bias=epsb[:, 0:1])
                    A("dve", lambda e, rinv=rinv, tl=tl: e.reciprocal(rinv[:, 0:tl], rinv[:, 0:tl]), reads=[("rinv", b)], writes=[("rinv", b)])
                    pc = ppcols[cq]
                    STT(xn[:, 0:tl], raw[:, 0:tl], pp[:, l * PPN + pc: l * PPN + pc + 1], rinv[:, 0:tl], ALU.mult, ALU.mult,
                        [("raw", b), ("rinv", b), "pp"], [("xn", b)])
                    dst, dkey = dsts[cq](t0, tl, ti)
                    vw = (lambda a: a.rearrange("p (a b) -> p a b", b=128)) if len(dst.shape) == 3 else (lambda a: a)
                    if ti == 0:
                        ACTF(dst, vw(xn[:, 0:tl]), AF.Copy, [("xn", b)], [dkey])
                    else:
                        MM(ps3[:, 0:tl], cF_(permname), xn[:, 0:tl], [("xn", b), "kF"], [("ps", 4 + b)])
                        TT("dve", t1[:, 0:tl], ps3[:, 0:tl], tabs[:, 0:tl], ALU.mult, [("ps", 4 + b), "tabs"], ["t1"])
                        TT("dve", t2[:, 0:tl], xn[:, 0:tl], tabc[:, 0:tl], ALU.mult, [("xn", b), "tabc"], ["t2"])
                        TT("dve", dst, vw(t1[:, 0:tl]), vw(t2[:, 0:tl]), ALU.add, ["t1", "t2"], [dkey])

        def vproj(l, c0, n, nh, vdst, vkey):
            load_win(l, c0, n, 1)
            WK = [("w1b", 1, 0), ("w1b", 1, 1)]
            A("dve", lambda e: e.memset(vdst[:, :, :, 64:65], 1.0), writes=[vkey])
            for n_ in range(NT):
                b = n_ % 2
                ti = 0 if n_ < 2 else 1 + (n_ - 2) // 4
                ps = psl[6 + b]
                for k in range(8):
                    MM(ps[:, 0:n], hn[:, k, n_ * 128:(n_ + 1) * 128], w1b[1][:, k, 0:n], WK + ks("hn", k, ti), [("ps", 6 + b)], start=(k == 0), stop=(k == 7))
                if b == 0:
                    ACTF(vdst[:, n_, :, 0:64], ps[:, 0:n].rearrange("p (h d) -> p h d", d=64), AF.Copy, [("ps", 6 + b)], [vkey])
                else:
                    A("dve", lambda e, ps=ps, n_=n_: e.tensor_copy(vdst[:, n_, :, 0:64], ps[:, 0:n].rearrange("p (h d) -> p h d", d=64)),
                      reads=[("ps", 6 + b)], writes=[vkey])

        def wout_part(l, row0, nch, yT, ykeys, tiles):
            nld = min(nch, 2)
            A("pool", lambda e: e.dma_start(out=w2b[0][:, 0:nld, :], in_=w_out[l, row0:row0 + nld * 128, :].rearrange("(c p) n -> p c n", p=128)),
              writes=[("w2b", 0)], dma="w2b0")
            if nch > 2:
                A("pool", lambda e: e.dma_start(out=w2b[1][:, 0:1, :], in_=w_out[l, row0 + 256:row0 + 384, :].rearrange("(c p) n -> p c n", p=128)),
                  writes=[("w2b", 1)], dma="w2b1")
            for ti in tiles:
                t0, tl = BT[ti]
                w = 1 if ti == 0 else 0
                for dc in range(8):
                    po = psl[4 + dc % 4]; ko = ("ps", 4 + dc % 4)
                    for c in range(nch):
                        wsrc = w2b[0][:, c, dc * 128:(dc + 1) * 128] if c < 2 else w2b[1][:, 0, dc * 128:(dc + 1) * 128]
                        MM(po[:, 0:tl], wsrc, yT[:, c, t0:t0 + tl], [("w2b", 0), ("w2b", 1)] + ykeys, [ko], start=(c == 0), stop=(c == nch - 1))
                    STT(hT[:, dc, t0:t0 + tl], po[:, 0:tl], mod_sc(l, w, 5, dc), hT[:, dc, t0:t0 + tl], ALU.mult, ALU.add,
                        [ko] + ks("modT", l) + ks("hT", dc, ti), ks("hT", dc, ti))

        def mixer_A(l, need_ctx):
            lam_init = 0.8 - 0.6 * float(np.exp(-0.3 * l))
            qTa = carve(0, 2304, BF16).rearrange("p (c n) -> p c n", c=2)
            kTa = carve(2304, 2304, BF16).rearrange("p (c n) -> p c n", c=2)
            va = carve(4608, 2340, BF16).rearrange("p (n h d) -> p n h d", n=NT, h=4)
            dsts = [lambda t0, tl, ti, c=c: (qTa[:, c, t0:t0 + tl], ("qTa", c, ti)) for c in range(2)] + \
                   [lambda t0, tl, ti, c=c: (kTa[:, c, t0:t0 + tl], ("kTa", c, ti)) for c in range(2)]
            normrope(l, OFF_QA, 4, 32, lambda cq, k: w1b[0][:, k, cq * 128:(cq + 1) * 128], [0, 0, 1, 1], dsts, [0, 1, 2, 3, 4],
                     "bd32", "permA", "cosA", "sinA")
            vproj(l, OFF_VA, 256, 4, va, "va")
            P.barrier()
            sm = carve(TMP0, 512)
            lamt = sm[:, 0:64]; lam2 = sm[:, 64:66]; neglam = sm[:, 66:67]; subw = sm[:, 128:192]
            al = pbc_(l, "a_lambda")
            TT("dve", lamt.rearrange("p (a b) -> p a b", a=2), al.rearrange("p (a t b) -> p a t b", a=2, t=2)[:, :, 0, :],
               al.rearrange("p (a t b) -> p a t b", a=2, t=2)[:, :, 1, :], ALU.mult, ["pb"], ["lamt"])
            A("dve", lambda e: e.reduce_sum(lam2, lamt.rearrange("p (a b) -> p a b", a=2), AX.X), reads=["lamt"], writes=["lam2"])
            ACTF(lam2, lam2, AF.Exp, ["lam2"], ["lam2"])
            TT("dve", neglam, lam2[:, 1:2], lam2[:, 0:1], ALU.subtract, ["lam2"], ["neglam"])
            TS("dve", neglam, neglam, -lam_init, None, ALU.add, None, ["neglam"], ["neglam"])
            TS("dve", subw, pbc_(l, "a_subln"), 1.0 - lam_init, None, ALU.mult, None, ["pb"], ["subw"])
            eT = carve(6948 + 512, 2304, BF16).rearrange("p (m n) -> p m n", m=2)
            ytok = [carve(6948 + 512 + 2304 + i * 128, 128, BF16) for i in range(2)]
            fin = carve(6948 + 512 + 2304 + 256, 256)
            yAT = carve(10276, 2304, BF16).rearrange("p (c n) -> p c n", c=2)
            qtiles = list(range(2, NT)) + ([0, 1] if need_ctx else [])
            for qi, qt in enumerate(qtiles):
                kts = list(range(NT)) if qt >= 2 else [0, 1]
                tiq = 0 if qt < 2 else 1 + (qt - 2) // 4
                yt = ytok[qi % 2]; ykey = ("ytok", qi % 2)
                for h in range(4):
                    c = h // 2
                    for m in range(2):
                        base = (h % 2) * 64 + 32 * m
                        tp = (96, 0) if base == 96 else None
                        for kg in range(0, len(kts), 4):
                            grp = kts[kg:kg + 4]
                            pi = (m * 5 + kg // 4) % 4
                            ps = psl[pi]
                            for j, kt in enumerate(grp):
                                tik = 0 if kt < 2 else 1 + (kt - 2) // 4
                                MM(ps[:, j * 128:(j + 1) * 128], kTa[base:base + 32, c, kt * 128:(kt + 1) * 128],
                                   qTa[base:base + 32, c, qt * 128:(qt + 1) * 128], [("kTa", c, tik), ("qTa", c, tiq)], [("ps", pi)], tp=tp)
                            ACTF(eT[:, m, kg * 128:(kg + len(grp)) * 128], ps[:, 0:len(grp) * 128], AF.Exp, [("ps", pi)], [("eT", m)], scale=32 ** -0.5)
                    for m in range(2):
                        acc = psl[4 + m]
                        for j, kt in enumerate(kts):
                            MM(acc[:, 0:65], eT[:, m, j * 128:(j + 1) * 128], va[:, kt, h, :], [("eT", m), "va"], [("ps", 4 + m)],
                               start=(j == 0), stop=(j == len(kts) - 1))
                    rr = fin[:, 0:2]; o = fin[:, 64:128]; ss = fin[:, 2:3]; junk = fin[:, 128:192]
                    A("dve", lambda e, rr=rr: e.reciprocal(rr[:, 0:1], psl[4][:, 64:65]), reads=[("ps", 4)], writes=["rr"])
                    A("dve", lambda e, rr=rr: e.reciprocal(rr[:, 1:2], psl[5][:, 64:65]), reads=[("ps", 5)], writes=["rr"])
                    TT("dve", rr[:, 1:2], rr[:, 1:2], neglam, ALU.mult, ["rr", "neglam"], ["rr"])
                    TS("dve", o, psl[4][:, 0:64], rr[:, 0:1], None, ALU.mult, None, [("ps", 4), "rr"], ["o"])
                    STT(o, psl[5][:, 0:64], rr[:, 1:2], o, ALU.mult, ALU.add, [("ps", 5), "rr", "o"], ["o"])
                    A("act", lambda e, junk=junk, o=o, ss=ss: e.activation(out=junk, in_=o, func=AF.Square, accum_out=ss), reads=["o"], writes=["ss", "junk"])
                    ACTF(ss, ss, AF.Sqrt, ["ss", "epsb"], ["ss"], scale=1.0 / 64, bias=epsb[:, 0:1])
                    A("dve", lambda e, ss=ss: e.reciprocal(ss, ss), reads=["ss"], writes=["ss"])
                    STT(yt[:, h * 64:(h + 1) * 64], o, ss, subw, ALU.mult, ALU.mult, ["o", "ss", "subw"], [ykey])
                for c in range(2):
                    pt = psl[6 + c]
                    TR(pt[:].bitcast(BF16)[:, 0:128], yt[:, c * 128:(c + 1) * 128], cB_("ident"), [ykey, "kB"], [("ps", 6 + c)])
                    A("dve", lambda e, pt=pt, c=c, qt=qt: e.tensor_copy(yAT[:, c, qt * 128:(qt + 1) * 128], pt[:].bitcast(BF16)[:, 0:128]),
                      reads=[("ps", 6 + c)], writes=[("yAT", tiq)])
            wout_part(l, 0, 2, yAT, [("yAT", i) for i in range(5)], [0, 1, 2, 3, 4] if need_ctx else [1, 2, 3, 4])
            P.barrier()

        def mixer_B(l, need_ctx):
            qTb = carve(0, 3456, BF16).rearrange("p (n c q) -> p n c q", n=NT, c=3)
            kTb = carve(3456, 1152, BF16)
            vb = carve(4608, 1170, BF16).rearrange("p (n h d) -> p n h d", n=NT, h=2)
            dsts = [lambda t0, tl, ti, c=c: (qTb[:, t0 // 128:(t0 + tl) // 128, c, :], ("qTb", c, ti)) for c in range(3)] + \
                   [lambda t0, tl, ti: (kTb[:, t0:t0 + tl], ("kTb", ti))]

            def lhs(cq, k):
                return w1b[0][:, k, cq * 128:(cq + 1) * 128]

            def loaderB():
                for i in range(3):
                    for g in range(2):
                        cc = OFF_QB + (g * 3 + i) * 64
                        A("pool", lambda e, i=i, g=g, cc=cc: e.dma_start(out=w1b[0][:, :, i * 128 + g * 64:i * 128 + g * 64 + 64],
                                                              in_=w_in[l, :, cc:cc + 64].rearrange("(k p) n -> p k n", p=128)),
                          writes=[("w1b", 0, 0), ("w1b", 0, 1)], dma="w1b00")
                A("pool", lambda e: e.dma_start(out=w1b[0][:, :, 384:512], in_=w_in[l, :, OFF_KB:OFF_KB + 128].rearrange("(k p) n -> p k n", p=128)),
                  writes=[("w1b", 0, 0), ("w1b", 0, 1)], dma="w1b00")
            normrope(l, OFF_QB, 4, 64, lhs, [2, 2, 2, 3], dsts, [0, 1, 2, 3, 4], "bd64", "permB", "cosB", "sinB", loader=loaderB)
            vproj(l, OFF_VB, 128, 2, vb, "vb")
            P.barrier()
            sm = carve(TMP0, 512)
            esink = sm[:, 0:6]
            ACTF(esink, pbc_(l, "b_sink"), AF.Exp, ["pb"], ["esink"])
            eTb = carve(TMP0 + 512, 960, BF16).rearrange("p (j n) -> p j n", j=5)
            ytok = [carve(TMP0 + 1536 + i * 192, 192, BF16) for i in range(2)]
            fin = carve(TMP0 + 2048, 128)
            yBT = carve(TMP0 + 2304, 3456, BF16).rearrange("p (c n) -> p c n", c=3)
            blocks = [(2 + n, n) for n in range(16)] + ([(0, -1), (1, -1)] if need_ctx else [])
            for bi, (qt, n) in enumerate(blocks):
                tiq = 0 if qt < 2 else 1 + (qt - 2) // 4
                kts = [(0, None), (1, None)]
                if n >= 0:
                    if n > 0:
                        kts.append((2 + n - 1, "mprev"))
                    kts.append((2 + n, None))
                    if n < 15:
                        kts.append((2 + n + 1, "mnext"))
                yt = ytok[bi % 2]; ykey = ("ytokb", bi % 2)
                for g in range(2):
                    for j, (kt, mk) in enumerate(kts):
                        tik = 0 if kt < 2 else 1 + (kt - 2) // 4
                        ps = psl[j % 4]
                        MM(ps[:, 0:384], kTb[g * 64:(g + 1) * 64, kt * 128:(kt + 1) * 128], qTb[g * 64:(g + 1) * 64, qt, :, :].rearrange("p c q -> p (c q)"),
                           [("kTb", tik)] + [("qTb", c, tiq) for c in range(3)], [("ps", j % 4)])
                        ACTF(eTb[:, j, :], ps[:, 0:384], AF.Exp, [("ps", j % 4)], [("eTb", j)], scale=0.125)
                        if mk is not None and not os.environ.get("NOMASK"):
                            for i3 in range(3):
                                TT("dve", eTb[:, j, i3 * 128:(i3 + 1) * 128], eTb[:, j, i3 * 128:(i3 + 1) * 128], cB_(mk), ALU.mult,
                                   [("eTb", j), "kB"], [("eTb", j)])
                    for i in range(3):
                        hq = g * 3 + i
                        pa = 4 + hq % 4
                        acc = psl[pa]
                        for j, (kt, mk) in enumerate(kts):
                            MM(acc[:, 0:65], eTb[:, j, i * 128:(i + 1) * 128], vb[:, kt, g, :], [("eTb", j), "vb"], [("ps", pa)],
                               start=(j == 0), stop=(j == len(kts) - 1))
                        den = fin[:, hq:hq + 1]
                        TS("dve", den, acc[:, 64:65], esink[:, hq:hq + 1], None, ALU.add, None, [("ps", pa), "esink"], [("den", hq)])
                        A("dve", lambda e, den=den: e.reciprocal(den, den), reads=[("den", hq)], writes=[("den", hq)])
                        TS("dve", yt[:, hq * 64:(hq + 1) * 64], acc[:, 0:64], den, None, ALU.mult, None, [("ps", pa), ("den", hq)], [ykey])
                for c in range(3):
                    pt = psl[c % 2]
                    TR(pt[:].bitcast(BF16)[:, 0:128], yt[:, c * 128:(c + 1) * 128], cB_("ident"), [ykey, "kB"], [("ps", c % 2)])
                    A("dve", lambda e, pt=pt, c=c, qt=qt: e.tensor_copy(yBT[:, c, qt * 128:(qt + 1) * 128], pt[:].bitcast(BF16)[:, 0:128]),
                      reads=[("ps", c % 2)], writes=[("yBT", tiq)])
            wout_part(l, 256, 3, yBT, [("yBT", i) for i in range(5)], [0, 1, 2, 3, 4] if need_ctx else [1, 2, 3, 4])
            P.barrier()

        def mixer_C(l, need_ctx):
            SMW = 216
            g_all = carve(0 * SMW, SMW).rearrange("p (d n h) -> p d n h", d=2, n=NT)
            beta = carve(1 * SMW, SMW).rearrange("p (d n h) -> p d n h", d=2, n=NT)
            negb = carve(2 * SMW, SMW).rearrange("p (d n h) -> p d n h", d=2, n=NT)
            beg = carve(3 * SMW, SMW).rearrange("p (d n h) -> p d n h", d=2, n=NT)
            eG = carve(4 * SMW, SMW).rearrange("p (d n h) -> p d n h", d=2, n=NT)
            kds = carve(5 * SMW, SMW).rearrange("p (d n h) -> p d n h", d=2, n=NT)
            glv = carve(6 * SMW, SMW).rearrange("p (d n h) -> p d n h", d=2, n=NT)
            QT = carve(1512, 1152, BF16); KT = carve(2664, 1152, BF16)
            KTOK = carve(3816, 1152, BF16).rearrange("p (n c) -> p n c", n=NT)
            VTOK = carve(4968, 1152, BF16).rearrange("p (n c) -> p n c", n=NT)
            W0 = 6120
            load_win(l, OFF_A, 24, 1)
            WK1 = [("w1b", 1, 0), ("w1b", 1, 1)]
            for n_ in range(NT):
                b = n_ % 2
                ti = 0 if n_ < 2 else 1 + (n_ - 2) // 4
                ps = psl[6 + b]
                for k in range(8):
                    MM(ps[:, 0:24], hn[:, k, n_ * 128:(n_ + 1) * 128], w1b[1][:, k, 0:24], WK1 + ks("hn", k, ti), [("ps", 6 + b)], start=(k == 0), stop=(k == 7))
                TT("dve", g_all[:, :, n_, :], ps[:, 0:12].rearrange("p (d h) -> p d h", d=2),
                   pbc_(l, "c_dt_bias").rearrange("p (d h) -> p d h", d=2), ALU.add, [("ps", 6 + b), "pb"], ["g_all"])
                A("dve", lambda e, ps=ps, n_=n_: e.tensor_copy(beta[:, :, n_, :], ps[:, 12:24].rearrange("p (d h) -> p d h", d=2)),
                  reads=[("ps", 6 + b)], writes=["beta"])
            gf = carve(0, SMW); bf_ = carve(SMW, SMW)
            tmpA = carve(W0, SMW); nar = carve(W0 + 256, 12); Gc = carve(W0 + 512, SMW); Gl = carve(W0 + 768, SMW)
            ACTF(gf, gf, AF.Exp, ["g_all"], ["g_all"])
            ACTF(gf, gf, AF.Ln, ["g_all", "oneb"], ["g_all"], bias=oneb[:, 0:1])
            ACTF(nar, pbc_(l, "c_A_log"), AF.Exp, ["pb"], ["nar"])
            TS("dve", nar, nar, -1.0, None, ALU.mult, None, ["nar"], ["nar"])
            for n_ in range(NT):
                TT("dve", g_all[:, :, n_, :], g_all[:, :, n_, :], nar.rearrange("p (d h) -> p d h", d=2), ALU.mult, ["g_all", "nar"], ["g_all"])
            ACTF(bf_, bf_, AF.Exp, ["beta"], ["beta"], scale=-1.0)
            TS("dve", bf_, bf_, 1.0, None, ALU.add, None, ["beta"], ["beta"])
            A("dve", lambda e: e.reciprocal(bf_, bf_), reads=["beta"], writes=["beta"])
            TS("dve", carve(2 * SMW, SMW), bf_, -1.0, None, ALU.mult, None, ["beta"], ["negb"])
            onesF = carve(W0 + 1024, 128)
            A("dve", lambda e: e.memset(onesF, 1.0), writes=["onesF"])
            for d in range(2):
                MM(psl[0][:, d * 108:(d + 1) * 108], cF_("tri_f" if d == 0 else "tri_b"), gf[:, d * 108:(d + 1) * 108], ["g_all", "kF"], [("ps", 0)])
            MM(psl[1][:, 0:SMW], onesF, gf, ["g_all", "onesF"], [("ps", 1)])
            A("dve", lambda e: e.tensor_copy(Gc, psl[0][:, 0:SMW]), reads=[("ps", 0)], writes=["Gc"])
            A("dve", lambda e: e.tensor_copy(Gl, psl[1][:, 0:SMW]), reads=[("ps", 1)], writes=["Gl"])
            ACTF(carve(4 * SMW, SMW), Gc, AF.Exp, ["Gc"], ["eG"])
            TT("dve", tmpA, Gl, Gc, ALU.subtract, ["Gl", "Gc"], ["tmpA"])
            ACTF(carve(5 * SMW, SMW), tmpA, AF.Exp, ["tmpA"], ["kds"])
            ACTF(Gl, Gl, AF.Exp, ["Gl"], ["Gl"])
            TT("dve", carve(3 * SMW, SMW), bf_, carve(4 * SMW, SMW), ALU.mult, ["beta", "eG"], ["beg"])
            Glv = Gl.rearrange("p (d n hp two) -> p d n hp two", d=2, n=NT, two=2)
            for hh in range(2):
                A("dve", lambda e, hh=hh: e.tensor_copy(glv[hh * 64:(hh + 1) * 64, :, :, 0:3], Glv[hh * 64:(hh + 1) * 64, :, :, :, hh]),
                  reads=["Gl"], writes=["glv"])
            P.barrier()
            CST = float(os.environ.get('CSTOP', '99'))
            if CST <= 0:
                return
            order = [list(range(NT)), [1, 0] + list(range(NT - 1, 1, -1))]
            for hp in range(3):
                raw = carve(W0, 2312); acc = carve(W0 + 2312, 2308); sqb = carve(W0 + 4620, 1154, BF16)
                for j3 in range(3):
                    cc = j3 * 3 + hp
                    A("pool", lambda e, j3=j3, cc=cc: e.dma_start(out=w1b[0][:, :, j3 * 128:(j3 + 1) * 128],
                                                            in_=w_in[l, :, OFF_C + cc * 128:OFF_C + (cc + 1) * 128].rearrange("(k p) n -> p k n", p=128)),
                      writes=[("w1b", 0, 0), ("w1b", 0, 1)], dma="w1b00")
                WK0 = [("w1b", 0, 0), ("w1b", 0, 1)]
                A("dve", lambda e: e.memset(raw, 0.0), writes=["raw"])
                for j3 in range(3):
                    cc = j3 * 3 + hp
                    for ti in range(5):
                        t0, tl = BT[ti]
                        b = ti % 2
                        ps = psl[b]
                        for k in range(8):
                            MM(ps[:, 0:tl], w1b[0][:, k, j3 * 128:(j3 + 1) * 128], hn[:, k, t0:t0 + tl], WK0 + ks("hn", k, ti), [("ps", b)], start=(k == 0), stop=(k == 7))
                        ro = 2 + t0 if ti == 0 else 6 + t0
                        ACTF(raw[:, ro:ro + tl], ps[:, 0:tl], AF.Copy, [("ps", b)], ["raw"])
                    cw = lambda k: pp[:, l * PPN + 4 + cc * 5 + k: l * PPN + 4 + cc * 5 + k + 1]
                    TS("dve", acc, raw[:, 0:2308], cw(0), None, ALU.mult, None, ["raw", "pp"], ["acc"])
                    for k in range(1, 5):
                        STT(acc, raw[:, k:k + 2308], cw(k), acc, ALU.mult, ALU.add, ["raw", "pp", "acc"], ["acc"])
                    ACTF(acc, acc, AF.Silu, ["acc"], ["acc"])
                    if j3 < 2:
                        ACTF(sqb, acc, AF.Square, ["acc"], ["sqb"])
                        dstT = QT if j3 == 0 else KT
                        dkey = "QT" if j3 == 0 else "KT"
                        for ti in range(5):
                            t0, tl = BT[ti]
                            a0 = t0 if ti == 0 else 4 + t0
                            b = ti % 2
                            ps2 = psl[2 + b]
                            rin = carve(W0 + 5776, 512)
                            MM(ps2[:, 0:tl], cB_("bd64"), sqb[:, a0:a0 + tl], ["sqb", "kB"], [("ps", 2 + b)])
                            ACTF(rin[:, 0:tl], ps2[:, 0:tl], AF.Sqrt, [("ps", 2 + b), "epsb"], ["rin"], bias=epsb[:, 0:1])
                            A("dve", lambda e, rin=rin, tl=tl: e.reciprocal(rin[:, 0:tl], rin[:, 0:tl]), reads=["rin"], writes=["rin"])
                            if j3 == 0:
                                STT(dstT[:, t0:t0 + tl], acc[:, a0:a0 + tl], 0.125, rin[:, 0:tl], ALU.mult, ALU.mult, ["acc", "rin"], [dkey])
                            else:
                                TT("dve", dstT[:, t0:t0 + tl], acc[:, a0:a0 + tl], rin[:, 0:tl], ALU.mult, ["acc", "rin"], [dkey])
                    else:
                        VT = carve(W0 + 4620, 1152, BF16)
                        ACTF(VT[:, 0:256], acc[:, 0:256], AF.Copy, ["acc"], ["sqb"])
                        ACTF(VT[:, 256:T], acc[:, 260:2308], AF.Copy, ["acc"], ["sqb"])
                    if j3 >= 1:
                        srcT = KT if j3 == 1 else carve(W0 + 4620, 1152, BF16)
                        skey = "KT" if j3 == 1 else "sqb"
                        dtok = KTOK if j3 == 1 else VTOK
                        tkey = "KTOK" if j3 == 1 else "VTOK"
                        for n_ in range(NT):
                            b = n_ % 2
                            pt = psl[4 + b]
                            TR(pt[:].bitcast(BF16)[:, 0:128], srcT[:, n_ * 128:(n_ + 1) * 128], cB_("ident"), [skey, "kB"], [("ps", 4 + b)])
                            if b == 0:
                                A("dve", lambda e, pt=pt, n_=n_, dtok=dtok: e.tensor_copy(dtok[:, n_, :], pt[:].bitcast(BF16)[:, 0:128]),
                                  reads=[("ps", 4 + b)], writes=[tkey])
                            else:
                                ACTF(dtok[:, n_, :], pt[:].bitcast(BF16)[:, 0:128], AF.Copy, [("ps", 4 + b)], [tkey])
                P.barrier()
                if CST <= 1:
                    continue
                S0 = carve(W0, 512); S1 = carve(W0 + 512, 512); S2 = carve(W0 + 1024, 512); S3 = carve(W0 + 1536, 512); S4 = carve(W0 + 2048, 512)
                QKT = carve(W0 + 2560, 256, BF16); bv = carve(W0 + 2816, 128, BF16); bk = carve(W0 + 2944, 128, BF16); kd = carve(W0 + 3072, 128, BF16)
                u = carve(W0 + 3200, 256); wT = carve(W0 + 3456, 128, BF16).rearrange("p (d i) -> p d i", d=2)
                vnew = carve(W0 + 3712, 128, BF16); Sst = carve(W0 + 3840, 128).rearrange("p (d v) -> p d v", d=2)
                Sb = carve(W0 + 3968, 64, BF16).rearrange("p (d v) -> p d v", d=2)
                Rb = carve(W0 + 4032, 256, BF16)
                oacc = carve(W0 + 4288, 2304).rearrange("p (n c) -> p n c", n=NT)
                A("dve", lambda e: e.memset(oacc, 0.0), writes=["oacc"])
                A("dve", lambda e: e.memset(Sst, 0.0), writes=["S"])
                A("dve", lambda e: e.memset(Sb, 0.0), writes=["Sb"])
                sl = lambda qi: slice(qi * 128, (qi + 1) * 128)
                CSTEP = int(os.environ.get('CSTEP', '99'))
                for s in range(min(NT, CSTEP)):
                    tl_ = [order[0][s], order[1][s]]
                    QI = [(d, hh) for hh in range(2) for d in range(2)]
                    if CST <= 1.05:
                        continue
                    for qi, (d, hh) in enumerate(QI):
                        t = tl_[d]
                        kTs = KT[hh * 64:(hh + 1) * 64, t * 128:(t + 1) * 128]
                        MM(psl[0 + hh][:, d * 128:(d + 1) * 128], kTs, kTs, ["KT"], [("ps", 0 + hh)])
                        MM(psl[2 + hh][:, d * 128:(d + 1) * 128], QT[hh * 64:(hh + 1) * 64, t * 128:(t + 1) * 128], kTs, ["KT", "QT"], [("ps", 2 + hh)])
                    if CST <= 1.1:
                        continue
                    for qi, (d, hh) in enumerate(QI):
                        t = tl_[d]; h = hp * 2 + hh
                        TS("dve", S0[:, sl(qi)], cF_("msk4")[:, sl(qi)], g_all[:, d, t, h:h + 1], None, ALU.mult, None, ["kF", "g_all"], ["S0"])
                    for qi, (d, hh) in enumerate(QI):
                        MM(psl[4][:, sl(qi)], cF_("tri_f" if d == 0 else "tri_b"), S0[:, sl(qi)], ["S0", "kF"], [("ps", 4)])
                    ACTF(S1, psl[4][:, 0:512], AF.Exp, [("ps", 4)], ["S1"])
                    TT("dve", S2, S1, cF_("strict4"), ALU.mult, ["S1", "kF"], ["S2"])
                    TT("dve", S3, S1, cF_("incl4"), ALU.mult, ["S1", "kF"], ["S3"])
                    if CST <= 1.3:
                        continue
                    for qi, (d, hh) in enumerate(QI):
                        t = tl_[d]; h = hp * 2 + hh
                        STT(S2[:, sl(qi)], psl[0 + hh][:, d * 128:(d + 1) * 128], negb[:, d, t, h:h + 1], S2[:, sl(qi)], ALU.mult, ALU.mult,
                            [("ps", 0 + hh), "negb", "S2"], ["S2"])
                    for hh in range(2):
                        TT("dve", S3[:, hh * 256:(hh + 1) * 256], psl[2 + hh][:, 0:256], S3[:, hh * 256:(hh + 1) * 256], ALU.mult, [("ps", 2 + hh), "S3"], ["S3"])
                    if CST <= 1.5:
                        continue
                    for qi in range(4):
                        MM(psl[5][:, sl(qi)], S2[:, sl(qi)], cF_("ident"), ["S2", "kF"], [("ps", 5)])
                        MM(psl[6][:, sl(qi)], S3[:, sl(qi)], cF_("ident"), ["S3", "kF"], [("ps", 6)])
                    A("dve", lambda e: e.tensor_copy(S0, psl[5][:, 0:512]), reads=[("ps", 5)], writes=["S0"])
                    ACTF(QKT, psl[6][:, 0:512], AF.Copy, [("ps", 6)], ["QKT"])
                    if CST <= 1.7:
                        continue
                    EXa = w2b[0][:].rearrange("p c n -> p (c n)").bitcast(F32)
                    EXb = w2b[1][:].rearrange("p c n -> p (c n)").bitcast(F32)
                    E0, E1, E2 = EXa[:, 0:512], EXa[:, 512:1024], EXb[:, 0:512]
                    K0, K1 = ("w2b", 0), ("w2b", 1)
                    TT("dve", E0, S2, cF_("bd32_4"), ALU.mult, ["S2", "kF"], [K0])
                    TT("dve", E1, S0, cF_("bd32_4"), ALU.mult, ["S0", "kF"], [K0])
                    TT("dve", S4, E1, cF_("ident4"), ALU.add, [K0, "kF"], ["S4"])
                    TT("dve", E2, E0, cF_("ident4"), ALU.add, [K0, "kF"], [K1])
                    Xc, Yc, Xk, Yk = E0, E1, K0, K0
                    Xn_, Yn_, Xnk, Ynk = S1, S3, "S1", "S3"
                    for kk_ in range(1, 5):
                        for qi in range(4):
                            MM(psl[5][:, sl(qi)], Yc[:, sl(qi)], Xc[:, sl(qi)], [Xk, Yk], [("ps", 5)])
                        for qi in range(4):
                            MM(psl[6][:, sl(qi)], Xc[:, sl(qi)], Yc[:, sl(qi)], [Xk, Yk], [("ps", 6)])
                        ACTF(Xn_, psl[5][:, 0:512], AF.Copy, [("ps", 5)], [Xnk])
                        A("dve", lambda e, Yn_=Yn_: e.tensor_copy(Yn_, psl[6][:, 0:512]), reads=[("ps", 6)], writes=[Ynk])
                        for qi in range(4):
                            MM(psl[7][:, sl(qi)], Xn_[:, sl(qi)], S4[:, sl(qi)], [Xnk, "S4"], [("ps", 7)])
                        for qi in range(4):
                            MM(psl[4][:, sl(qi)], Yn_[:, sl(qi)], E2[:, sl(qi)], [Ynk, K1], [("ps", 4)])
                        TT("dve", S4, S4, psl[7][:, 0:512], ALU.add, ["S4", ("ps", 7)], ["S4"])
                        TT("dve", E2, E2, psl[4][:, 0:512], ALU.add, [K1, ("ps", 4)], [K1])
                        Xc, Xn_ = Xn_, Xc; Xk, Xnk = Xnk, Xk
                        Yc, Yn_ = Yn_, Yc; Yk, Ynk = Ynk, Yk
                    for lvl, mname in enumerate(("m1_4", "m2_4")):
                        need_tm = (lvl == 0)
                        TT("dve", E0, S2, cF_(mname), ALU.mult, ["S2", "kF"], [K0])
                        for qi in range(4):
                            MM(psl[5][:, sl(qi)], E0[:, sl(qi)], S4[:, sl(qi)], [K0, "S4"], [("ps", 5)])
                        ACTF(S1, psl[5][:, 0:512], AF.Copy, [("ps", 5)], ["S1"])
                        if need_tm:
                            TT("dve", E1, S0, cF_(mname), ALU.mult, ["S0", "kF"], [K0])
                            for qi in range(4):
                                MM(psl[6][:, sl(qi)], E1[:, sl(qi)], E2[:, sl(qi)], [K0, K1], [("ps", 6)])
                            A("dve", lambda e: e.tensor_copy(S3, psl[6][:, 0:512]), reads=[("ps", 6)], writes=["S3"])
                        for qi in range(4):
                            MM(psl[7][:, sl(qi)], E2[:, sl(qi)], S1[:, sl(qi)], [K1, "S1"], [("ps", 7)])
                        if need_tm:
                            for qi in range(4):
                                MM(psl[4][:, sl(qi)], S4[:, sl(qi)], S3[:, sl(qi)], ["S4", "S3"], [("ps", 4)])
                        TT("dve", S4, S4, psl[7][:, 0:512], ALU.add, ["S4", ("ps", 7)], ["S4"])
                        if need_tm:
                            TT("dve", E2, E2, psl[4][:, 0:512], ALU.add, [K1, ("ps", 4)], [K1])
                    A("dve", lambda e: e.tensor_copy(Rb, S4), reads=["S4"], writes=["Rb"])
                    if CST <= 2:
                        continue
                    for qi, (d, hh) in enumerate(QI):
                        t = tl_[d]; h = hp * 2 + hh
                        c64 = slice(hh * 64, (hh + 1) * 64); o64 = slice(qi * 64, (qi + 1) * 64)
                        TS("dve", bv[:, o64], VTOK[:, t, c64], beta[:, d, t, h:h + 1], None, ALU.mult, None, ["VTOK", "beta"], ["bv"])
                        ACTF(bk[:, o64], KTOK[:, t, c64], AF.Identity, ["KTOK", "beg"], ["bk"], scale=beg[:, d, t, h:h + 1])
                        ACTF(kd[:, o64], KTOK[:, t, c64], AF.Identity, ["KTOK", "kds"], ["kd"], scale=kds[:, d, t, h:h + 1])
                    for qi, (d, hh) in enumerate(QI):
                        o64 = slice(qi * 64, (qi + 1) * 64)
                        MM(psl[0][:, o64], Rb[:, sl(qi)], bv[:, o64], ["Rb", "bv"], [("ps", 0)])
                    A("dve", lambda e: e.tensor_copy(u, psl[0][:, 0:256]), reads=[("ps", 0)], writes=["u"])
                    for qi, (d, hh) in enumerate(QI):
                        o64 = slice(qi * 64, (qi + 1) * 64); r64 = slice(hh * 64, (hh + 1) * 64)
                        MM(psl[1 + hh][r64, d * 128:(d + 1) * 128], bk[:, o64], Rb[:, sl(qi)], ["Rb", "bk"], [("ps", 1 + hh)], tp=(0, hh * 64))
                    for hh in range(2):
                        r64 = slice(hh * 64, (hh + 1) * 64)
                        ACTF(wT[r64, :, :], psl[1 + hh][r64, 0:256].rearrange("p (d i) -> p d i", d=2), AF.Copy, [("ps", 1 + hh)], ["wT"])
                    if CST <= 3:
                        continue
                    for qi, (d, hh) in enumerate(QI):
                        r64 = slice(hh * 64, (hh + 1) * 64)
                        MM(psl[3 + hh][:, d * 64:(d + 1) * 64], wT[r64, d, :], Sb[r64, d, :], ["wT", "Sb"], [("ps", 3 + hh)])
                    for hh in range(2):
                        TT("dve", vnew[:, hh * 128:(hh + 1) * 128], u[:, hh * 128:(hh + 1) * 128], psl[3 + hh][:, 0:128], ALU.subtract,
                           ["u", ("ps", 3 + hh)], ["vnew"])
                    for qi, (d, hh) in enumerate(QI):
                        t = tl_[d]
                        o64 = slice(qi * 64, (qi + 1) * 64); r64 = slice(hh * 64, (hh + 1) * 64)
                        MM(psl[5 + hh][:, d * 64:(d + 1) * 64], QT[r64, t * 128:(t + 1) * 128], Sb[r64, d, :], ["QT", "Sb"], [("ps", 5 + hh)])
                    for qi, (d, hh) in enumerate(QI):
                        o64 = slice(qi * 64, (qi + 1) * 64)
                        MM(psl[7][:, o64], QKT[:, sl(qi)], vnew[:, o64], ["QKT", "vnew"], [("ps", 7)])
                    for qi, (d, hh) in enumerate(QI):
                        o64 = slice(qi * 64, (qi + 1) * 64); r64 = slice(hh * 64, (hh + 1) * 64)
                        MM(psl[1 + hh][r64, d * 64:(d + 1) * 64], kd[:, o64], vnew[:, o64], ["kd", "vnew"], [("ps", 1 + hh)], tp=(0, hh * 64))
                    for qi, (d, hh) in enumerate(QI):
                        t = tl_[d]; h = hp * 2 + hh
                        o64 = slice(qi * 64, (qi + 1) * 64); c64 = slice(hh * 64, (hh + 1) * 64)
                        STT(oacc[:, t, c64], psl[5 + hh][:, d * 64:(d + 1) * 64], eG[:, d, t, h:h + 1], oacc[:, t, c64], ALU.mult, ALU.add,
                            [("ps", 5 + hh), "eG", "oacc"], ["oacc"])
                        TT("dve", oacc[:, t, c64], oacc[:, t, c64], psl[7][:, o64], ALU.add, [("ps", 7), "oacc"], ["oacc"])
                    for d in range(2):
                        t = tl_[d]
                        for hh in range(2):
                            r64 = slice(hh * 64, (hh + 1) * 64)
                            STT(Sst[r64, d, :], Sst[r64, d, :], glv[r64, d, t, hp:hp + 1], psl[1 + hh][r64, d * 64:(d + 1) * 64], ALU.mult, ALU.add,
                                ["S", "glv", ("ps", 1 + hh)], ["S"])
                    ACTF(Sb, Sst, AF.Copy, ["S"], ["Sb"])
                P.barrier()
                if CSTEP < 99:
                    return
                G = carve(W0, 2304).rearrange("p (n c) -> p n c", n=NT)
                ssq = carve(W0 + 2304, 36); junk = carve(W0 + 2368, 64); y1 = carve(W0 + 2432, 64)
                ytk = carve(W0 + 2560, 1152, BF16).rearrange("p (n c) -> p n c", n=NT)
                yCT = carve(1512, 1152, BF16).rearrange("p (c n) -> p c n", c=1)
                load_win(l, OFF_G + hp * 128, 128, 1)
                for n_ in range(NT):
                    b = n_ % 2
                    ti = 0 if n_ < 2 else 1 + (n_ - 2) // 4
                    ps = psl[6 + b]
                    for k in range(8):
                        MM(ps[:, 0:128], hn[:, k, n_ * 128:(n_ + 1) * 128], w1b[1][:, k, 0:128], WK1 + ks("hn", k, ti), [("ps", 6 + b)], start=(k == 0), stop=(k == 7))
                    ACTF(G[:, n_, :], ps[:, 0:128], AF.Silu, [("ps", 6 + b)], ["G"])
                for n_ in range(NT):
                    for hh in range(2):
                        c64 = slice(hh * 64, (hh + 1) * 64)
                        ix = n_ * 2 + hh
                        A("act", lambda e, n_=n_, c64=c64, ix=ix: e.activation(out=junk, in_=oacc[:, n_, c64], func=AF.Square, accum_out=ssq[:, ix:ix + 1]),
                          reads=["oacc"], writes=["ssq", "junk"])
                ACTF(ssq, ssq, AF.Sqrt, ["ssq", "epsb"], ["ssq"], scale=1.0 / 64, bias=epsb[:, 0:1])
                A("dve", lambda e: e.reciprocal(ssq, ssq), reads=["ssq"], writes=["ssq"])
                for n_ in range(NT):
                    for hh in range(2):
                        c64 = slice(hh * 64, (hh + 1) * 64)
                        ix = n_ * 2 + hh
                        STT(y1, oacc[:, n_, c64], ssq[:, ix:ix + 1], pbc_(l, "c_onorm"), ALU.mult, ALU.mult, ["oacc", "ssq", "pb"], ["y1"])
                        TT("dve", ytk[:, n_, c64], y1, G[:, n_, c64], ALU.mult, ["y1", "G"], ["ytk"])
                for n_ in range(NT):
                    b = n_ % 2
                    pt = psl[4 + b]
                    TR(pt[:].bitcast(BF16)[:, 0:128], ytk[:, n_, :], cB_("ident"), ["ytk", "kB"], [("ps", 4 + b)])
                    A("dve", lambda e, pt=pt, n_=n_: e.tensor_copy(yCT[:, 0, n_ * 128:(n_ + 1) * 128], pt[:].bitcast(BF16)[:, 0:128]),
                      reads=[("ps", 4 + b)], writes=["yCT"])
                if not os.environ.get('CSKIPW'):
                    wout_part(l, 640 + hp * 128, 1, yCT, ["yCT"], [0, 1, 2, 3, 4] if need_ctx else [1, 2, 3, 4])
                P.barrier()
                if int(os.environ.get('CHP', '99')) <= hp + 1:
                    return

        stages = []
        ALLT = [0, 1, 2, 3, 4]
        LAT = [1, 2, 3, 4]
        for l in range(DEPTH):
            last = (l == DEPTH - 1)
            ffn(l, 0, f1w1, f1w2, ALLT)
            if upto in ("ffn1", "ffn1_%d" % l):
                break
            norm_mod(l, 3, ALLT)
            P.barrier()
            mixer_A(l, not last)
            if upto in ('mixA', 'mixA_%d' % l):
                break
            mixer_B(l, not last)
            if upto in ('mixAB', 'mixAB_%d' % l):
                break
            mixer_C(l, not last)
            if upto in ('mix', 'mix_%d' % l):
                break
            ffn(l, 6, f2w1, f2w2, LAT if last else ALLT)
            if upto in ('l0', 'ffn2_%d' % l):
                break

        P.barrier()
        if dbg:
            A("sp", lambda e: e.dma_start(out=dbg2, in_=arena[:]), writes=[("out", "d2")], dma="st_d2")
            for c in range(8):
                A("sp", lambda e, c=c: e.dma_start(out=dbgT[c * 128:(c + 1) * 128, :], in_=hT[:, c, :]),
                  reads=ks("hT", c, range(5)), writes=[("out", "d", c)], dma="st_d%d" % c)
        for c in range(8):
            A("sp", lambda e, c=c: e.dma_start(out=outT[c * 128:(c + 1) * 128, :], in_=hT[:, c, NCTX:T]),
              reads=ks("hT", c, range(5)), writes=[("out", c)], dma="st_o%d" % c)
        fin = Op()
        fin.eng = "sp"; fin.fn = None; fin.dma = None; fin.rk = set(); fin.wk = set(); fin.signal = False; fin.count = 0; fin.semkey = "sp"
        fin.deps = [op for op in P.ops if op.dma is not None and op.dma.startswith("st_")]
        for d in fin.deps:
            d.signal = True
        P.ops.append(fin)
        P.emit(stack)
    return nc, cF, cB


def make_inputs(inputs, b, cF, cB):
    f = lambda a: np.ascontiguousarray(np.asarray(a, dtype=np.float32))
    x = f(inputs["x"]); ctx = f(inputs["ctx"]); c = f(inputs["c"]); c_ctx = f(inputs["c_ctx"])
    m = {}
    m["xT"] = np.ascontiguousarray(np.concatenate([ctx[b].T, x[b].T], axis=1))
    cT = np.concatenate([c[b].reshape(8, 128).T, c_ctx.reshape(8, 128).T], axis=1)
    m["cT"] = np.ascontiguousarray(cT)
    m["w_mod"] = f(inputs["w_mod"])
    bm = f(inputs["b_mod"])
    m["b_modT"] = np.ascontiguousarray(bm.reshape(DEPTH, 72, 128).transpose(2, 0, 1).reshape(128, DEPTH * 72))
    for n in ["ffn1_w1", "ffn1_w2", "ffn2_w1", "ffn2_w2", "w_in", "w_out"]:
        m[n] = f(inputs[n])
    rows = []
    for l in range(DEPTH):
        rows.append(np.concatenate([f(inputs[n])[l].reshape(-1) for n in
                                    ["a_qnorm", "a_knorm", "a_lambda", "a_subln", "b_qnorm", "b_knorm", "b_sink", "c_A_log", "c_dt_bias", "c_onorm"]]))
    row = np.concatenate(rows)
    m["pbc"] = np.ascontiguousarray(np.broadcast_to(row[None, :], (128, row.size)))
    ppl = []
    p = np.arange(128)
    for l in range(DEPTH):
        cols = [f(inputs["a_qnorm"])[l][p % 32], f(inputs["a_knorm"])[l][p % 32], f(inputs["b_qnorm"])[l][p % 64], f(inputs["b_knorm"])[l][p % 64]]
        cv = f(inputs["c_conv"])[l]
        for cc in range(9):
            for k in range(5):
                cols.append(cv[k, cc * 128 + p])
        ppl.append(np.stack(cols, axis=1))
    m["ppar"] = np.ascontiguousarray(np.concatenate(ppl, axis=1))
    m["constF"] = cF
    m["constB"] = cB.astype(ml_dtypes.bfloat16)
    return m


_CACHE = {}


def kernel(**inputs):
    if "nc" not in _CACHE:
        _CACHE["nc"] = build()
    nc, cF, cB = _CACHE["nc"]
    in_maps = [make_inputs(inputs, b, cF, cB) for b in range(8)]
    res = run_bass_kernel_spmd(nc, in_maps, core_ids=list(range(8)))
    out = np.stack([np.ascontiguousarray(res.results[b]["outT"].T) for b in range(8)], axis=0)
    return out.astype(np.float32)
```

```python
import contextlib
import os
import numpy as np
import ml_dtypes
import concourse.bass as bass
import concourse.mybir as mybir
from concourse.bass_utils import run_bass_kernel_spmd

F32 = mybir.dt.float32
BF16 = mybir.dt.bfloat16
AF = mybir.ActivationFunctionType
ALU = mybir.AluOpType
AX = mybir.AxisListType

D = 1024; T = 2304; NCTX = 256; NLAT = 2048; NT = 18; DFF = 2816; NF = 22
DEPTH = 2
INC = 2968
OFF_QA, OFF_KA, OFF_VA, OFF_QB, OFF_KB, OFF_VB, OFF_C, OFF_G, OFF_A, OFF_B = 0, 256, 512, 768, 1152, 1280, 1408, 2560, 2944, 2956
BT = [(0, 256), (256, 512), (768, 512), (1280, 512), (1792, 512)]
EPS = 1e-6
SAME_ENG_SYNC = True


class Op:
    __slots__ = ("eng", "fn", "rk", "wk", "dma", "deps", "signal", "count", "semkey")


class Prog:
    def __init__(self, nc):
        self.nc = nc
        self.ops = []
        self.lastw = {}
        self.readers = {}
        self.last_eng = {}

    def add(self, eng, fn, reads=(), writes=(), dma=None):
        op = Op()
        op.eng = eng; op.fn = fn; op.dma = dma
        op.rk = set(reads); op.wk = set(writes)
        if dma is not None:
            op.rk.add(("slot", dma)); op.wk.add(("slot", dma))
        op.signal = dma is not None; op.count = 0
        op.semkey = ("dma", dma) if dma is not None else eng
        deps = []
        seen = set()

        def consider(d, raw):
            if d is None or id(d) in seen:
                return
            if d.dma is None and dma is None and d.eng == eng:
                if eng == "pe" or not raw or not SAME_ENG_SYNC:
                    return
            seen.add(id(d)); deps.append(d); d.signal = True

        for k in op.rk:
            consider(self.lastw.get(k), True)
        for k in op.wk:
            consider(self.lastw.get(k), False)
            for r in self.readers.get(k, ()):
                consider(r, False)
        op.deps = deps
        for k in op.wk:
            self.lastw[k] = op
            self.readers[k] = []
        for k in op.rk:
            if k not in op.wk:
                self.readers.setdefault(k, []).append(op)
        self.ops.append(op)
        if dma is None:
            self.last_eng[eng] = op
        return op

    def barrier(self, engs=("pe", "act", "dve")):
        lasts = [self.last_eng[e] for e in engs if e in self.last_eng]
        for e in tuple(engs) + ("sp",):
            op = Op()
            op.eng = e; op.fn = None; op.dma = None; op.rk = set(); op.wk = set()
            op.signal = False; op.count = 0; op.semkey = e
            op.deps = [d for d in lasts if d.eng != e]
            for d in op.deps:
                d.signal = True
            self.ops.append(op)

    def emit(self, stack):
        nc = self.nc
        semkeys = []
        for op in self.ops:
            if op.signal and op.semkey not in semkeys:
                semkeys.append(op.semkey)
        sems = {}
        for i, k in enumerate(semkeys):
            sems[k] = stack.enter_context(nc.semaphore("s%d" % i))
        cnt = {}
        for op in self.ops:
            if op.signal:
                inc = 16 if op.dma is not None else 1
                cnt[op.semkey] = cnt.get(op.semkey, 0) + inc
                op.count = cnt[op.semkey]
        per = {"pe": [], "act": [], "dve": [], "pool": [], "sp": []}
        for op in self.ops:
            per[op.eng].append(op)
        block = stack.enter_context(nc.Block())

        def run(e, lst):
            waited = {}
            for op in lst:
                need = {}
                for d in op.deps:
                    if d.count > need.get(d.semkey, 0):
                        need[d.semkey] = d.count
                for k, v in need.items():
                    if waited.get(k, 0) < v:
                        e.wait_ge(sems[k], v)
                        waited[k] = v
                if op.fn is None:
                    continue
                try:
                    inst = op.fn(e)
                except BaseException:
                    print('FAILED OP', op.eng, sorted(map(str, op.rk)), sorted(map(str, op.wk)))
                    raise
                if op.signal:
                    inst.then_inc(sems[op.semkey], 16 if op.dma is not None else 1)

        @block.tensor
        def _(e):
            run(e, per["pe"])

        @block.scalar
        def _(e):
            run(e, per["act"])

        @block.vector
        def _(e):
            run(e, per["dve"])

        @block.gpsimd
        def _(e):
            run(e, per["pool"])

        @block.sync
        def _(e):
            run(e, per["sp"])


def ks(name, *ranges):
    out = [(name,)]
    for r in ranges:
        if isinstance(r, int):
            r = [r]
        out = [o + (i,) for o in out for i in r]
    return out


def _rope_tables(dim):
    half = dim // 2
    nf = half // 2
    inv = 10000.0 ** (-np.arange(0, half, 2, dtype=np.float32) / half)
    pos_row = np.repeat(np.arange(32, dtype=np.float32), 64)
    pos_col = np.tile(np.arange(64, dtype=np.float32), 32)
    cos = np.zeros((128, NLAT), np.float32); sin = np.zeros((128, NLAT), np.float32)
    perm = np.zeros((128, 128), np.float32)
    for p in range(128):
        e = p % dim
        hid = e // half
        w = e % half
        f = w % nf
        second = w // nf
        pos = pos_row if hid == 0 else pos_col
        ang = (pos * inv[f]).astype(np.float32)
        cos[p] = np.cos(ang)
        sin[p] = np.sin(ang) * (1.0 if second else -1.0)
        partner = p - nf if second else p + nf
        perm[partner, p] = 1.0
    return cos, sin, perm


def _consts():
    c = {}
    i = np.arange(128)
    I = (i[:, None] == i[None, :]).astype(np.float32)
    c["ident"] = I
    bd32 = ((i[:, None] // 32) == (i[None, :] // 32)).astype(np.float32)
    bd64 = ((i[:, None] // 64) == (i[None, :] // 64)).astype(np.float32)
    c["bd32"] = bd32; c["bd64"] = bd64; c["ones"] = np.ones((128, 128), np.float32)
    cosA, sinA, permA = _rope_tables(32)
    cosB, sinB, permB = _rope_tables(64)
    c["permA"] = permA; c["permB"] = permB
    c["cosA"] = cosA; c["sinA"] = sinA; c["cosB"] = cosB; c["sinB"] = sinB
    c["mprev"] = (i[:, None] >= i[None, :]).astype(np.float32)
    c["mnext"] = (i[:, None] <= i[None, :]).astype(np.float32)
    le = (i[:, None] <= i[None, :]).astype(np.float32)
    ge = (i[:, None] >= i[None, :]).astype(np.float32)
    lt = (i[:, None] < i[None, :]).astype(np.float32)
    gt = (i[:, None] > i[None, :]).astype(np.float32)
    c["tri_f"] = le
    c["tri_b"] = ge
    c["msk4"] = np.concatenate([gt, lt, gt, lt], axis=1)
    c["strict4"] = np.concatenate([gt, lt, gt, lt], axis=1)
    c["incl4"] = np.concatenate([ge, le, ge, le], axis=1)
    c["bd32_4"] = np.tile(bd32, (1, 4))
    c["m1_4"] = np.tile(bd64 - bd32, (1, 4))
    c["m2_4"] = np.tile(1.0 - bd64, (1, 4))
    c["ident4"] = np.tile(I, (1, 4))
    return c


CONST_F32 = ["ident", "permA", "permB", "tri_f", "tri_b", "msk4", "strict4", "incl4", "bd32_4", "m1_4", "m2_4", "ident4",
             "cosA", "sinA", "cosB", "sinB"]
CONST_BF = ["ident", "bd32", "bd64", "ones", "mprev", "mnext"]


def _pack(names, cdict):
    offs = {}
    cols = []
    o = 0
    for n in names:
        a = cdict[n]
        offs[n] = (o, a.shape[1])
        o += a.shape[1]
        cols.append(a)
    return np.concatenate(cols, axis=1), offs


PB = {}
_o = 0
for _n, _s in [("a_qnorm", 32), ("a_knorm", 32), ("a_lambda", 128), ("a_subln", 64), ("b_qnorm", 64), ("b_knorm", 64),
               ("b_sink", 6), ("c_A_log", 12), ("c_dt_bias", 12), ("c_onorm", 64)]:
    PB[_n] = (_o, _s)
    _o += _s
PBN = _o
PPN = 4 + 45


def build(upto="all", dbg=False):
    nc = bass.Bass("TRN2", target_bir_lowering=False)
    cd = _consts()
    cF, offF = _pack(CONST_F32, cd)
    cB, offB = _pack(CONST_BF, cd)
    NCF = cF.shape[1]; NCB = cB.shape[1]
    NCF_RES = offF["cosA"][0]

    dt = nc.dram_tensor
    xT = dt("xT", [D, T], F32, kind="ExternalInput").ap()
    cT = dt("cT", [128, 16], F32, kind="ExternalInput").ap()
    w_mod = dt("w_mod", [DEPTH, D, 9 * D], F32, kind="ExternalInput").ap()
    b_modT = dt("b_modT", [128, DEPTH * 72], F32, kind="ExternalInput").ap()
    f1w1 = dt("ffn1_w1", [DEPTH, D, 2 * DFF], F32, kind="ExternalInput").ap()
    f1w2 = dt("ffn1_w2", [DEPTH, DFF, D], F32, kind="ExternalInput").ap()
    f2w1 = dt("ffn2_w1", [DEPTH, D, 2 * DFF], F32, kind="ExternalInput").ap()
    f2w2 = dt("ffn2_w2", [DEPTH, DFF, D], F32, kind="ExternalInput").ap()
    w_in = dt("w_in", [DEPTH, D, INC], F32, kind="ExternalInput").ap()
    w_out = dt("w_out", [DEPTH, D, D], F32, kind="ExternalInput").ap()
    pbc = dt("pbc", [128, DEPTH * PBN], F32, kind="ExternalInput").ap()
    ppar = dt("ppar", [128, DEPTH * PPN], F32, kind="ExternalInput").ap()
    constF = dt("constF", [128, NCF], F32, kind="ExternalInput").ap()
    constB = dt("constB", [128, NCB], BF16, kind="ExternalInput").ap()
    outT = dt("outT", [D, NLAT], F32, kind="ExternalOutput").ap()
    if dbg:
        dbgT = dt("dbgT", [D, T], F32, kind="ExternalOutput").ap()
        dbg2 = dt("dbg2", [128, 12800], F32, kind="ExternalOutput").ap()

    stack = contextlib.ExitStack()
    with stack:
        sb = lambda n, s, d: stack.enter_context(nc.sbuf_tensor(n, s, d))
        hT = sb("hT", [128, 8, T], F32)
        hn = sb("hn", [128, 8, T], BF16)
        modT = sb("modT", [128, DEPTH, 2, 72], F32)
        bmod = sb("bmod", [128, DEPTH * 72], F32)
        cTs = sb("cTs", [128, 16], F32)
        pb = sb("pb", [128, DEPTH * PBN], F32)
        pp = sb("pp", [128, DEPTH * PPN], F32)
        kF = sb("kF", [128, NCF_RES], F32)
        kB = sb("kB", [128, NCB], BF16)
        w1b = [sb("w1b%d" % i, [128, 8, 512], BF16) for i in range(2)]
        w2b = [sb("w2b%d" % i, [128, 2, 1024], BF16) for i in range(2)]
        ARENA_W = 12800
        arena = sb("arena", [128, ARENA_W], F32)
        psl = [stack.enter_context(nc.psum_tensor("ps%d" % i, [128, 512], F32)) for i in range(8)]

        def cF_(n):
            o, w = offF[n]
            return kF[:, o:o + w]

        def cB_(n):
            o, w = offB[n]
            return kB[:, o:o + w]

        def carve(off_words, nwords, dtype=F32):
            a = arena[:, off_words:off_words + nwords]
            return a.bitcast(dtype) if dtype != F32 else a

        P = Prog(nc)
        A = P.add

        A("sp", lambda e: e.dma_start(out=cTs[:], in_=cT), writes=ks("cTs"), dma="ld_c")
        A("sp", lambda e: e.dma_start(out=bmod[:], in_=b_modT), writes=ks("bmod"), dma="ld_b")
        A("sp", lambda e: e.dma_start(out=pb[:], in_=pbc), writes=ks("pb"), dma="ld_pb")
        A("sp", lambda e: e.dma_start(out=pp[:], in_=ppar), writes=ks("pp"), dma="ld_pp")
        A("sp", lambda e: e.dma_start(out=kF[:], in_=constF[:, 0:NCF_RES]), writes=ks("kF"), dma="ld_kF")
        A("sp", lambda e: e.dma_start(out=kB[:], in_=constB), writes=ks("kB"), dma="ld_kB")
        for c in range(8):
            A("sp", lambda e, c=c: e.dma_start(out=hT[:, c, :], in_=xT[c * 128:(c + 1) * 128, :]),
              writes=ks("hT", c, range(5)), dma="ld_x%d" % c)

        sc = sb("silu_c", [128, 16], F32)
        A("act", lambda e: e.activation(out=sc[:], in_=cTs[:], func=AF.Silu), reads=ks("cTs"), writes=ks("sc"))
        wm = [carve(i * 4096, 4096).rearrange("p (k n) -> p k n", k=8) for i in range(2)]
        gi = 0
        for l in range(DEPTH):
            for g in range(18):
                buf = wm[gi % 2]; bk = ("wm", gi % 2)
                A("sp", lambda e, buf=buf, l=l, g=g: e.dma_start(
                    out=buf, in_=w_mod[l, :, g * 512:(g + 1) * 512].rearrange("(k p) n -> p k n", p=128)),
                  writes=[bk], dma="wm%d" % (gi % 2))
                ps = psl[gi % 2]
                for n4 in range(4):
                    for k in range(8):
                        A("pe", lambda e, ps=ps, buf=buf, n4=n4, k=k: e.matmul(
                            ps[:, n4 * 2:n4 * 2 + 2], buf[:, k, n4 * 128:(n4 + 1) * 128],
                            sc[:].rearrange("p (w k) -> p k w", w=2)[:, k, :], start=(k == 0), stop=(k == 7)),
                          reads=[bk] + ks("sc"), writes=[("ps", gi % 2)])
                for w in range(2):
                    A("dve", lambda e, ps=ps, l=l, g=g, w=w: e.tensor_tensor(
                        modT[:, l, w, g * 4:(g + 1) * 4], ps[:, 0:8].rearrange("p (n w) -> p n w", w=2)[:, :, w],
                        bmod[:, l * 72 + g * 4: l * 72 + (g + 1) * 4], ALU.add),
                      reads=[("ps", gi % 2)] + ks("bmod"), writes=ks("modT", l))
                gi += 1
            for j in (1, 4, 7):
                A("dve", lambda e, l=l, j=j: e.tensor_scalar(modT[:, l, :, j * 8:(j + 1) * 8], modT[:, l, :, j * 8:(j + 1) * 8],
                                                            1.0, None, ALU.add), reads=ks("modT", l), writes=ks("modT", l))
            for j in (2, 8):
                A("dve", lambda e, l=l, j=j: e.tensor_scalar(modT[:, l, :, j * 8:(j + 1) * 8], modT[:, l, :, j * 8:(j + 1) * 8],
                                                            0.5, None, ALU.mult), reads=ks("modT", l), writes=ks("modT", l))
        P.barrier()

        def mod_sc(l, w, j, c):
            return modT[:, l, w, j * 8 + c: j * 8 + c + 1]

        def norm_mod(l, j, tiles):
            for ti in tiles:
                t0, tl = BT[ti]
                w = 1 if ti == 0 else 0
                sq = carve((ti % 2) * 2048, 2048, BF16).rearrange("p (k n) -> p k n", k=8)
                rs = carve(4096 + (ti % 2) * 512, 512)
                ps = psl[ti % 2]
                for k in range(8):
                    A("act", lambda e, sq=sq, k=k, t0=t0, tl=tl: e.activation(out=sq[:, k, 0:tl], in_=hT[:, k, t0:t0 + tl], func=AF.Square),
                      reads=ks("hT", k, ti), writes=[("sq", ti % 2, k)])
                for k in range(8):
                    A("pe", lambda e, ps=ps, sq=sq, k=k, tl=tl: e.matmul(ps[:, 0:tl], cB_("ones"), sq[:, k, 0:tl], start=(k == 0), stop=(k == 7)),
                      reads=[("sq", ti % 2, k)] + ks("kB"), writes=[("ps", ti % 2)])
                A("act", lambda e, ps=ps, rs=rs, tl=tl: e.activation(out=rs[:, 0:tl], in_=ps[:, 0:tl], func=AF.Sqrt, scale=1.0 / D, bias=epsb[:, 0:1]),
                  reads=[("ps", ti % 2)] + ks("epsb"), writes=[("rs", ti % 2)])
                A("dve", lambda e, rs=rs, tl=tl: e.reciprocal(rs[:, 0:tl], rs[:, 0:tl]), reads=[("rs", ti % 2)], writes=[("rs", ti % 2)])
                for k in range(8):
                    tmp = carve(5120 + (k % 2) * 512, 512)
                    A("dve", lambda e, tmp=tmp, k=k, rs=rs, t0=t0, tl=tl: e.tensor_tensor(tmp[:, 0:tl], hT[:, k, t0:t0 + tl], rs[:, 0:tl], ALU.mult),
                      reads=ks("hT", k, ti) + [("rs", ti % 2)], writes=[("ntmp", k % 2)])
                    A("act", lambda e, tmp=tmp, k=k, t0=t0, tl=tl, w=w: e.activation(
                        out=hn[:, k, t0:t0 + tl], in_=tmp[:, 0:tl], func=AF.Identity, scale=mod_sc(l, w, j + 1, k), bias=mod_sc(l, w, j, k)),
                      reads=[("ntmp", k % 2)] + ks("modT", l), writes=ks("hn", k, ti))

        epsb = sb("epsb", [128, 1], F32)
        oneb = sb("oneb", [128, 1], F32)
        A("dve", lambda e: e.memset(oneb[:], 1.0), writes=["oneb"])
        A("dve", lambda e: e.memset(epsb[:], EPS), writes=ks("epsb"))

        def ffn(l, j, w1d, w2d, tiles):
            norm_mod(l, j, tiles)
            P.barrier()
            for part in range(11):
                wb = w1b[part % 2]; w2 = w2b[part % 2]
                wbv = wb[:].rearrange("p k (g n) -> p k g n", g=2)
                for g in range(2):
                    A("pool", lambda e, wbv=wbv, g=g, part=part: e.dma_start(
                        out=wbv[:, :, g, :], in_=w1d[l, :, g * DFF + part * 256: g * DFF + (part + 1) * 256].rearrange("(k p) n -> p k n", p=128)),
                      writes=[("w1b", part % 2, g)], dma="w1b%d%d" % (part % 2, g))
                A("pool", lambda e, w2=w2, part=part: e.dma_start(
                    out=w2[:], in_=w2d[l, part * 256:(part + 1) * 256, :].rearrange("(c p) n -> p c n", p=128)),
                  writes=[("w2b", part % 2)], dma="w2b%d" % (part % 2))
                act = carve((part % 2) * 2304, 2304, BF16).rearrange("p (c n) -> p c n", c=2)
                for fc in range(2):
                    for ti in tiles:
                        t0, tl = BT[ti]
                        pg = psl[(2 * (fc * 5 + ti)) % 4]; pu = psl[(2 * (fc * 5 + ti)) % 4 + 1]
                        kg = ("ps", (2 * (fc * 5 + ti)) % 4); ku = ("ps", (2 * (fc * 5 + ti)) % 4 + 1)
                        for k in range(8):
                            A("pe", lambda e, pg=pg, wbv=wbv, k=k, fc=fc, t0=t0, tl=tl: e.matmul(
                                pg[:, 0:tl], wbv[:, k, 0, fc * 128:(fc + 1) * 128], hn[:, k, t0:t0 + tl], start=(k == 0), stop=(k == 7)),
                              reads=[("w1b", part % 2, 0)] + ks("hn", k, ti), writes=[kg])
                        for k in range(8):
                            A("pe", lambda e, pu=pu, wbv=wbv, k=k, fc=fc, t0=t0, tl=tl: e.matmul(
                                pu[:, 0:tl], wbv[:, k, 1, fc * 128:(fc + 1) * 128], hn[:, k, t0:t0 + tl], start=(k == 0), stop=(k == 7)),
                              reads=[("w1b", part % 2, 1)] + ks("hn", k, ti), writes=[ku])
                        st = carve(4608 + ((fc * 5 + ti) % 2) * 512, 512)
                        skey = ("silut", (fc * 5 + ti) % 2)
                        A("act", lambda e, st=st, pg=pg, tl=tl: e.activation(out=st[:, 0:tl], in_=pg[:, 0:tl], func=AF.Silu),
                          reads=[kg], writes=[skey])
                        A("dve", lambda e, st=st, pu=pu, act=act, fc=fc, t0=t0, tl=tl: e.tensor_tensor(
                            act[:, fc, t0:t0 + tl], pu[:, 0:tl], st[:, 0:tl], ALU.mult),
                          reads=[ku, skey], writes=[("act", part % 2, fc, ti)])
                for ti in tiles:
                    t0, tl = BT[ti]
                    w = 1 if ti == 0 else 0
                    for dc in range(8):
                        po = psl[4 + (ti * 8 + dc) % 4]; ko = ("ps", 4 + (ti * 8 + dc) % 4)
                        for fc in range(2):
                            A("pe", lambda e, po=po, w2=w2, fc=fc, dc=dc, act=act, t0=t0, tl=tl: e.matmul(
                                po[:, 0:tl], w2[:, fc, dc * 128:(dc + 1) * 128], act[:, fc, t0:t0 + tl], start=(fc == 0), stop=(fc == 1)),
                              reads=[("w2b", part % 2), ("act", part % 2, fc, ti)], writes=[ko])
                        A("dve", lambda e, po=po, dc=dc, t0=t0, tl=tl, w=w: e.scalar_tensor_tensor(
                            hT[:, dc, t0:t0 + tl], po[:, 0:tl], mod_sc(l, w, j + 2, dc), hT[:, dc, t0:t0 + tl], ALU.mult, ALU.add),
                          reads=[ko] + ks("modT", l) + ks("hT", dc, ti), writes=ks("hT", dc, ti))
            P.barrier()

        def ACTF(out, in_, func, r, w, **kw):
            A("act", lambda e: e.activation(out=out, in_=in_, func=func, **kw), reads=r, writes=w)

        def TT(eng, out, a, b, op, r, w):
            A(eng, lambda e: e.tensor_tensor(out, a, b, op), reads=r, writes=w)

        def TS(eng, out, a, s1, s2, op0, op1, r, w):
            if op1 is None:
                A(eng, lambda e: e.tensor_scalar(out, a, s1, None, op0), reads=r, writes=w)
            else:
                A(eng, lambda e: e.tensor_scalar(out, a, s1, s2, op0, op1), reads=r, writes=w)

        def STT(out, a, s, b, op0, op1, r, w):
            A("dve", lambda e: e.scalar_tensor_tensor(out, a, s, b, op0, op1), reads=r, writes=w)

        def MM(out, lhsT, rhs, r, w, start=True, stop=True, tp=None):
            if tp is None:
                A("pe", lambda e: e.matmul(out, lhsT, rhs, start=start, stop=stop), reads=r, writes=w)
            else:
                A("pe", lambda e: e.matmul(out, lhsT, rhs, start=start, stop=stop, tile_position=tp), reads=r, writes=w)

        def TR(out, in_, ident, r, w):
            A("pe", lambda e: e.transpose(out, in_, ident), reads=r, writes=w)

        def load_win(l, c0, n, slot):
            A("pool", lambda e: e.dma_start(out=w1b[slot][:, :, 0:n], in_=w_in[l, :, c0:c0 + n].rearrange("(k p) n -> p k n", p=128)),
              writes=[("w1b", slot, 0), ("w1b", slot, 1)], dma="w1b%d0" % slot)

        def pbc_(l, name):
            o, n = PB[name]
            return pb[:, l * PBN + o: l * PBN + o + n]

        TMP0 = 6948

        def normrope(l, c0, nchunks, dim, lhs_fn, ppcols, dsts, tiles, bdname, permname, cosname, sinname, loader=None):
            if loader is None:
                load_win(l, c0, 512, 0)
            else:
                loader()
            WK = [("w1b", 0, 0), ("w1b", 0, 1)]
            tabc = carve(TMP0 + 0, 512); tabs = carve(TMP0 + 512, 512)
            t1 = carve(TMP0 + 1024, 512); t2 = carve(TMP0 + 1536, 512)
            for ti in tiles:
                t0, tl = BT[ti]
                if ti > 0:
                    oc = offF[cosname][0] + (ti - 1) * 512; os_ = offF[sinname][0] + (ti - 1) * 512
                    A("sp", lambda e, oc=oc: e.dma_start(out=tabc, in_=constF[:, oc:oc + 512]), writes=["tabc"], dma="tabc")
                    A("sp", lambda e, os_=os_: e.dma_start(out=tabs, in_=constF[:, os_:os_ + 512]), writes=["tabs"], dma="tabs")
                for cq in range(nchunks):
                    b = cq % 2
                    raw = carve(TMP0 + 2048 + b * 512, 512); sq = carve(TMP0 + 3072 + b * 256, 256, BF16)
                    rinv = carve(TMP0 + 3584 + b * 512, 512); xn = carve(TMP0 + 4608 + b * 512, 512)
                    ps = psl[b]; ps2 = psl[2 + b]; ps3 = psl[4 + b]
                    for k in range(8):
                        MM(ps[:, 0:tl], lhs_fn(cq, k), hn[:, k, t0:t0 + tl], WK + ks("hn", k, ti), [("ps", b)], start=(k == 0), stop=(k == 7))
                    ACTF(raw[:, 0:tl], ps[:, 0:tl], AF.Copy, [("ps", b)], [("raw", b)])
                    ACTF(sq[:, 0:tl], ps[:, 0:tl], AF.Square, [("ps", b)], [("sq", b)])
                    MM(ps2[:, 0:tl], cB_(bdname), sq[:, 0:tl], [("sq", b), "kB"], [("ps", 2 + b)])
                    ACTF(rinv[:, 0:tl], ps2[:, 0:tl], AF.Sqrt, [("ps", 2 + b), "epsb"], [("rinv", b)], scale=1.0 / dim, bias=epsb[:, 0:1])
                    A("dve", lambda e, rinv=rinv, tl=tl: e.reciprocal(rinv[:, 0:tl], rinv[:, 0:tl]), reads=[("rinv", b)], writes=[("rinv", b)])
                    pc = ppcols[cq]
                    STT(xn[:, 0:tl], raw[:, 0:tl], pp[:, l * PPN + pc: l * PPN + pc + 1], rinv[:, 0:tl], ALU.mult, ALU.mult,
                        [("raw", b), ("rinv", b), "pp"], [("xn", b)])
                    dst, dkey = dsts[cq](t0, tl, ti)
                    vw = (lambda a: a.rearrange("p (a b) -> p a b", b=128)) if len(dst.shape) == 3 else (lambda a: a)
                    if ti == 0:
                        ACTF(dst, vw(xn[:, 0:tl]), AF.Copy, [("xn", b)], [dkey])
                    else:
                        MM(ps3[:, 0:tl], cF_(permname), xn[:, 0:tl], [("xn", b), "kF"], [("ps", 4 + b)])
                        TT("dve", t1[:, 0:tl], ps3[:, 0:tl], tabs[:, 0:tl], ALU.mult, [("ps", 4 + b), "tabs"], ["t1"])
                        TT("dve", t2[:, 0:tl], xn[:, 0:tl], tabc[:, 0:tl], ALU.mult, [("xn", b), "tabc"], ["t2"])
                        TT("dve", dst, vw(t1[:, 0:tl]), vw(t2[:, 0:tl]), ALU.add, ["t1", "t2"], [dkey])

        def vproj(l, c0, n, nh, vdst, vkey):
            load_win(l, c0, n, 1)
            WK = [("w1b", 1, 0), ("w1b", 1, 1)]
            A("dve", lambda e: e.memset(vdst[:, :, :, 64:65], 1.0), writes=[vkey])
            for n_ in range(NT):
                b = n_ % 2
                ti = 0 if n_ < 2 else 1 + (n_ - 2) // 4
                ps = psl[6 + b]
                for k in range(8):
                    MM(ps[:, 0:n], hn[:, k, n_ * 128:(n_ + 1) * 128], w1b[1][:, k, 0:n], WK + ks("hn", k, ti), [("ps", 6 + b)], start=(k == 0), stop=(k == 7))
                if b == 0:
                    ACTF(vdst[:, n_, :, 0:64], ps[:, 0:n].rearrange("p (h d) -> p h d", d=64), AF.Copy, [("ps", 6 + b)], [vkey])
                else:
                    A("dve", lambda e, ps=ps, n_=n_: e.tensor_copy(vdst[:, n_, :, 0:64], ps[:, 0:n].rearrange("p (h d) -> p h d", d=64)),
                      reads=[("ps", 6 + b)], writes=[vkey])

        def wout_part(l, row0, nch, yT, ykeys, tiles):
            nld = min(nch, 2)
            A("pool", lambda e: e.dma_start(out=w2b[0][:, 0:nld, :], in_=w_out[l, row0:row0 + nld * 128, :].rearrange("(c p) n -> p c n", p=128)),
              writes=[("w2b", 0)], dma="w2b0")
            if nch > 2:
                A("pool", lambda e: e.dma_start(out=w2b[1][:, 0:1, :], in_=w_out[l, row0 + 256:row0 + 384, :].rearrange("(c p) n -> p c n", p=128)),
                  writes=[("w2b", 1)], dma="w2b1")
            for ti in tiles:
                t0, tl = BT[ti]
                w = 1 if ti == 0 else 0
                for dc in range(8):
                    po = psl[4 + dc % 4]; ko = ("ps", 4 + dc % 4)
                    for c in range(nch):
                        wsrc = w2b[0][:, c, dc * 128:(dc + 1) * 128] if c < 2 else w2b[1][:, 0, dc * 128:(dc + 1) * 128]
                        MM(po[:, 0:tl], wsrc, yT[:, c, t0:t0 + tl], [("w2b", 0), ("w2b", 1)] + ykeys, [ko], start=(c == 0), stop=(c == nch - 1))
                    STT(hT[:, dc, t0:t0 + tl], po[:, 0:tl], mod_sc(l, w, 5, dc), hT[:, dc, t0:t0 + tl], ALU.mult, ALU.add,
                        [ko] + ks("modT", l) + ks("hT", dc, ti), ks("hT", dc, ti))

        def mixer_A(l, need_ctx):
            lam_init = 0.8 - 0.6 * float(np.exp(-0.3 * l))
            qTa = carve(0, 2304, BF16).rearrange("p (c n) -> p c n", c=2)
            kTa = carve(2304, 2304, BF16).rearrange("p (c n) -> p c n", c=2)
            va = carve(4608, 2340, BF16).rearrange("p (n h d) -> p n h d", n=NT, h=4)
            dsts = [lambda t0, tl, ti, c=c: (qTa[:, c, t0:t0 + tl], ("qTa", c, ti)) for c in range(2)] + \
                   [lambda t0, tl, ti, c=c: (kTa[:, c, t0:t0 + tl], ("kTa", c, ti)) for c in range(2)]
            normrope(l, OFF_QA, 4, 32, lambda cq, k: w1b[0][:, k, cq * 128:(cq + 1) * 128], [0, 0, 1, 1], dsts, [0, 1, 2, 3, 4],
                     "bd32", "permA", "cosA", "sinA")
            vproj(l, OFF_VA, 256, 4, va, "va")
            P.barrier()
            sm = carve(TMP0, 512)
            lamt = sm[:, 0:64]; lam2 = sm[:, 64:66]; neglam = sm[:, 66:67]; subw = sm[:, 128:192]
            al = pbc_(l, "a_lambda")
            TT("dve", lamt.rearrange("p (a b) -> p a b", a=2), al.rearrange("p (a t b) -> p a t b", a=2, t=2)[:, :, 0, :],
               al.rearrange("p (a t b) -> p a t b", a=2, t=2)[:, :, 1, :], ALU.mult, ["pb"], ["lamt"])
            A("dve", lambda e: e.reduce_sum(lam2, lamt.rearrange("p (a b) -> p a b", a=2), AX.X), reads=["lamt"], writes=["lam2"])
            ACTF(lam2, lam2, AF.Exp, ["lam2"], ["lam2"])
            TT("dve", neglam, lam2[:, 1:2], lam2[:, 0:1], ALU.subtract, ["lam2"], ["neglam"])
            TS("dve", neglam, neglam, -lam_init, None, ALU.add, None, ["neglam"], ["neglam"])
            TS("dve", subw, pbc_(l, "a_subln"), 1.0 - lam_init, None, ALU.mult, None, ["pb"], ["subw"])
            eT = carve(6948 + 512, 2304, BF16).rearrange("p (m n) -> p m n", m=2)
            ytok = [carve(6948 + 512 + 2304 + i * 128, 128, BF16) for i in range(2)]
            fin = carve(6948 + 512 + 2304 + 256, 256)
            yAT = carve(10276, 2304, BF16).rearrange("p (c n) -> p c n", c=2)
            qtiles = list(range(2, NT)) + ([0, 1] if need_ctx else [])
            for qi, qt in enumerate(qtiles):
                kts = list(range(NT)) if qt >= 2 else [0, 1]
                tiq = 0 if qt < 2 else 1 + (qt - 2) // 4
                yt = ytok[qi % 2]; ykey = ("ytok", qi % 2)
                for h in range(4):
                    c = h // 2
                    for m in range(2):
                        base = (h % 2) * 64 + 32 * m
                        tp = (96, 0) if base == 96 else None
                        for kg in range(0, len(kts), 4):
                            grp = kts[kg:kg + 4]
                            pi = (m * 5 + kg // 4) % 4
                            ps = psl[pi]
                            for j, kt in enumerate(grp):
                                tik = 0 if kt < 2 else 1 + (kt - 2) // 4
                                MM(ps[:, j * 128:(j + 1) * 128], kTa[base:base + 32, c, kt * 128:(kt + 1) * 128],
                                   qTa[base:base + 32, c, qt * 128:(qt + 1) * 128], [("kTa", c, tik), ("qTa", c, tiq)], [("ps", pi)], tp=tp)
                            ACTF(eT[:, m, kg * 128:(kg + len(grp)) * 128], ps[:, 0:len(grp) * 128], AF.Exp, [("ps", pi)], [("eT", m)], scale=32 ** -0.5)
                    for m in range(2):
                        acc = psl[4 + m]
                        for j, kt in enumerate(kts):
                            MM(acc[:, 0:65], eT[:, m, j * 128:(j + 1) * 128], va[:, kt, h, :], [("eT", m), "va"], [("ps", 4 + m)],
                               start=(j == 0), stop=(j == len(kts) - 1))
                    rr = fin[:, 0:2]; o = fin[:, 64:128]; ss = fin[:, 2:3]; junk = fin[:, 128:192]
                    A("dve", lambda e, rr=rr: e.reciprocal(rr[:, 0:1], psl[4][:, 64:65]), reads=[("ps", 4)], writes=["rr"])
                    A("dve", lambda e, rr=rr: e.reciprocal(rr[:, 1:2], psl[5][:, 64:65]), reads=[("ps", 5)], writes=["rr"])
                    TT("dve", rr[:, 1:2], rr[:, 1:2], neglam, ALU.mult, ["rr", "neglam"], ["rr"])
                    TS("dve", o, psl[4][:, 0:64], rr[:, 0:1], None, ALU.mult, None, [("ps", 4), "rr"], ["o"])
                    STT(o, psl[5][:, 0:64], rr[:, 1:2], o, ALU.mult, ALU.add, [("ps", 5), "rr", "o"], ["o"])
                    A("act", lambda e, junk=junk, o=o, ss=ss: e.activation(out=junk, in_=o, func=AF.Square, accum_out=ss), reads=["o"], writes=["ss", "junk"])
                    ACTF(ss, ss, AF.Sqrt, ["ss", "epsb"], ["ss"], scale=1.0 / 64, bias=epsb[:, 0:1])
                    A("dve", lambda e, ss=ss: e.reciprocal(ss, ss), reads=["ss"], writes=["ss"])
                    STT(yt[:, h * 64:(h + 1) * 64], o, ss, subw, ALU.mult, ALU.mult, ["o", "ss", "subw"], [ykey])
                for c in range(2):
                    pt = psl[6 + c]
                    TR(pt[:].bitcast(BF16)[:, 0:128], yt[:, c * 128:(c + 1) * 128], cB_("ident"), [ykey, "kB"], [("ps", 6 + c)])
                    A("dve", lambda e, pt=pt, c=c, qt=qt: e.tensor_copy(yAT[:, c, qt * 128:(qt + 1) * 128], pt[:].bitcast(BF16)[:, 0:128]),
                      reads=[("ps", 6 + c)], writes=[("yAT", tiq)])
            wout_part(l, 0, 2, yAT, [("yAT", i) for i in range(5)], [0, 1, 2, 3, 4] if need_ctx else [1, 2, 3, 4])
            P.barrier()

        def mixer_B(l, need_ctx):
            qTb = carve(0, 3456, BF16).rearrange("p (n c q) -> p n c q", n=NT, c=3)
            kTb = carve(3456, 1152, BF16)
            vb = carve(4608, 1170, BF16).rearrange("p (n h d) -> p n h d", n=NT, h=2)
            dsts = [lambda t0, tl, ti, c=c: (qTb[:, t0 // 128:(t0 + tl) // 128, c, :], ("qTb", c, ti)) for c in range(3)] + \
                   [lambda t0, tl, ti: (kTb[:, t0:t0 + tl], ("kTb", ti))]

            def lhs(cq, k):
                return w1b[0][:, k, cq * 128:(cq + 1) * 128]

            def loaderB():
                for i in range(3):
                    for g in range(2):
                        cc = OFF_QB + (g * 3 + i) * 64
                        A("pool", lambda e, i=i, g=g, cc=cc: e.dma_start(out=w1b[0][:, :, i * 128 + g * 64:i * 128 + g * 64 + 64],
                                                              in_=w_in[l, :, cc:cc + 64].rearrange("(k p) n -> p k n", p=128)),
                          writes=[("w1b", 0, 0), ("w1b", 0, 1)], dma="w1b00")
                A("pool", lambda e: e.dma_start(out=w1b[0][:, :, 384:512], in_=w_in[l, :, OFF_KB:OFF_KB + 128].rearrange("(k p) n -> p k n", p=128)),
                  writes=[("w1b", 0, 0), ("w1b", 0, 1)], dma="w1b00")
            normrope(l, OFF_QB, 4, 64, lhs, [2, 2, 2, 3], dsts, [0, 1, 2, 3, 4], "bd64", "permB", "cosB", "sinB", loader=loaderB)
            vproj(l, OFF_VB, 128, 2, vb, "vb")
            P.barrier()
            sm = carve(TMP0, 512)
            esink = sm[:, 0:6]
            ACTF(esink, pbc_(l, "b_sink"), AF.Exp, ["pb"], ["esink"])
            eTb = carve(TMP0 + 512, 960, BF16).rearrange("p (j n) -> p j n", j=5)
            ytok = [carve(TMP0 + 1536 + i * 192, 192, BF16) for i in range(2)]
            fin = carve(TMP0 + 2048, 128)
            yBT = carve(TMP0 + 2304, 3456, BF16).rearrange("p (c n) -> p c n", c=3)
            blocks = [(2 + n, n) for n in range(16)] + ([(0, -1), (1, -1)] if need_ctx else [])
            for bi, (qt, n) in enumerate(blocks):
                tiq = 0 if qt < 2 else 1 + (qt - 2) // 4
                kts = [(0, None), (1, None)]
                if n >= 0:
                    if n > 0:
                        kts.append((2 + n - 1, "mprev"))
                    kts.append((2 + n, None))
                    if n < 15:
                        kts.append((2 + n + 1, "mnext"))
                yt = ytok[bi % 2]; ykey = ("ytokb", bi % 2)
                for g in range(2):
                    for j, (kt, mk) in enumerate(kts):
                        tik = 0 if kt < 2 else 1 + (kt - 2) // 4
                        ps = psl[j % 4]
                        MM(ps[:, 0:384], kTb[g * 64:(g + 1) * 64, kt * 128:(kt + 1) * 128], qTb[g * 64:(g + 1) * 64, qt, :, :].rearrange("p c q -> p (c q)"),
                           [("kTb", tik)] + [("qTb", c, tiq) for c in range(3)], [("ps", j % 4)])
                        ACTF(eTb[:, j, :], ps[:, 0:384], AF.Exp, [("ps", j % 4)], [("eTb", j)], scale=0.125)
                        if mk is not None and not os.environ.get("NOMASK"):
                            for i3 in range(3):
                                TT("dve", eTb[:, j, i3 * 128:(i3 + 1) * 128], eTb[:, j, i3 * 128:(i3 + 1) * 128], cB_(mk), ALU.mult,
                                   [("eTb", j), "kB"], [("eTb", j)])
                    for i in range(3):
                        hq = g * 3 + i
                        pa = 4 + hq % 4
                        acc = psl[pa]
                        for j, (kt, mk) in enumerate(kts):
                            MM(acc[:, 0:65], eTb[:, j, i * 128:(i + 1) * 128], vb[:, kt, g, :], [("eTb", j), "vb"], [("ps", pa)],
                               start=(j == 0), stop=(j == len(kts) - 1))
                        den = fin[:, hq:hq + 1]
                        TS("dve", den, acc[:, 64:65], esink[:, hq:hq + 1], None, ALU.add, None, [("ps", pa), "esink"], [("den", hq)])
                        A("dve", lambda e, den=den: e.reciprocal(den, den), reads=[("den", hq)], writes=[("den", hq)])
                        TS("dve", yt[:, hq * 64:(hq + 1) * 64], acc[:, 0:64], den, None, ALU.mult, None, [("ps", pa), ("den", hq)], [ykey])
                for c in range(3):
                    pt = psl[c % 2]
                    TR(pt[:].bitcast(BF16)[:, 0:128], yt[:, c * 128:(c + 1) * 128], cB_("ident"), [ykey, "kB"], [("ps", c % 2)])
                    A("dve", lambda e, pt=pt, c=c, qt=qt: e.tensor_copy(yBT[:, c, qt * 128:(qt + 1) * 128], pt[:].bitcast(BF16)[:, 0:128]),
                      reads=[("ps", c % 2)], writes=[("yBT", tiq)])
            wout_part(l, 256, 3, yBT, [("yBT", i) for i in range(5)], [0, 1, 2, 3, 4] if need_ctx else [1, 2, 3, 4])
            P.barrier()

        def mixer_C(l, need_ctx):
            SMW = 216
            g_all = carve(0 * SMW, SMW).rearrange("p (d n h) -> p d n h", d=2, n=NT)
            beta = carve(1 * SMW, SMW).rearrange("p (d n h) -> p d n h", d=2, n=NT)
            negb = carve(2 * SMW, SMW).rearrange("p (d n h) -> p d n h", d=2, n=NT)
            beg = carve(3 * SMW, SMW).rearrange("p (d n h) -> p d n h", d=2, n=NT)
            eG = carve(4 * SMW, SMW).rearrange("p (d n h) -> p d n h", d=2, n=NT)
            kds = carve(5 * SMW, SMW).rearrange("p (d n h) -> p d n h", d=2, n=NT)
            glv = carve(6 * SMW, SMW).rearrange("p (d n h) -> p d n h", d=2, n=NT)
            QT = carve(1512, 1152, BF16); KT = carve(2664, 1152, BF16)
            KTOK = carve(3816, 1152, BF16).rearrange("p (n c) -> p n c", n=NT)
            VTOK = carve(4968, 1152, BF16).rearrange("p (n c) -> p n c", n=NT)
            W0 = 6120
            load_win(l, OFF_A, 24, 1)
            WK1 = [("w1b", 1, 0), ("w1b", 1, 1)]
            for n_ in range(NT):
                b = n_ % 2
                ti = 0 if n_ < 2 else 1 + (n_ - 2) // 4
                ps = psl[6 + b]
                for k in range(8):
                    MM(ps[:, 0:24], hn[:, k, n_ * 128:(n_ + 1) * 128], w1b[1][:, k, 0:24], WK1 + ks("hn", k, ti), [("ps", 6 + b)], start=(k == 0), stop=(k == 7))
                TT("dve", g_all[:, :, n_, :], ps[:, 0:12].rearrange("p (d h) -> p d h", d=2),
                   pbc_(l, "c_dt_bias").rearrange("p (d h) -> p d h", d=2), ALU.add, [("ps", 6 + b), "pb"], ["g_all"])
                A("dve", lambda e, ps=ps, n_=n_: e.tensor_copy(beta[:, :, n_, :], ps[:, 12:24].rearrange("p (d h) -> p d h", d=2)),
                  reads=[("ps", 6 + b)], writes=["beta"])
            gf = carve(0, SMW); bf_ = carve(SMW, SMW)
            tmpA = carve(W0, SMW); nar = carve(W0 + 256, 12); Gc = carve(W0 + 512, SMW); Gl = carve(W0 + 768, SMW)
            ACTF(gf, gf, AF.Exp, ["g_all"], ["g_all"])
            ACTF(gf, gf, AF.Ln, ["g_all", "oneb"], ["g_all"], bias=oneb[:, 0:1])
            ACTF(nar, pbc_(l, "c_A_log"), AF.Exp, ["pb"], ["nar"])
            TS("dve", nar, nar, -1.0, None, ALU.mult, None, ["nar"], ["nar"])
            for n_ in range(NT):
                TT("dve", g_all[:, :, n_, :], g_all[:, :, n_, :], nar.rearrange("p (d h) -> p d h", d=2), ALU.mult, ["g_all", "nar"], ["g_all"])
            ACTF(bf_, bf_, AF.Exp, ["beta"], ["beta"], scale=-1.0)
            TS("dve", bf_, bf_, 1.0, None, ALU.add, None, ["beta"], ["beta"])
            A("dve", lambda e: e.reciprocal(bf_, bf_), reads=["beta"], writes=["beta"])
            TS("dve", carve(2 * SMW, SMW), bf_, -1.0, None, ALU.mult, None, ["beta"], ["negb"])
            onesF = carve(W0 + 1024, 128)
            A("dve", lambda e: e.memset(onesF, 1.0), writes=["onesF"])
            for d in range(2):
                MM(psl[0][:, d * 108:(d + 1) * 108], cF_("tri_f" if d == 0 else "tri_b"), gf[:, d * 108:(d + 1) * 108], ["g_all", "kF"], [("ps", 0)])
            MM(psl[1][:, 0:SMW], onesF, gf, ["g_all", "onesF"], [("ps", 1)])
            A("dve", lambda e: e.tensor_copy(Gc, psl[0][:, 0:SMW]), reads=[("ps", 0)], writes=["Gc"])
            A("dve", lambda e: e.tensor_copy(Gl, psl[1][:, 0:SMW]), reads=[("ps", 1)], writes=["Gl"])
            ACTF(carve(4 * SMW, SMW), Gc, AF.Exp, ["Gc"], ["eG"])
            TT("dve", tmpA, Gl, Gc, ALU.subtract, ["Gl", "Gc"], ["tmpA"])
            ACTF(carve(5 * SMW, SMW), tmpA, AF.Exp, ["tmpA"], ["kds"])
            ACTF(Gl, Gl, AF.Exp, ["Gl"], ["Gl"])
            TT("dve", carve(3 * SMW, SMW), bf_, carve(4 * SMW, SMW), ALU.mult, ["beta", "eG"], ["beg"])
            Glv = Gl.rearrange("p (d n hp two) -> p d n hp two", d=2, n=NT, two=2)
            for hh in range(2):
                A("dve", lambda e, hh=hh: e.tensor_copy(glv[hh * 64:(hh + 1) * 64, :, :, 0:3], Glv[hh * 64:(hh + 1) * 64, :, :, :, hh]),
                  reads=["Gl"], writes=["glv"])
            P.barrier()
            CST = float(os.environ.get('CSTOP', '99'))
            if CST <= 0:
                return
            order = [list(range(NT)), [1, 0] + list(range(NT - 1, 1, -1))]
            for hp in range(3):
                raw = carve(W0, 2312); acc = carve(W0 + 2312, 2308); sqb = carve(W0 + 4620, 1154, BF16)
                for j3 in range(3):
                    cc = j3 * 3 + hp
                    A("pool", lambda e, j3=j3, cc=cc: e.dma_start(out=w1b[0][:, :, j3 * 128:(j3 + 1) * 128],
                                                            in_=w_in[l, :, OFF_C + cc * 128:OFF_C + (cc + 1) * 128].rearrange("(k p) n -> p k n", p=128)),
                      writes=[("w1b", 0, 0), ("w1b", 0, 1)], dma="w1b00")
                WK0 = [("w1b", 0, 0), ("w1b", 0, 1)]
                A("dve", lambda e: e.memset(raw, 0.0), writes=["raw"])
                for j3 in range(3):
                    cc = j3 * 3 + hp
                    for ti in range(5):
                        t0, tl = BT[ti]
                        b = ti % 2
                        ps = psl[b]
                        for k in range(8):
                            MM(ps[:, 0:tl], w1b[0][:, k, j3 * 128:(j3 + 1) * 128], hn[:, k, t0:t0 + tl], WK0 + ks("hn", k, ti), [("ps", b)], start=(k == 0), stop=(k == 7))
                        ro = 2 + t0 if ti == 0 else 6 + t0
                        ACTF(raw[:, ro:ro + tl], ps[:, 0:tl], AF.Copy, [("ps", b)], ["raw"])
                    cw = lambda k: pp[:, l * PPN + 4 + cc * 5 + k: l * PPN + 4 + cc * 5 + k + 1]
                    TS("dve", acc, raw[:, 0:2308], cw(0), None, ALU.mult, None, ["raw", "pp"], ["acc"])
                    for k in range(1, 5):
                        STT(acc, raw[:, k:k + 2308], cw(k), acc, ALU.mult, ALU.add, ["raw", "pp", "acc"], ["acc"])
                    ACTF(acc, acc, AF.Silu, ["acc"], ["acc"])
                    if j3 < 2:
                        ACTF(sqb, acc, AF.Square, ["acc"], ["sqb"])
                        dstT = QT if j3 == 0 else KT
                        dkey = "QT" if j3 == 0 else "KT"
                        for ti in range(5):
                            t0, tl = BT[ti]
                            a0 = t0 if ti == 0 else 4 + t0
                            b = ti % 2
                            ps2 = psl[2 + b]
                            rin = carve(W0 + 5776, 512)
                            MM(ps2[:, 0:tl], cB_("bd64"), sqb[:, a0:a0 + tl], ["sqb", "kB"], [("ps", 2 + b)])
                            ACTF(rin[:, 0:tl], ps2[:, 0:tl], AF.Sqrt, [("ps", 2 + b), "epsb"], ["rin"], bias=epsb[:, 0:1])
                            A("dve", lambda e, rin=rin, tl=tl: e.reciprocal(rin[:, 0:tl], rin[:, 0:tl]), reads=["rin"], writes=["rin"])
                            if j3 == 0:
                                STT(dstT[:, t0:t0 + tl], acc[:, a0:a0 + tl], 0.125, rin[:, 0:tl], ALU.mult, ALU.mult, ["acc", "rin"], [dkey])
                            else:
                                TT("dve", dstT[:, t0:t0 + tl], acc[:, a0:a0 + tl], rin[:, 0:tl], ALU.mult, ["acc", "rin"], [dkey])
                    else:
                        VT = carve(W0 + 4620, 1152, BF16)
                        ACTF(VT[:, 0:256], acc[:, 0:256], AF.Copy, ["acc"], ["sqb"])
                        ACTF(VT[:, 256:T], acc[:, 260:2308], AF.Copy, ["acc"], ["sqb"])
                    if j3 >= 1:
                        srcT = KT if j3 == 1 else carve(W0 + 4620, 1152, BF16)
                        skey = "KT" if j3 == 1 else "sqb"
                        dtok = KTOK if j3 == 1 else VTOK
                        tkey = "KTOK" if j3 == 1 else "VTOK"
                        for n_ in range(NT):
                            b = n_ % 2
                            pt = psl[4 + b]
                            TR(pt[:].bitcast(BF16)[:, 0:128], srcT[:, n_ * 128:(n_ + 1) * 128], cB_("ident"), [skey, "kB"], [("ps", 4 + b)])
                            if b == 0:
                                A("dve", lambda e, pt=pt, n_=n_, dtok=dtok: e.tensor_copy(dtok[:, n_, :], pt[:].bitcast(BF16)[:, 0:128]),
                                  reads=[("ps", 4 + b)], writes=[tkey])
                            else:
                                ACTF(dtok[:, n_, :], pt[:].bitcast(BF16)[:, 0:128], AF.Copy, [("ps", 4 + b)], [tkey])
                P.barrier()
                if CST <= 1:
                    continue
                S0 = carve(W0, 512); S1 = carve(W0 + 512, 512); S2 = carve(W0 + 1024, 512); S3 = carve(W0 + 1536, 512); S4 = carve(W0 + 2048, 512)
                QKT = carve(W0 + 2560, 256, BF16); bv = carve(W0 + 2816, 128, BF16); bk = carve(W0 + 2944, 128, BF16); kd = carve(W0 + 3072, 128, BF16)
                u = carve(W0 + 3200, 256); wT = carve(W0 + 3456, 128, BF16).rearrange("p (d i) -> p d i", d=2)
                vnew = carve(W0 + 3712, 128, BF16); Sst = carve(W0 + 3840, 128).rearrange("p (d v) -> p d v", d=2)
                Sb = carve(W0 + 3968, 64, BF16).rearrange("p (d v) -> p d v", d=2)
                Rb = carve(W0 + 4032, 256, BF16)
                oacc = carve(W0 + 4288, 2304).rearrange("p (n c) -> p n c", n=NT)
                A("dve", lambda e: e.memset(oacc, 0.0), writes=["oacc"])
                A("dve", lambda e: e.memset(Sst, 0.0), writes=["S"])
                A("dve", lambda e: e.memset(Sb, 0.0), writes=["Sb"])
                sl = lambda qi: slice(qi * 128, (qi + 1) * 128)
                CSTEP = int(os.environ.get('CSTEP', '99'))
                for s in range(min(NT, CSTEP)):
                    tl_ = [order[0][s], order[1][s]]
                    QI = [(d, hh) for hh in range(2) for d in range(2)]
                    if CST <= 1.05:
                        continue
                    for qi, (d, hh) in enumerate(QI):
                        t = tl_[d]
                        kTs = KT[hh * 64:(hh + 1) * 64, t * 128:(t + 1) * 128]
                        MM(psl[0 + hh][:, d * 128:(d + 1) * 128], kTs, kTs, ["KT"], [("ps", 0 + hh)])
                        MM(psl[2 + hh][:, d * 128:(d + 1) * 128], QT[hh * 64:(hh + 1) * 64, t * 128:(t + 1) * 128], kTs, ["KT", "QT"], [("ps", 2 + hh)])
                    if CST <= 1.1:
                        continue
                    for qi, (d, hh) in enumerate(QI):
                        t = tl_[d]; h = hp * 2 + hh
                        TS("dve", S0[:, sl(qi)], cF_("msk4")[:, sl(qi)], g_all[:, d, t, h:h + 1], None, ALU.mult, None, ["kF", "g_all"], ["S0", "S0a", "S0b"])
                    for qi, (d, hh) in enumerate(QI):
                        MM(psl[4][:, sl(qi)], cF_("tri_f" if d == 0 else "tri_b"), S0[:, sl(qi)], ["S0", "kF"], [("ps", 4)])
                    ACTF(S1, psl[4][:, 0:512], AF.Exp, [("ps", 4)], ["S1", "S1a", "S1b"])
                    TT("dve", S2, S1, cF_("strict4"), ALU.mult, ["S1", "kF"], ["S2", "S2a", "S2b"])
                    TT("dve", S3, S1, cF_("incl4"), ALU.mult, ["S1", "kF"], ["S3", "S3a", "S3b"])
                    if CST <= 1.3:
                        continue
                    bfv = lambda off: carve(off, 256, BF16)
                    XB, QKB = bfv(W0 + 2048), bfv(W0 + 2304)
                    YB, TMB = bfv(W0 + 0), bfv(W0 + 256)
                    XcB, YcB = bfv(W0 + 512), bfv(W0 + 768)
                    XnB, YnB = bfv(W0 + 1024), bfv(W0 + 1280)
                    P1B, Q1B = bfv(W0 + 1536), bfv(W0 + 1792)
                    for qi, (d, hh) in enumerate(QI):
                        t = tl_[d]; h = hp * 2 + hh
                        STT(XB[:, sl(qi)], psl[0 + hh][:, d * 128:(d + 1) * 128], negb[:, d, t, h:h + 1], S2[:, sl(qi)], ALU.mult, ALU.mult,
                            [("ps", 0 + hh), "negb", "S2"], ["S4a"])
                    for hh in range(2):
                        TT("dve", QKB[:, hh * 256:(hh + 1) * 256], psl[2 + hh][:, 0:256], S3[:, hh * 256:(hh + 1) * 256], ALU.mult, [("ps", 2 + hh), "S3"], ["S4b"])
                    if CST <= 1.5:
                        continue
                    for qi in range(4):
                        MM(psl[5][:, sl(qi)], XB[:, sl(qi)], cB_("ident"), ["S4a", "kB"], [("ps", 5)])
                        MM(psl[6][:, sl(qi)], QKB[:, sl(qi)], cB_("ident"), ["S4b", "kB"], [("ps", 6)])
                    A("dve", lambda e: e.tensor_copy(YB, psl[5][:, 0:512]), reads=[("ps", 5), "S0"], writes=["S0a"])
                    ACTF(QKT, psl[6][:, 0:512], AF.Copy, [("ps", 6)], ["QKT"])
                    if CST <= 1.7:
                        continue
                    TT("dve", XcB, XB, cF_("bd32_4"), ALU.mult, ["S4a", "kF", "S1"], ["S1a"])
                    TT("dve", YcB, YB, cF_("bd32_4"), ALU.mult, ["S0a", "kF", "S1"], ["S1b"])
                    TT("dve", Rb, YcB, cF_("ident4"), ALU.add, ["S1b", "kF"], ["Rb"])
                    TT("dve", TMB, XcB, cF_("ident4"), ALU.add, ["S1a", "kF", "S0"], ["S0b"])
                    Xc, Yc, Xk, Yk = XcB, YcB, "S1a", "S1b"
                    Xn_, Yn_, Xnk, Ynk = XnB, YnB, "S2a", "S2b"
                    for kk_ in range(1, 5):
                        for qi in range(4):
                            MM(psl[5][:, sl(qi)], Yc[:, sl(qi)], Xc[:, sl(qi)], [Xk, Yk], [("ps", 5)])
                        for qi in range(4):
                            MM(psl[6][:, sl(qi)], Xc[:, sl(qi)], Yc[:, sl(qi)], [Xk, Yk], [("ps", 6)])
                        ACTF(Xn_, psl[5][:, 0:512], AF.Copy, [("ps", 5), "S2"], [Xnk])
                        A("dve", lambda e, Yn_=Yn_: e.tensor_copy(Yn_, psl[6][:, 0:512]), reads=[("ps", 6), "S2"], writes=[Ynk])
                        for qi in range(4):
                            MM(psl[7][:, sl(qi)], Xn_[:, sl(qi)], Rb[:, sl(qi)], [Xnk, "Rb"], [("ps", 7)])
                        for qi in range(4):
                            MM(psl[4][:, sl(qi)], Yn_[:, sl(qi)], TMB[:, sl(qi)], [Ynk, "S0b"], [("ps", 4)])
                        TT("dve", Rb, Rb, psl[7][:, 0:512], ALU.add, ["Rb", ("ps", 7)], ["Rb"])
                        TT("dve", TMB, TMB, psl[4][:, 0:512], ALU.add, ["S0b", ("ps", 4)], ["S0b"])
                        Xc, Xn_ = Xn_, Xc; Xk, Xnk = Xnk, Xk
                        Yc, Yn_ = Yn_, Yc; Yk, Ynk = Ynk, Yk
                    for lvl, mname in enumerate(("m1_4", "m2_4")):
                        need_tm = (lvl == 0)
                        TT("dve", XcB, XB, cF_(mname), ALU.mult, ["S4a", "kF"], ["S1a"])
                        for qi in range(4):
                            MM(psl[5][:, sl(qi)], XcB[:, sl(qi)], Rb[:, sl(qi)], ["S1a", "Rb"], [("ps", 5)])
                        ACTF(P1B, psl[5][:, 0:512], AF.Copy, [("ps", 5), "S3"], ["S3a"])
                        if need_tm:
                            TT("dve", YcB, YB, cF_(mname), ALU.mult, ["S0a", "kF"], ["S1b"])
                            for qi in range(4):
                                MM(psl[6][:, sl(qi)], YcB[:, sl(qi)], TMB[:, sl(qi)], ["S1b", "S0b"], [("ps", 6)])
                            A("dve", lambda e: e.tensor_copy(Q1B, psl[6][:, 0:512]), reads=[("ps", 6), "S3"], writes=["S3b"])
                        for qi in range(4):
                            MM(psl[7][:, sl(qi)], TMB[:, sl(qi)], P1B[:, sl(qi)], ["S0b", "S3a"], [("ps", 7)])
                        if need_tm:
                            for qi in range(4):
                                MM(psl[4][:, sl(qi)], Rb[:, sl(qi)], Q1B[:, sl(qi)], ["Rb", "S3b"], [("ps", 4)])
                        TT("dve", Rb, Rb, psl[7][:, 0:512], ALU.add, ["Rb", ("ps", 7)], ["Rb"])
                        if need_tm:
                            TT("dve", TMB, TMB, psl[4][:, 0:512], ALU.add, ["S0b", ("ps", 4)], ["S0b"])
                    if CST <= 2:
                        continue
                    for qi, (d, hh) in enumerate(QI):
                        t = tl_[d]; h = hp * 2 + hh
                        c64 = slice(hh * 64, (hh + 1) * 64); o64 = slice(qi * 64, (qi + 1) * 64)
                        TS("dve", bv[:, o64], VTOK[:, t, c64], beta[:, d, t, h:h + 1], None, ALU.mult, None, ["VTOK", "beta"], ["bv"])
                        ACTF(bk[:, o64], KTOK[:, t, c64], AF.Identity, ["KTOK", "beg"], ["bk"], scale=beg[:, d, t, h:h + 1])
                        ACTF(kd[:, o64], KTOK[:, t, c64], AF.Identity, ["KTOK", "kds"], ["kd"], scale=kds[:, d, t, h:h + 1])
                    for qi, (d, hh) in enumerate(QI):
                        o64 = slice(qi * 64, (qi + 1) * 64)
                        MM(psl[0][:, o64], Rb[:, sl(qi)], bv[:, o64], ["Rb", "bv"], [("ps", 0)])
                    A("dve", lambda e: e.tensor_copy(u, psl[0][:, 0:256]), reads=[("ps", 0)], writes=["u"])
                    for qi, (d, hh) in enumerate(QI):
                        o64 = slice(qi * 64, (qi + 1) * 64); r64 = slice(hh * 64, (hh + 1) * 64)
                        MM(psl[1 + hh][r64, d * 128:(d + 1) * 128], bk[:, o64], Rb[:, sl(qi)], ["Rb", "bk"], [("ps", 1 + hh)], tp=(0, hh * 64))
                    for hh in range(2):
                        r64 = slice(hh * 64, (hh + 1) * 64)
                        ACTF(wT[r64, :, :], psl[1 + hh][r64, 0:256].rearrange("p (d i) -> p d i", d=2), AF.Copy, [("ps", 1 + hh)], ["wT"])
                    if CST <= 3:
                        continue
                    for qi, (d, hh) in enumerate(QI):
                        r64 = slice(hh * 64, (hh + 1) * 64)
                        MM(psl[3 + hh][:, d * 64:(d + 1) * 64], wT[r64, d, :], Sb[r64, d, :], ["wT", "Sb"], [("ps", 3 + hh)])
                    for hh in range(2):
                        TT("dve", vnew[:, hh * 128:(hh + 1) * 128], u[:, hh * 128:(hh + 1) * 128], psl[3 + hh][:, 0:128], ALU.subtract,
                           ["u", ("ps", 3 + hh)], ["vnew"])
                    for qi, (d, hh) in enumerate(QI):
                        t = tl_[d]
                        o64 = slice(qi * 64, (qi + 1) * 64); r64 = slice(hh * 64, (hh + 1) * 64)
                        MM(psl[5 + hh][:, d * 64:(d + 1) * 64], QT[r64, t * 128:(t + 1) * 128], Sb[r64, d, :], ["QT", "Sb"], [("ps", 5 + hh)])
                    for qi, (d, hh) in enumerate(QI):
                        o64 = slice(qi * 64, (qi + 1) * 64)
                        MM(psl[7][:, o64], QKT[:, sl(qi)], vnew[:, o64], ["QKT", "vnew"], [("ps", 7)])
                    for qi, (d, hh) in enumerate(QI):
                        o64 = slice(qi * 64, (qi + 1) * 64); r64 = slice(hh * 64, (hh + 1) * 64)
                        MM(psl[1 + hh][r64, d * 64:(d + 1) * 64], kd[:, o64], vnew[:, o64], ["kd", "vnew"], [("ps", 1 + hh)], tp=(0, hh * 64))
                    for qi, (d, hh) in enumerate(QI):
                        t = tl_[d]; h = hp * 2 + hh
                        o64 = slice(qi * 64, (qi + 1) * 64); c64 = slice(hh * 64, (hh + 1) * 64)
                        STT(oacc[:, t, c64], psl[5 + hh][:, d * 64:(d + 1) * 64], eG[:, d, t, h:h + 1], oacc[:, t, c64], ALU.mult, ALU.add,
                            [("ps", 5 + hh), "eG", "oacc"], ["oacc"])
                        TT("dve", oacc[:, t, c64], oacc[:, t, c64], psl[7][:, o64], ALU.add, [("ps", 7), "oacc"], ["oacc"])
                    for d in range(2):
                        t = tl_[d]
                        for hh in range(2):
                            r64 = slice(hh * 64, (hh + 1) * 64)
                            STT(Sst[r64, d, :], Sst[r64, d, :], glv[r64, d, t, hp:hp + 1], psl[1 + hh][r64, d * 64:(d + 1) * 64], ALU.mult, ALU.add,
                                ["S", "glv", ("ps", 1 + hh)], ["S"])
                    ACTF(Sb, Sst, AF.Copy, ["S"], ["Sb"])
                P.barrier()
                if CSTEP < 99:
                    return
                G = carve(W0, 2304).rearrange("p (n c) -> p n c", n=NT)
                ssq = carve(W0 + 2304, 36); junk = carve(W0 + 2368, 64); y1 = carve(W0 + 2432, 64)
                ytk = carve(W0 + 2560, 1152, BF16).rearrange("p (n c) -> p n c", n=NT)
                yCT = carve(1512, 1152, BF16).rearrange("p (c n) -> p c n", c=1)
                load_win(l, OFF_G + hp * 128, 128, 1)
                for n_ in range(NT):
                    b = n_ % 2
                    ti = 0 if n_ < 2 else 1 + (n_ - 2) // 4
                    ps = psl[6 + b]
                    for k in range(8):
                        MM(ps[:, 0:128], hn[:, k, n_ * 128:(n_ + 1) * 128], w1b[1][:, k, 0:128], WK1 + ks("hn", k, ti), [("ps", 6 + b)], start=(k == 0), stop=(k == 7))
                    ACTF(G[:, n_, :], ps[:, 0:128], AF.Silu, [("ps", 6 + b)], ["G"])
                for n_ in range(NT):
                    for hh in range(2):
                        c64 = slice(hh * 64, (hh + 1) * 64)
                        ix = n_ * 2 + hh
                        A("act", lambda e, n_=n_, c64=c64, ix=ix: e.activation(out=junk, in_=oacc[:, n_, c64], func=AF.Square, accum_out=ssq[:, ix:ix + 1]),
                          reads=["oacc"], writes=["ssq", "junk"])
                ACTF(ssq, ssq, AF.Sqrt, ["ssq", "epsb"], ["ssq"], scale=1.0 / 64, bias=epsb[:, 0:1])
                A("dve", lambda e: e.reciprocal(ssq, ssq), reads=["ssq"], writes=["ssq"])
                for n_ in range(NT):
                    for hh in range(2):
                        c64 = slice(hh * 64, (hh + 1) * 64)
                        ix = n_ * 2 + hh
                        STT(y1, oacc[:, n_, c64], ssq[:, ix:ix + 1], pbc_(l, "c_onorm"), ALU.mult, ALU.mult, ["oacc", "ssq", "pb"], ["y1"])
                        TT("dve", ytk[:, n_, c64], y1, G[:, n_, c64], ALU.mult, ["y1", "G"], ["ytk"])
                for n_ in range(NT):
                    b = n_ % 2
                    pt = psl[4 + b]
                    TR(pt[:].bitcast(BF16)[:, 0:128], ytk[:, n_, :], cB_("ident"), ["ytk", "kB"], [("ps", 4 + b)])
                    A("dve", lambda e, pt=pt, n_=n_: e.tensor_copy(yCT[:, 0, n_ * 128:(n_ + 1) * 128], pt[:].bitcast(BF16)[:, 0:128]),
                      reads=[("ps", 4 + b)], writes=["yCT"])
                if not os.environ.get('CSKIPW'):
                    wout_part(l, 640 + hp * 128, 1, yCT, ["yCT"], [0, 1, 2, 3, 4] if need_ctx else [1, 2, 3, 4])
                P.barrier()
                if int(os.environ.get('CHP', '99')) <= hp + 1:
                    return

        stages = []
        ALLT = [0, 1, 2, 3, 4]
        LAT = [1, 2, 3, 4]
        for l in range(DEPTH):
            last = (l == DEPTH - 1)
            ffn(l, 0, f1w1, f1w2, ALLT)
            if upto in ("ffn1", "ffn1_%d" % l):
                break
            norm_mod(l, 3, ALLT)
            P.barrier()
            mixer_A(l, not last)
            if upto in ('mixA', 'mixA_%d' % l):
                break
            mixer_B(l, not last)
            if upto in ('mixAB', 'mixAB_%d' % l):
                break
            mixer_C(l, not last)
            if upto in ('mix', 'mix_%d' % l):
                break
            ffn(l, 6, f2w1, f2w2, LAT if last else ALLT)
            if upto in ('l0', 'ffn2_%d' % l):
                break

        P.barrier()
        if dbg:
            A("sp", lambda e: e.dma_start(out=dbg2, in_=arena[:]), writes=[("out", "d2")], dma="st_d2")
            for c in range(8):
                A("sp", lambda e, c=c: e.dma_start(out=dbgT[c * 128:(c + 1) * 128, :], in_=hT[:, c, :]),
                  reads=ks("hT", c, range(5)), writes=[("out", "d", c)], dma="st_d%d" % c)
        for c in range(8):
            A("sp", lambda e, c=c: e.dma_start(out=outT[c * 128:(c + 1) * 128, :], in_=hT[:, c, NCTX:T]),
              reads=ks("hT", c, range(5)), writes=[("out", c)], dma="st_o%d" % c)
        fin = Op()
        fin.eng = "sp"; fin.fn = None; fin.dma = None; fin.rk = set(); fin.wk = set(); fin.signal = False; fin.count = 0; fin.semkey = "sp"
        fin.deps = [op for op in P.ops if op.dma is not None and op.dma.startswith("st_")]
        for d in fin.deps:
            d.signal = True
        P.ops.append(fin)
        P.emit(stack)
    return nc, cF, cB


def make_inputs(inputs, b, cF, cB):
    f = lambda a: np.ascontiguousarray(np.asarray(a, dtype=np.float32))
    x = f(inputs["x"]); ctx = f(inputs["ctx"]); c = f(inputs["c"]); c_ctx = f(inputs["c_ctx"])
    m = {}
    m["xT"] = np.ascontiguousarray(np.concatenate([ctx[b].T, x[b].T], axis=1))
    cT = np.concatenate([c[b].reshape(8, 128).T, c_ctx.reshape(8, 128).T], axis=1)
    m["cT"] = np.ascontiguousarray(cT)
    m["w_mod"] = f(inputs["w_mod"])
    bm = f(inputs["b_mod"])
    m["b_modT"] = np.ascontiguousarray(bm.reshape(DEPTH, 72, 128).transpose(2, 0, 1).reshape(128, DEPTH * 72))
    for n in ["ffn1_w1", "ffn1_w2", "ffn2_w1", "ffn2_w2", "w_in", "w_out"]:
        m[n] = f(inputs[n])
    rows = []
    for l in range(DEPTH):
        rows.append(np.concatenate([f(inputs[n])[l].reshape(-1) for n in
                                    ["a_qnorm", "a_knorm", "a_lambda", "a_subln", "b_qnorm", "b_knorm", "b_sink", "c_A_log", "c_dt_bias", "c_onorm"]]))
    row = np.concatenate(rows)
    m["pbc"] = np.ascontiguousarray(np.broadcast_to(row[None, :], (128, row.size)))
    ppl = []
    p = np.arange(128)
    for l in range(DEPTH):
        cols = [f(inputs["a_qnorm"])[l][p % 32], f(inputs["a_knorm"])[l][p % 32], f(inputs["b_qnorm"])[l][p % 64], f(inputs["b_knorm"])[l][p % 64]]
        cv = f(inputs["c_conv"])[l]
        for cc in range(9):
            for k in range(5):
                cols.append(cv[k, cc * 128 + p])
        ppl.append(np.stack(cols, axis=1))
    m["ppar"] = np.ascontiguousarray(np.concatenate(ppl, axis=1))
    m["constF"] = cF
    m["constB"] = cB.astype(ml_dtypes.bfloat16)
    return m


_CACHE = {}


def kernel(**inputs):
    if "nc" not in _CACHE:
        _CACHE["nc"] = build()
    nc, cF, cB = _CACHE["nc"]
    in_maps = [make_inputs(inputs, b, cF, cB) for b in range(8)]
    res = run_bass_kernel_spmd(nc, in_maps, core_ids=list(range(8)))
    out = np.stack([np.ascontiguousarray(res.results[b]["outT"].T) for b in range(8)], axis=0)
    return out.astype(np.float32)
```

```python
import contextlib
import os
import numpy as np
import ml_dtypes
import concourse.bass as bass
import concourse.mybir as mybir
from concourse.bass_utils import run_bass_kernel_spmd

F32 = mybir.dt.float32
BF16 = mybir.dt.bfloat16
AF = mybir.ActivationFunctionType
ALU = mybir.AluOpType
AX = mybir.AxisListType

D = 1024; T = 2304; NCTX = 256; NLAT = 2048; NT = 18; DFF = 2816; NF = 22
DEPTH = 2
INC = 2968
OFF_QA, OFF_KA, OFF_VA, OFF_QB, OFF_KB, OFF_VB, OFF_C, OFF_G, OFF_A, OFF_B = 0, 256, 512, 768, 1152, 1280, 1408, 2560, 2944, 2956
BT = [(0, 256), (256, 512), (768, 512), (1280, 512), (1792, 512)]
EPS = 1e-6
SAME_ENG_SYNC = True


class Op:
    __slots__ = ("eng", "fn", "rk", "wk", "dma", "deps", "signal", "count", "semkey")


class Prog:
    def __init__(self, nc):
        self.nc = nc
        self.ops = []
        self.lastw = {}
        self.readers = {}
        self.last_eng = {}

    def add(self, eng, fn, reads=(), writes=(), dma=None):
        op = Op()
        op.eng = eng; op.fn = fn; op.dma = dma
        op.rk = set(reads); op.wk = set(writes)
        if dma is not None:
            op.rk.add(("slot", dma)); op.wk.add(("slot", dma))
        op.signal = dma is not None; op.count = 0
        op.semkey = ("dma", dma) if dma is not None else eng
        deps = []
        seen = set()

        def consider(d, raw):
            if d is None or id(d) in seen:
                return
            if d.dma is None and dma is None and d.eng == eng:
                if eng == "pe" or not raw or not SAME_ENG_SYNC:
                    return
            seen.add(id(d)); deps.append(d); d.signal = True

        for k in op.rk:
            consider(self.lastw.get(k), True)
        for k in op.wk:
            consider(self.lastw.get(k), False)
            for r in self.readers.get(k, ()):
                consider(r, False)
        op.deps = deps
        for k in op.wk:
            self.lastw[k] = op
            self.readers[k] = []
        for k in op.rk:
            if k not in op.wk:
                self.readers.setdefault(k, []).append(op)
        self.ops.append(op)
        if dma is None:
            self.last_eng[eng] = op
        return op

    def barrier(self, engs=("pe", "act", "dve")):
        lasts = [self.last_eng[e] for e in engs if e in self.last_eng]
        for e in tuple(engs) + ("sp",):
            op = Op()
            op.eng = e; op.fn = None; op.dma = None; op.rk = set(); op.wk = set()
            op.signal = False; op.count = 0; op.semkey = e
            op.deps = [d for d in lasts if d.eng != e]
            for d in op.deps:
                d.signal = True
            self.ops.append(op)

    def emit(self, stack):
        nc = self.nc
        semkeys = []
        for op in self.ops:
            if op.signal and op.semkey not in semkeys:
                semkeys.append(op.semkey)
        sems = {}
        for i, k in enumerate(semkeys):
            sems[k] = stack.enter_context(nc.semaphore("s%d" % i))
        cnt = {}
        for op in self.ops:
            if op.signal:
                inc = 16 if op.dma is not None else 1
                cnt[op.semkey] = cnt.get(op.semkey, 0) + inc
                op.count = cnt[op.semkey]
        per = {"pe": [], "act": [], "dve": [], "pool": [], "sp": []}
        for op in self.ops:
            per[op.eng].append(op)
        block = stack.enter_context(nc.Block())

        def run(e, lst):
            waited = {}
            for op in lst:
                need = {}
                for d in op.deps:
                    if d.count > need.get(d.semkey, 0):
                        need[d.semkey] = d.count
                for k, v in need.items():
                    if waited.get(k, 0) < v:
                        e.wait_ge(sems[k], v)
                        waited[k] = v
                if op.fn is None:
                    continue
                try:
                    inst = op.fn(e)
                except BaseException:
                    print('FAILED OP', op.eng, sorted(map(str, op.rk)), sorted(map(str, op.wk)))
                    raise
                if op.signal:
                    inst.then_inc(sems[op.semkey], 16 if op.dma is not None else 1)

        @block.tensor
        def _(e):
            run(e, per["pe"])

        @block.scalar
        def _(e):
            run(e, per["act"])

        @block.vector
        def _(e):
            run(e, per["dve"])

        @block.gpsimd
        def _(e):
            run(e, per["pool"])

        @block.sync
        def _(e):
            run(e, per["sp"])


def ks(name, *ranges):
    out = [(name,)]
    for r in ranges:
        if isinstance(r, int):
            r = [r]
        out = [o + (i,) for o in out for i in r]
    return out


def _rope_tables(dim):
    half = dim // 2
    nf = half // 2
    inv = 10000.0 ** (-np.arange(0, half, 2, dtype=np.float32) / half)
    pos_row = np.repeat(np.arange(32, dtype=np.float32), 64)
    pos_col = np.tile(np.arange(64, dtype=np.float32), 32)
    cos = np.zeros((128, NLAT), np.float32); sin = np.zeros((128, NLAT), np.float32)
    perm = np.zeros((128, 128), np.float32)
    for p in range(128):
        e = p % dim
        hid = e // half
        w = e % half
        f = w % nf
        second = w // nf
        pos = pos_row if hid == 0 else pos_col
        ang = (pos * inv[f]).astype(np.float32)
        cos[p] = np.cos(ang)
        sin[p] = np.sin(ang) * (1.0 if second else -1.0)
        partner = p - nf if second else p + nf
        perm[partner, p] = 1.0
    return cos, sin, perm


def _consts():
    c = {}
    i = np.arange(128)
    I = (i[:, None] == i[None, :]).astype(np.float32)
    c["ident"] = I
    bd32 = ((i[:, None] // 32) == (i[None, :] // 32)).astype(np.float32)
    bd64 = ((i[:, None] // 64) == (i[None, :] // 64)).astype(np.float32)
    c["bd32"] = bd32; c["bd64"] = bd64; c["ones"] = np.ones((128, 128), np.float32)
    cosA, sinA, permA = _rope_tables(32)
    cosB, sinB, permB = _rope_tables(64)
    c["permA"] = permA; c["permB"] = permB
    c["cosA"] = cosA; c["sinA"] = sinA; c["cosB"] = cosB; c["sinB"] = sinB
    c["mprev"] = (i[:, None] >= i[None, :]).astype(np.float32)
    c["mnext"] = (i[:, None] <= i[None, :]).astype(np.float32)
    le = (i[:, None] <= i[None, :]).astype(np.float32)
    ge = (i[:, None] >= i[None, :]).astype(np.float32)
    lt = (i[:, None] < i[None, :]).astype(np.float32)
    gt = (i[:, None] > i[None, :]).astype(np.float32)
    c["tri_f"] = le
    c["tri_b"] = ge
    c["msk4"] = np.concatenate([gt, lt, gt, lt], axis=1)
    c["strict4"] = np.concatenate([gt, lt, gt, lt], axis=1)
    c["incl4"] = np.concatenate([ge, le, ge, le], axis=1)
    c["bd32_4"] = np.tile(bd32, (1, 4))
    c["m1_4"] = np.tile(bd64 - bd32, (1, 4))
    c["m2_4"] = np.tile(1.0 - bd64, (1, 4))
    c["ident4"] = np.tile(I, (1, 4))
    return c


CONST_F32 = ["ident", "permA", "permB", "tri_f", "tri_b", "msk4", "strict4", "incl4", "bd32_4", "m1_4", "m2_4", "ident4",
             "cosA", "sinA", "cosB", "sinB"]
CONST_BF = ["ident", "bd32", "bd64", "ones", "mprev", "mnext"]


def _pack(names, cdict):
    offs = {}
    cols = []
    o = 0
    for n in names:
        a = cdict[n]
        offs[n] = (o, a.shape[1])
        o += a.shape[1]
        cols.append(a)
    return np.concatenate(cols, axis=1), offs


PB = {}
_o = 0
for _n, _s in [("a_qnorm", 32), ("a_knorm", 32), ("a_lambda", 128), ("a_subln", 64), ("b_qnorm", 64), ("b_knorm", 64),
               ("b_sink", 6), ("c_A_log", 12), ("c_dt_bias", 12), ("c_onorm", 64)]:
    PB[_n] = (_o, _s)
    _o += _s
PBN = _o
PPN = 4 + 45


def build(upto="all", dbg=False):
    nc = bass.Bass("TRN2", target_bir_lowering=False)
    cd = _consts()
    cF, offF = _pack(CONST_F32, cd)
    cB, offB = _pack(CONST_BF, cd)
    NCF = cF.shape[1]; NCB = cB.shape[1]
    NCF_RES = offF["cosA"][0]

    dt = nc.dram_tensor
    xT = dt("xT", [D, T], F32, kind="ExternalInput").ap()
    cT = dt("cT", [128, 16], F32, kind="ExternalInput").ap()
    w_mod = dt("w_mod", [DEPTH, D, 9 * D], F32, kind="ExternalInput").ap()
    b_modT = dt("b_modT", [128, DEPTH * 72], F32, kind="ExternalInput").ap()
    f1w1 = dt("ffn1_w1", [DEPTH, D, 2 * DFF], F32, kind="ExternalInput").ap()
    f1w2 = dt("ffn1_w2", [DEPTH, DFF, D], F32, kind="ExternalInput").ap()
    f2w1 = dt("ffn2_w1", [DEPTH, D, 2 * DFF], F32, kind="ExternalInput").ap()
    f2w2 = dt("ffn2_w2", [DEPTH, DFF, D], F32, kind="ExternalInput").ap()
    w_in = dt("w_in", [DEPTH, D, INC], F32, kind="ExternalInput").ap()
    w_out = dt("w_out", [DEPTH, D, D], F32, kind="ExternalInput").ap()
    pbc = dt("pbc", [128, DEPTH * PBN], F32, kind="ExternalInput").ap()
    ppar = dt("ppar", [128, DEPTH * PPN], F32, kind="ExternalInput").ap()
    constF = dt("constF", [128, NCF], F32, kind="ExternalInput").ap()
    constB = dt("constB", [128, NCB], BF16, kind="ExternalInput").ap()
    outT = dt("outT", [D, NLAT], F32, kind="ExternalOutput").ap()
    if dbg:
        dbgT = dt("dbgT", [D, T], F32, kind="ExternalOutput").ap()
        dbg2 = dt("dbg2", [128, 12800], F32, kind="ExternalOutput").ap()

    stack = contextlib.ExitStack()
    with stack:
        sb = lambda n, s, d: stack.enter_context(nc.sbuf_tensor(n, s, d))
        hT = sb("hT", [128, 8, T], F32)
        hn = sb("hn", [128, 8, T], BF16)
        modT = sb("modT", [128, DEPTH, 2, 72], F32)
        bmod = sb("bmod", [128, DEPTH * 72], F32)
        cTs = sb("cTs", [128, 16], F32)
        pb = sb("pb", [128, DEPTH * PBN], F32)
        pp = sb("pp", [128, DEPTH * PPN], F32)
        kF = sb("kF", [128, NCF_RES], F32)
        kB = sb("kB", [128, NCB], BF16)
        w1b = [sb("w1b%d" % i, [128, 8, 512], BF16) for i in range(2)]
        w2b = [sb("w2b%d" % i, [128, 2, 1024], BF16) for i in range(2)]
        ARENA_W = 12800
        arena = sb("arena", [128, ARENA_W], F32)
        psl = [stack.enter_context(nc.psum_tensor("ps%d" % i, [128, 512], F32)) for i in range(8)]

        def cF_(n):
            o, w = offF[n]
            return kF[:, o:o + w]

        def cB_(n):
            o, w = offB[n]
            return kB[:, o:o + w]

        def carve(off_words, nwords, dtype=F32):
            a = arena[:, off_words:off_words + nwords]
            return a.bitcast(dtype) if dtype != F32 else a

        P = Prog(nc)
        A = P.add

        A("sp", lambda e: e.dma_start(out=cTs[:], in_=cT), writes=ks("cTs"), dma="ld_c")
        A("sp", lambda e: e.dma_start(out=bmod[:], in_=b_modT), writes=ks("bmod"), dma="ld_b")
        A("sp", lambda e: e.dma_start(out=pb[:], in_=pbc), writes=ks("pb"), dma="ld_pb")
        A("sp", lambda e: e.dma_start(out=pp[:], in_=ppar), writes=ks("pp"), dma="ld_pp")
        A("sp", lambda e: e.dma_start(out=kF[:], in_=constF[:, 0:NCF_RES]), writes=ks("kF"), dma="ld_kF")
        A("sp", lambda e: e.dma_start(out=kB[:], in_=constB), writes=ks("kB"), dma="ld_kB")
        for c in range(8):
            A("sp", lambda e, c=c: e.dma_start(out=hT[:, c, :], in_=xT[c * 128:(c + 1) * 128, :]),
              writes=ks("hT", c, range(5)), dma="ld_x%d" % c)

        sc = sb("silu_c", [128, 16], F32)
        A("act", lambda e: e.activation(out=sc[:], in_=cTs[:], func=AF.Silu), reads=ks("cTs"), writes=ks("sc"))
        wm = [carve(i * 4096, 4096).rearrange("p (k n) -> p k n", k=8) for i in range(2)]
        gi = 0
        for l in range(DEPTH):
            for g in range(18):
                buf = wm[gi % 2]; bk = ("wm", gi % 2)
                A("sp", lambda e, buf=buf, l=l, g=g: e.dma_start(
                    out=buf, in_=w_mod[l, :, g * 512:(g + 1) * 512].rearrange("(k p) n -> p k n", p=128)),
                  writes=[bk], dma="wm%d" % (gi % 2))
                ps = psl[gi % 2]
                for n4 in range(4):
                    for k in range(8):
                        A("pe", lambda e, ps=ps, buf=buf, n4=n4, k=k: e.matmul(
                            ps[:, n4 * 2:n4 * 2 + 2], buf[:, k, n4 * 128:(n4 + 1) * 128],
                            sc[:].rearrange("p (w k) -> p k w", w=2)[:, k, :], start=(k == 0), stop=(k == 7)),
                          reads=[bk] + ks("sc"), writes=[("ps", gi % 2)])
                for w in range(2):
                    A("dve", lambda e, ps=ps, l=l, g=g, w=w: e.tensor_tensor(
                        modT[:, l, w, g * 4:(g + 1) * 4], ps[:, 0:8].rearrange("p (n w) -> p n w", w=2)[:, :, w],
                        bmod[:, l * 72 + g * 4: l * 72 + (g + 1) * 4], ALU.add),
                      reads=[("ps", gi % 2)] + ks("bmod"), writes=ks("modT", l))
                gi += 1
            for j in (1, 4, 7):
                A("dve", lambda e, l=l, j=j: e.tensor_scalar(modT[:, l, :, j * 8:(j + 1) * 8], modT[:, l, :, j * 8:(j + 1) * 8],
                                                            1.0, None, ALU.add), reads=ks("modT", l), writes=ks("modT", l))
            for j in (2, 8):
                A("dve", lambda e, l=l, j=j: e.tensor_scalar(modT[:, l, :, j * 8:(j + 1) * 8], modT[:, l, :, j * 8:(j + 1) * 8],
                                                            0.5, None, ALU.mult), reads=ks("modT", l), writes=ks("modT", l))
        P.barrier()

        def mod_sc(l, w, j, c):
            return modT[:, l, w, j * 8 + c: j * 8 + c + 1]

        def norm_mod(l, j, tiles):
            for ti in tiles:
                t0, tl = BT[ti]
                w = 1 if ti == 0 else 0
                sq = carve((ti % 2) * 2048, 2048, BF16).rearrange("p (k n) -> p k n", k=8)
                rs = carve(4096 + (ti % 2) * 512, 512)
                ps = psl[ti % 2]
                for k in range(8):
                    A("act", lambda e, sq=sq, k=k, t0=t0, tl=tl: e.activation(out=sq[:, k, 0:tl], in_=hT[:, k, t0:t0 + tl], func=AF.Square),
                      reads=ks("hT", k, ti), writes=[("sq", ti % 2, k)])
                for k in range(8):
                    A("pe", lambda e, ps=ps, sq=sq, k=k, tl=tl: e.matmul(ps[:, 0:tl], cB_("ones"), sq[:, k, 0:tl], start=(k == 0), stop=(k == 7)),
                      reads=[("sq", ti % 2, k)] + ks("kB"), writes=[("ps", ti % 2)])
                A("act", lambda e, ps=ps, rs=rs, tl=tl: e.activation(out=rs[:, 0:tl], in_=ps[:, 0:tl], func=AF.Sqrt, scale=1.0 / D, bias=epsb[:, 0:1]),
                  reads=[("ps", ti % 2)] + ks("epsb"), writes=[("rs", ti % 2)])
                A("dve", lambda e, rs=rs, tl=tl: e.reciprocal(rs[:, 0:tl], rs[:, 0:tl]), reads=[("rs", ti % 2)], writes=[("rs", ti % 2)])
                for k in range(8):
                    tmp = carve(5120 + (k % 2) * 512, 512)
                    A("dve", lambda e, tmp=tmp, k=k, rs=rs, t0=t0, tl=tl: e.tensor_tensor(tmp[:, 0:tl], hT[:, k, t0:t0 + tl], rs[:, 0:tl], ALU.mult),
                      reads=ks("hT", k, ti) + [("rs", ti % 2)], writes=[("ntmp", k % 2)])
                    A("act", lambda e, tmp=tmp, k=k, t0=t0, tl=tl, w=w: e.activation(
                        out=hn[:, k, t0:t0 + tl], in_=tmp[:, 0:tl], func=AF.Identity, scale=mod_sc(l, w, j + 1, k), bias=mod_sc(l, w, j, k)),
                      reads=[("ntmp", k % 2)] + ks("modT", l), writes=ks("hn", k, ti))

        epsb = sb("epsb", [128, 1], F32)
        oneb = sb("oneb", [128, 1], F32)
        A("dve", lambda e: e.memset(oneb[:], 1.0), writes=["oneb"])
        A("dve", lambda e: e.memset(epsb[:], EPS), writes=ks("epsb"))

        def ffn(l, j, w1d, w2d, tiles):
            norm_mod(l, j, tiles)
            P.barrier()
            for part in range(11):
                wb = w1b[part % 2]; w2 = w2b[part % 2]
                wbv = wb[:].rearrange("p k (g n) -> p k g n", g=2)
                for g in range(2):
                    A("pool", lambda e, wbv=wbv, g=g, part=part: e.dma_start(
                        out=wbv[:, :, g, :], in_=w1d[l, :, g * DFF + part * 256: g * DFF + (part + 1) * 256].rearrange("(k p) n -> p k n", p=128)),
                      writes=[("w1b", part % 2, g)], dma="w1b%d%d" % (part % 2, g))
                A("pool", lambda e, w2=w2, part=part: e.dma_start(
                    out=w2[:], in_=w2d[l, part * 256:(part + 1) * 256, :].rearrange("(c p) n -> p c n", p=128)),
                  writes=[("w2b", part % 2)], dma="w2b%d" % (part % 2))
                act = carve((part % 2) * 2304, 2304, BF16).rearrange("p (c n) -> p c n", c=2)
                for fc in range(2):
                    for ti in tiles:
                        t0, tl = BT[ti]
                        pg = psl[(2 * (fc * 5 + ti)) % 4]; pu = psl[(2 * (fc * 5 + ti)) % 4 + 1]
                        kg = ("ps", (2 * (fc * 5 + ti)) % 4); ku = ("ps", (2 * (fc * 5 + ti)) % 4 + 1)
                        for k in range(8):
                            A("pe", lambda e, pg=pg, wbv=wbv, k=k, fc=fc, t0=t0, tl=tl: e.matmul(
                                pg[:, 0:tl], wbv[:, k, 0, fc * 128:(fc + 1) * 128], hn[:, k, t0:t0 + tl], start=(k == 0), stop=(k == 7)),
                              reads=[("w1b", part % 2, 0)] + ks("hn", k, ti), writes=[kg])
                        for k in range(8):
                            A("pe", lambda e, pu=pu, wbv=wbv, k=k, fc=fc, t0=t0, tl=tl: e.matmul(
                                pu[:, 0:tl], wbv[:, k, 1, fc * 128:(fc + 1) * 128], hn[:, k, t0:t0 + tl], start=(k == 0), stop=(k == 7)),
                              reads=[("w1b", part % 2, 1)] + ks("hn", k, ti), writes=[ku])
                        st = carve(4608 + ((fc * 5 + ti) % 2) * 512, 512)
                        skey = ("silut", (fc * 5 + ti) % 2)
                        A("act", lambda e, st=st, pg=pg, tl=tl: e.activation(out=st[:, 0:tl], in_=pg[:, 0:tl], func=AF.Silu),
                          reads=[kg], writes=[skey])
                        A("dve", lambda e, st=st, pu=pu, act=act, fc=fc, t0=t0, tl=tl: e.tensor_tensor(
                            act[:, fc, t0:t0 + tl], pu[:, 0:tl], st[:, 0:tl], ALU.mult),
                          reads=[ku, skey], writes=[("act", part % 2, fc, ti)])
                for ti in tiles:
                    t0, tl = BT[ti]
                    w = 1 if ti == 0 else 0
                    for dc in range(8):
                        po = psl[4 + (ti * 8 + dc) % 4]; ko = ("ps", 4 + (ti * 8 + dc) % 4)
                        for fc in range(2):
                            A("pe", lambda e, po=po, w2=w2, fc=fc, dc=dc, act=act, t0=t0, tl=tl: e.matmul(
                                po[:, 0:tl], w2[:, fc, dc * 128:(dc + 1) * 128], act[:, fc, t0:t0 + tl], start=(fc == 0), stop=(fc == 1)),
                              reads=[("w2b", part % 2), ("act", part % 2, fc, ti)], writes=[ko])
                        A("dve", lambda e, po=po, dc=dc, t0=t0, tl=tl, w=w: e.scalar_tensor_tensor(
                            hT[:, dc, t0:t0 + tl], po[:, 0:tl], mod_sc(l, w, j + 2, dc), hT[:, dc, t0:t0 + tl], ALU.mult, ALU.add),
                          reads=[ko] + ks("modT", l) + ks("hT", dc, ti), writes=ks("hT", dc, ti))
            P.barrier()

        def ACTF(out, in_, func, r, w, **kw):
            A("act", lambda e: e.activation(out=out, in_=in_, func=func, **kw), reads=r, writes=w)

        def TT(eng, out, a, b, op, r, w):
            A(eng, lambda e: e.tensor_tensor(out, a, b, op), reads=r, writes=w)

        def TS(eng, out, a, s1, s2, op0, op1, r, w):
            if op1 is None:
                A(eng, lambda e: e.tensor_scalar(out, a, s1, None, op0), reads=r, writes=w)
            else:
                A(eng, lambda e: e.tensor_scalar(out, a, s1, s2, op0, op1), reads=r, writes=w)

        def STT(out, a, s, b, op0, op1, r, w):
            A("dve", lambda e: e.scalar_tensor_tensor(out, a, s, b, op0, op1), reads=r, writes=w)

        def MM(out, lhsT, rhs, r, w, start=True, stop=True, tp=None):
            if tp is None:
                A("pe", lambda e: e.matmul(out, lhsT, rhs, start=start, stop=stop), reads=r, writes=w)
            else:
                A("pe", lambda e: e.matmul(out, lhsT, rhs, start=start, stop=stop, tile_position=tp), reads=r, writes=w)

        def TR(out, in_, ident, r, w):
            A("pe", lambda e: e.transpose(out, in_, ident), reads=r, writes=w)

        def load_win(l, c0, n, slot):
            A("pool", lambda e: e.dma_start(out=w1b[slot][:, :, 0:n], in_=w_in[l, :, c0:c0 + n].rearrange("(k p) n -> p k n", p=128)),
              writes=[("w1b", slot, 0), ("w1b", slot, 1)], dma="w1b%d0" % slot)

        def pbc_(l, name):
            o, n = PB[name]
            return pb[:, l * PBN + o: l * PBN + o + n]

        TMP0 = 6948

        def normrope(l, c0, nchunks, dim, lhs_fn, ppcols, dsts, tiles, bdname, permname, cosname, sinname, loader=None):
            if loader is None:
                load_win(l, c0, 512, 0)
            else:
                loader()
            WK = [("w1b", 0, 0), ("w1b", 0, 1)]
            tabc = carve(TMP0 + 0, 512); tabs = carve(TMP0 + 512, 512)
            t1 = carve(TMP0 + 1024, 512); t2 = carve(TMP0 + 1536, 512)
            for ti in tiles:
                t0, tl = BT[ti]
                if ti > 0:
                    oc = offF[cosname][0] + (ti - 1) * 512; os_ = offF[sinname][0] + (ti - 1) * 512
                    A("sp", lambda e, oc=oc: e.dma_start(out=tabc, in_=constF[:, oc:oc + 512]), writes=["tabc"], dma="tabc")
                    A("sp", lambda e, os_=os_: e.dma_start(out=tabs, in_=constF[:, os_:os_ + 512]), writes=["tabs"], dma="tabs")
                for cq in range(nchunks):
                    b = cq % 2
                    raw = carve(TMP0 + 2048 + b * 512, 512); sq = carve(TMP0 + 3072 + b * 256, 256, BF16)
                    rinv = carve(TMP0 + 3584 + b * 512, 512); xn = carve(TMP0 + 4608 + b * 512, 512)
                    ps = psl[b]; ps2 = psl[2 + b]; ps3 = psl[4 + b]
                    for k in range(8):
                        MM(ps[:, 0:tl], lhs_fn(cq, k), hn[:, k, t0:t0 + tl], WK + ks("hn", k, ti), [("ps", b)], start=(k == 0), stop=(k == 7))
                    ACTF(raw[:, 0:tl], ps[:, 0:tl], AF.Copy, [("ps", b)], [("raw", b)])
                    ACTF(sq[:, 0:tl], ps[:, 0:tl], AF.Square, [("ps", b)], [("sq", b)])
                    MM(ps2[:, 0:tl], cB_(bdname), sq[:, 0:tl], [("sq", b), "kB"], [("ps", 2 + b)])
                    ACTF(rinv[:, 0:tl], ps2[:, 0:tl], AF.Sqrt, [("ps", 2 + b), "epsb"], [("rinv", b)], scale=1.0 / dim, bias=epsb[:, 0:1])
                    A("dve", lambda e, rinv=rinv, tl=tl: e.reciprocal(rinv[:, 0:tl], rinv[:, 0:tl]), reads=[("rinv", b)], writes=[("rinv", b)])
                    pc = ppcols[cq]
                    STT(xn[:, 0:tl], raw[:, 0:tl], pp[:, l * PPN + pc: l * PPN + pc + 1], rinv[:, 0:tl], ALU.mult, ALU.mult,
                        [("raw", b), ("rinv", b), "pp"], [("xn", b)])
                    dst, dkey = dsts[cq](t0, tl, ti)
                    vw = (lambda a: a.rearrange("p (a b) -> p a b", b=128)) if len(dst.shape) == 3 else (lambda a: a)
                    if ti == 0:
                        ACTF(dst, vw(xn[:, 0:tl]), AF.Copy, [("xn", b)], [dkey])
                    else:
                        MM(ps3[:, 0:tl], cF_(permname), xn[:, 0:tl], [("xn", b), "kF"], [("ps", 4 + b)])
                        TT("dve", t1[:, 0:tl], ps3[:, 0:tl], tabs[:, 0:tl], ALU.mult, [("ps", 4 + b), "tabs"], ["t1"])
                        TT("dve", t2[:, 0:tl], xn[:, 0:tl], tabc[:, 0:tl], ALU.mult, [("xn", b), "tabc"], ["t2"])
                        TT("dve", dst, vw(t1[:, 0:tl]), vw(t2[:, 0:tl]), ALU.add, ["t1", "t2"], [dkey])

        def vproj(l, c0, n, nh, vdst, vkey):
            load_win(l, c0, n, 1)
            WK = [("w1b", 1, 0), ("w1b", 1, 1)]
            A("dve", lambda e: e.memset(vdst[:, :, :, 64:65], 1.0), writes=[vkey])
            for n_ in range(NT):
                b = n_ % 2
                ti = 0 if n_ < 2 else 1 + (n_ - 2) // 4
                ps = psl[6 + b]
                for k in range(8):
                    MM(ps[:, 0:n], hn[:, k, n_ * 128:(n_ + 1) * 128], w1b[1][:, k, 0:n], WK + ks("hn", k, ti), [("ps", 6 + b)], start=(k == 0), stop=(k == 7))
                if b == 0:
                    ACTF(vdst[:, n_, :, 0:64], ps[:, 0:n].rearrange("p (h d) -> p h d", d=64), AF.Copy, [("ps", 6 + b)], [vkey])
                else:
                    A("dve", lambda e, ps=ps, n_=n_: e.tensor_copy(vdst[:, n_, :, 0:64], ps[:, 0:n].rearrange("p (h d) -> p h d", d=64)),
                      reads=[("ps", 6 + b)], writes=[vkey])

        def wout_part(l, row0, nch, yT, ykeys, tiles):
            nld = min(nch, 2)
            A("pool", lambda e: e.dma_start(out=w2b[0][:, 0:nld, :], in_=w_out[l, row0:row0 + nld * 128, :].rearrange("(c p) n -> p c n", p=128)),
              writes=[("w2b", 0)], dma="w2b0")
            if nch > 2:
                A("pool", lambda e: e.dma_start(out=w2b[1][:, 0:1, :], in_=w_out[l, row0 + 256:row0 + 384, :].rearrange("(c p) n -> p c n", p=128)),
                  writes=[("w2b", 1)], dma="w2b1")
            for ti in tiles:
                t0, tl = BT[ti]
                w = 1 if ti == 0 else 0
                for dc in range(8):
                    po = psl[4 + dc % 4]; ko = ("ps", 4 + dc % 4)
                    for c in range(nch):
                        wsrc = w2b[0][:, c, dc * 128:(dc + 1) * 128] if c < 2 else w2b[1][:, 0, dc * 128:(dc + 1) * 128]
                        MM(po[:, 0:tl], wsrc, yT[:, c, t0:t0 + tl], [("w2b", 0), ("w2b", 1)] + ykeys, [ko], start=(c == 0), stop=(c == nch - 1))
                    STT(hT[:, dc, t0:t0 + tl], po[:, 0:tl], mod_sc(l, w, 5, dc), hT[:, dc, t0:t0 + tl], ALU.mult, ALU.add,
                        [ko] + ks("modT", l) + ks("hT", dc, ti), ks("hT", dc, ti))

        def mixer_A(l, need_ctx):
            lam_init = 0.8 - 0.6 * float(np.exp(-0.3 * l))
            qTa = carve(0, 2304, BF16).rearrange("p (c n) -> p c n", c=2)
            kTa = carve(2304, 2304, BF16).rearrange("p (c n) -> p c n", c=2)
            va = carve(4608, 2340, BF16).rearrange("p (n h d) -> p n h d", n=NT, h=4)
            dsts = [lambda t0, tl, ti, c=c: (qTa[:, c, t0:t0 + tl], ("qTa", c, ti)) for c in range(2)] + \
                   [lambda t0, tl, ti, c=c: (kTa[:, c, t0:t0 + tl], ("kTa", c, ti)) for c in range(2)]
            normrope(l, OFF_QA, 4, 32, lambda cq, k: w1b[0][:, k, cq * 128:(cq + 1) * 128], [0, 0, 1, 1], dsts, [0, 1, 2, 3, 4],
                     "bd32", "permA", "cosA", "sinA")
            vproj(l, OFF_VA, 256, 4, va, "va")
            P.barrier()
            sm = carve(TMP0, 512)
            lamt = sm[:, 0:64]; lam2 = sm[:, 64:66]; neglam = sm[:, 66:67]; subw = sm[:, 128:192]
            al = pbc_(l, "a_lambda")
            TT("dve", lamt.rearrange("p (a b) -> p a b", a=2), al.rearrange("p (a t b) -> p a t b", a=2, t=2)[:, :, 0, :],
               al.rearrange("p (a t b) -> p a t b", a=2, t=2)[:, :, 1, :], ALU.mult, ["pb"], ["lamt"])
            A("dve", lambda e: e.reduce_sum(lam2, lamt.rearrange("p (a b) -> p a b", a=2), AX.X), reads=["lamt"], writes=["lam2"])
            ACTF(lam2, lam2, AF.Exp, ["lam2"], ["lam2"])
            TT("dve", neglam, lam2[:, 1:2], lam2[:, 0:1], ALU.subtract, ["lam2"], ["neglam"])
            TS("dve", neglam, neglam, -lam_init, None, ALU.add, None, ["neglam"], ["neglam"])
            TS("dve", subw, pbc_(l, "a_subln"), 1.0 - lam_init, None, ALU.mult, None, ["pb"], ["subw"])
            eT = carve(6948 + 512, 2304, BF16).rearrange("p (m n) -> p m n", m=2)
            ytok = [carve(6948 + 512 + 2304 + i * 128, 128, BF16) for i in range(2)]
            fin = carve(6948 + 512 + 2304 + 256, 256)
            yAT = carve(10276, 2304, BF16).rearrange("p (c n) -> p c n", c=2)
            qtiles = list(range(2, NT)) + ([0, 1] if need_ctx else [])
            for qi, qt in enumerate(qtiles):
                kts = list(range(NT)) if qt >= 2 else [0, 1]
                tiq = 0 if qt < 2 else 1 + (qt - 2) // 4
                yt = ytok[qi % 2]; ykey = ("ytok", qi % 2)
                for h in range(4):
                    c = h // 2
                    for m in range(2):
                        base = (h % 2) * 64 + 32 * m
                        tp = (96, 0) if base == 96 else None
                        for kg in range(0, len(kts), 4):
                            grp = kts[kg:kg + 4]
                            pi = (m * 5 + kg // 4) % 4
                            ps = psl[pi]
                            for j, kt in enumerate(grp):
                                tik = 0 if kt < 2 else 1 + (kt - 2) // 4
                                MM(ps[:, j * 128:(j + 1) * 128], kTa[base:base + 32, c, kt * 128:(kt + 1) * 128],
                                   qTa[base:base + 32, c, qt * 128:(qt + 1) * 128], [("kTa", c, tik), ("qTa", c, tiq)], [("ps", pi)], tp=tp)
                            ACTF(eT[:, m, kg * 128:(kg + len(grp)) * 128], ps[:, 0:len(grp) * 128], AF.Exp, [("ps", pi)], [("eT", m)], scale=32 ** -0.5)
                    for m in range(2):
                        acc = psl[4 + m]
                        for j, kt in enumerate(kts):
                            MM(acc[:, 0:65], eT[:, m, j * 128:(j + 1) * 128], va[:, kt, h, :], [("eT", m), "va"], [("ps", 4 + m)],
                               start=(j == 0), stop=(j == len(kts) - 1))
                    rr = fin[:, 0:2]; o = fin[:, 64:128]; ss = fin[:, 2:3]; junk = fin[:, 128:192]
                    A("dve", lambda e, rr=rr: e.reciprocal(rr[:, 0:1], psl[4][:, 64:65]), reads=[("ps", 4)], writes=["rr"])
                    A("dve", lambda e, rr=rr: e.reciprocal(rr[:, 1:2], psl[5][:, 64:65]), reads=[("ps", 5)], writes=["rr"])
                    TT("dve", rr[:, 1:2], rr[:, 1:2], neglam, ALU.mult, ["rr", "neglam"], ["rr"])
                    TS("dve", o, psl[4][:, 0:64], rr[:, 0:1], None, ALU.mult, None, [("ps", 4), "rr"], ["o"])
                    STT(o, psl[5][:, 0:64], rr[:, 1:2], o, ALU.mult, ALU.add, [("ps", 5), "rr", "o"], ["o"])
                    A("act", lambda e, junk=junk, o=o, ss=ss: e.activation(out=junk, in_=o, func=AF.Square, accum_out=ss), reads=["o"], writes=["ss", "junk"])
                    ACTF(ss, ss, AF.Sqrt, ["ss", "epsb"], ["ss"], scale=1.0 / 64, bias=epsb[:, 0:1])
                    A("dve", lambda e, ss=ss: e.reciprocal(ss, ss), reads=["ss"], writes=["ss"])
                    STT(yt[:, h * 64:(h + 1) * 64], o, ss, subw, ALU.mult, ALU.mult, ["o", "ss", "subw"], [ykey])
                for c in range(2):
                    pt = psl[6 + c]
                    TR(pt[:].bitcast(BF16)[:, 0:128], yt[:, c * 128:(c + 1) * 128], cB_("ident"), [ykey, "kB"], [("ps", 6 + c)])
                    A("dve", lambda e, pt=pt, c=c, qt=qt: e.tensor_copy(yAT[:, c, qt * 128:(qt + 1) * 128], pt[:].bitcast(BF16)[:, 0:128]),
                      reads=[("ps", 6 + c)], writes=[("yAT", tiq)])
            wout_part(l, 0, 2, yAT, [("yAT", i) for i in range(5)], [0, 1, 2, 3, 4] if need_ctx else [1, 2, 3, 4])
            P.barrier()

        def mixer_B(l, need_ctx):
            qTb = carve(0, 3456, BF16).rearrange("p (n c q) -> p n c q", n=NT, c=3)
            kTb = carve(3456, 1152, BF16)
            vb = carve(4608, 1170, BF16).rearrange("p (n h d) -> p n h d", n=NT, h=2)
            dsts = [lambda t0, tl, ti, c=c: (qTb[:, t0 // 128:(t0 + tl) // 128, c, :], ("qTb", c, ti)) for c in range(3)] + \
                   [lambda t0, tl, ti: (kTb[:, t0:t0 + tl], ("kTb", ti))]

            def lhs(cq, k):
                return w1b[0][:, k, cq * 128:(cq + 1) * 128]

            def loaderB():
                for i in range(3):
                    for g in range(2):
                        cc = OFF_QB + (g * 3 + i) * 64
                        A("pool", lambda e, i=i, g=g, cc=cc: e.dma_start(out=w1b[0][:, :, i * 128 + g * 64:i * 128 + g * 64 + 64],
                                                              in_=w_in[l, :, cc:cc + 64].rearrange("(k p) n -> p k n", p=128)),
                          writes=[("w1b", 0, 0), ("w1b", 0, 1)], dma="w1b00")
                A("pool", lambda e: e.dma_start(out=w1b[0][:, :, 384:512], in_=w_in[l, :, OFF_KB:OFF_KB + 128].rearrange("(k p) n -> p k n", p=128)),
                  writes=[("w1b", 0, 0), ("w1b", 0, 1)], dma="w1b00")
            normrope(l, OFF_QB, 4, 64, lhs, [2, 2, 2, 3], dsts, [0, 1, 2, 3, 4], "bd64", "permB", "cosB", "sinB", loader=loaderB)
            vproj(l, OFF_VB, 128, 2, vb, "vb")
            P.barrier()
            sm = carve(TMP0, 512)
            esink = sm[:, 0:6]
            ACTF(esink, pbc_(l, "b_sink"), AF.Exp, ["pb"], ["esink"])
            eTb = carve(TMP0 + 512, 960, BF16).rearrange("p (j n) -> p j n", j=5)
            ytok = [carve(TMP0 + 1536 + i * 192, 192, BF16) for i in range(2)]
            fin = carve(TMP0 + 2048, 128)
            yBT = carve(TMP0 + 2304, 3456, BF16).rearrange("p (c n) -> p c n", c=3)
            blocks = [(2 + n, n) for n in range(16)] + ([(0, -1), (1, -1)] if need_ctx else [])
            for bi, (qt, n) in enumerate(blocks):
                tiq = 0 if qt < 2 else 1 + (qt - 2) // 4
                kts = [(0, None), (1, None)]
                if n >= 0:
                    if n > 0:
                        kts.append((2 + n - 1, "mprev"))
                    kts.append((2 + n, None))
                    if n < 15:
                        kts.append((2 + n + 1, "mnext"))
                yt = ytok[bi % 2]; ykey = ("ytokb", bi % 2)
                for g in range(2):
                    for j, (kt, mk) in enumerate(kts):
                        tik = 0 if kt < 2 else 1 + (kt - 2) // 4
                        ps = psl[j % 4]
                        MM(ps[:, 0:384], kTb[g * 64:(g + 1) * 64, kt * 128:(kt + 1) * 128], qTb[g * 64:(g + 1) * 64, qt, :, :].rearrange("p c q -> p (c q)"),
                           [("kTb", tik)] + [("qTb", c, tiq) for c in range(3)], [("ps", j % 4)])
                        ACTF(eTb[:, j, :], ps[:, 0:384], AF.Exp, [("ps", j % 4)], [("eTb", j)], scale=0.125)
                        if mk is not None and not os.environ.get("NOMASK"):
                            for i3 in range(3):
                                TT("dve", eTb[:, j, i3 * 128:(i3 + 1) * 128], eTb[:, j, i3 * 128:(i3 + 1) * 128], cB_(mk), ALU.mult,
                                   [("eTb", j), "kB"], [("eTb", j)])
                    for i in range(3):
                        hq = g * 3 + i
                        pa = 4 + hq % 4
                        acc = psl[pa]
                        for j, (kt, mk) in enumerate(kts):
                            MM(acc[:, 0:65], eTb[:, j, i * 128:(i + 1) * 128], vb[:, kt, g, :], [("eTb", j), "vb"], [("ps", pa)],
                               start=(j == 0), stop=(j == len(kts) - 1))
                        den = fin[:, hq:hq + 1]
                        TS("dve", den, acc[:, 64:65], esink[:, hq:hq + 1], None, ALU.add, None, [("ps", pa), "esink"], [("den", hq)])
                        A("dve", lambda e, den=den: e.reciprocal(den, den), reads=[("den", hq)], writes=[("den", hq)])
                        TS("dve", yt[:, hq * 64:(hq + 1) * 64], acc[:, 0:64], den, None, ALU.mult, None, [("ps", pa), ("den", hq)], [ykey])
                for c in range(3):
                    pt = psl[c % 2]
                    TR(pt[:].bitcast(BF16)[:, 0:128], yt[:, c * 128:(c + 1) * 128], cB_("ident"), [ykey, "kB"], [("ps", c % 2)])
                    A("dve", lambda e, pt=pt, c=c, qt=qt: e.tensor_copy(yBT[:, c, qt * 128:(qt + 1) * 128], pt[:].bitcast(BF16)[:, 0:128]),
                      reads=[("ps", c % 2)], writes=[("yBT", tiq)])
            wout_part(l, 256, 3, yBT, [("yBT", i) for i in range(5)], [0, 1, 2, 3, 4] if need_ctx else [1, 2, 3, 4])
            P.barrier()

        def mixer_C(l, need_ctx):
            SMW = 216
            g_all = carve(0 * SMW, SMW).rearrange("p (d n h) -> p d n h", d=2, n=NT)
            beta = carve(1 * SMW, SMW).rearrange("p (d n h) -> p d n h", d=2, n=NT)
            negb = carve(2 * SMW, SMW).rearrange("p (d n h) -> p d n h", d=2, n=NT)
            beg = carve(3 * SMW, SMW).rearrange("p (d n h) -> p d n h", d=2, n=NT)
            eG = carve(4 * SMW, SMW).rearrange("p (d n h) -> p d n h", d=2, n=NT)
            kds = carve(5 * SMW, SMW).rearrange("p (d n h) -> p d n h", d=2, n=NT)
            glv = carve(6 * SMW, SMW).rearrange("p (d n h) -> p d n h", d=2, n=NT)
            QT = carve(1512, 1152, BF16); KT = carve(2664, 1152, BF16)
            KTOK = carve(3816, 1152, BF16).rearrange("p (n c) -> p n c", n=NT)
            VTOK = carve(4968, 1152, BF16).rearrange("p (n c) -> p n c", n=NT)
            W0 = 6120
            load_win(l, OFF_A, 24, 1)
            WK1 = [("w1b", 1, 0), ("w1b", 1, 1)]
            for n_ in range(NT):
                b = n_ % 2
                ti = 0 if n_ < 2 else 1 + (n_ - 2) // 4
                ps = psl[6 + b]
                for k in range(8):
                    MM(ps[:, 0:24], hn[:, k, n_ * 128:(n_ + 1) * 128], w1b[1][:, k, 0:24], WK1 + ks("hn", k, ti), [("ps", 6 + b)], start=(k == 0), stop=(k == 7))
                TT("dve", g_all[:, :, n_, :], ps[:, 0:12].rearrange("p (d h) -> p d h", d=2),
                   pbc_(l, "c_dt_bias").rearrange("p (d h) -> p d h", d=2), ALU.add, [("ps", 6 + b), "pb"], ["g_all"])
                A("dve", lambda e, ps=ps, n_=n_: e.tensor_copy(beta[:, :, n_, :], ps[:, 12:24].rearrange("p (d h) -> p d h", d=2)),
                  reads=[("ps", 6 + b)], writes=["beta"])
            gf = carve(0, SMW); bf_ = carve(SMW, SMW)
            tmpA = carve(W0, SMW); nar = carve(W0 + 256, 12); Gc = carve(W0 + 512, SMW); Gl = carve(W0 + 768, SMW)
            ACTF(gf, gf, AF.Exp, ["g_all"], ["g_all"])
            ACTF(gf, gf, AF.Ln, ["g_all", "oneb"], ["g_all"], bias=oneb[:, 0:1])
            ACTF(nar, pbc_(l, "c_A_log"), AF.Exp, ["pb"], ["nar"])
            TS("dve", nar, nar, -1.0, None, ALU.mult, None, ["nar"], ["nar"])
            for n_ in range(NT):
                TT("dve", g_all[:, :, n_, :], g_all[:, :, n_, :], nar.rearrange("p (d h) -> p d h", d=2), ALU.mult, ["g_all", "nar"], ["g_all"])
            ACTF(bf_, bf_, AF.Exp, ["beta"], ["beta"], scale=-1.0)
            TS("dve", bf_, bf_, 1.0, None, ALU.add, None, ["beta"], ["beta"])
            A("dve", lambda e: e.reciprocal(bf_, bf_), reads=["beta"], writes=["beta"])
            TS("dve", carve(2 * SMW, SMW), bf_, -1.0, None, ALU.mult, None, ["beta"], ["negb"])
            onesF = carve(W0 + 1024, 128)
            A("dve", lambda e: e.memset(onesF, 1.0), writes=["onesF"])
            for d in range(2):
                MM(psl[0][:, d * 108:(d + 1) * 108], cF_("tri_f" if d == 0 else "tri_b"), gf[:, d * 108:(d + 1) * 108], ["g_all", "kF"], [("ps", 0)])
            MM(psl[1][:, 0:SMW], onesF, gf, ["g_all", "onesF"], [("ps", 1)])
            A("dve", lambda e: e.tensor_copy(Gc, psl[0][:, 0:SMW]), reads=[("ps", 0)], writes=["Gc"])
            A("dve", lambda e: e.tensor_copy(Gl, psl[1][:, 0:SMW]), reads=[("ps", 1)], writes=["Gl"])
            ACTF(carve(4 * SMW, SMW), Gc, AF.Exp, ["Gc"], ["eG"])
            TT("dve", tmpA, Gl, Gc, ALU.subtract, ["Gl", "Gc"], ["tmpA"])
            ACTF(carve(5 * SMW, SMW), tmpA, AF.Exp, ["tmpA"], ["kds"])
            ACTF(Gl, Gl, AF.Exp, ["Gl"], ["Gl"])
            TT("dve", carve(3 * SMW, SMW), bf_, carve(4 * SMW, SMW), ALU.mult, ["beta", "eG"], ["beg"])
            Glv = Gl.rearrange("p (d n hp two) -> p d n hp two", d=2, n=NT, two=2)
            for hh in range(2):
                A("dve", lambda e, hh=hh: e.tensor_copy(glv[hh * 64:(hh + 1) * 64, :, :, 0:3], Glv[hh * 64:(hh + 1) * 64, :, :, :, hh]),
                  reads=["Gl"], writes=["glv"])
            P.barrier()
            CST = float(os.environ.get('CSTOP', '99'))
            if CST <= 0:
                return
            order = [list(range(NT)), [1, 0] + list(range(NT - 1, 1, -1))]
            for hp in range(3):
                raw = carve(W0, 2312); acc = carve(W0 + 2312, 2308); sqb = carve(W0 + 4620, 1154, BF16)
                for j3 in range(3):
                    cc = j3 * 3 + hp
                    A("pool", lambda e, j3=j3, cc=cc: e.dma_start(out=w1b[0][:, :, j3 * 128:(j3 + 1) * 128],
                                                            in_=w_in[l, :, OFF_C + cc * 128:OFF_C + (cc + 1) * 128].rearrange("(k p) n -> p k n", p=128)),
                      writes=[("w1b", 0, 0), ("w1b", 0, 1)], dma="w1b00")
                WK0 = [("w1b", 0, 0), ("w1b", 0, 1)]
                A("dve", lambda e: e.memset(raw, 0.0), writes=["raw"])
                for j3 in range(3):
                    cc = j3 * 3 + hp
                    for ti in range(5):
                        t0, tl = BT[ti]
                        b = ti % 2
                        ps = psl[b]
                        for k in range(8):
                            MM(ps[:, 0:tl], w1b[0][:, k, j3 * 128:(j3 + 1) * 128], hn[:, k, t0:t0 + tl], WK0 + ks("hn", k, ti), [("ps", b)], start=(k == 0), stop=(k == 7))
                        ro = 2 + t0 if ti == 0 else 6 + t0
                        ACTF(raw[:, ro:ro + tl], ps[:, 0:tl], AF.Copy, [("ps", b)], ["raw"])
                    cw = lambda k: pp[:, l * PPN + 4 + cc * 5 + k: l * PPN + 4 + cc * 5 + k + 1]
                    TS("dve", acc, raw[:, 0:2308], cw(0), None, ALU.mult, None, ["raw", "pp"], ["acc"])
                    for k in range(1, 5):
                        STT(acc, raw[:, k:k + 2308], cw(k), acc, ALU.mult, ALU.add, ["raw", "pp", "acc"], ["acc"])
                    ACTF(acc, acc, AF.Silu, ["acc"], ["acc"])
                    if j3 < 2:
                        ACTF(sqb, acc, AF.Square, ["acc"], ["sqb"])
                        dstT = QT if j3 == 0 else KT
                        dkey = "QT" if j3 == 0 else "KT"
                        for ti in range(5):
                            t0, tl = BT[ti]
                            a0 = t0 if ti == 0 else 4 + t0
                            b = ti % 2
                            ps2 = psl[2 + b]
                            rin = carve(W0 + 5776, 512)
                            MM(ps2[:, 0:tl], cB_("bd64"), sqb[:, a0:a0 + tl], ["sqb", "kB"], [("ps", 2 + b)])
                            ACTF(rin[:, 0:tl], ps2[:, 0:tl], AF.Sqrt, [("ps", 2 + b), "epsb"], ["rin"], bias=epsb[:, 0:1])
                            A("dve", lambda e, rin=rin, tl=tl: e.reciprocal(rin[:, 0:tl], rin[:, 0:tl]), reads=["rin"], writes=["rin"])
                            if j3 == 0:
                                STT(dstT[:, t0:t0 + tl], acc[:, a0:a0 + tl], 0.125, rin[:, 0:tl], ALU.mult, ALU.mult, ["acc", "rin"], [dkey])
                            else:
                                TT("dve", dstT[:, t0:t0 + tl], acc[:, a0:a0 + tl], rin[:, 0:tl], ALU.mult, ["acc", "rin"], [dkey])
                    else:
                        VT = carve(W0 + 4620, 1152, BF16)
                        ACTF(VT[:, 0:256], acc[:, 0:256], AF.Copy, ["acc"], ["sqb"])
                        ACTF(VT[:, 256:T], acc[:, 260:2308], AF.Copy, ["acc"], ["sqb"])
                    if j3 >= 1:
                        srcT = KT if j3 == 1 else carve(W0 + 4620, 1152, BF16)
                        skey = "KT" if j3 == 1 else "sqb"
                        dtok = KTOK if j3 == 1 else VTOK
                        tkey = "KTOK" if j3 == 1 else "VTOK"
                        for n_ in range(NT):
                            b = n_ % 2
                            pt = psl[4 + b]
                            TR(pt[:].bitcast(BF16)[:, 0:128], srcT[:, n_ * 128:(n_ + 1) * 128], cB_("ident"), [skey, "kB"], [("ps", 4 + b)])
                            if b == 0:
                                A("dve", lambda e, pt=pt, n_=n_, dtok=dtok: e.tensor_copy(dtok[:, n_, :], pt[:].bitcast(BF16)[:, 0:128]),
                                  reads=[("ps", 4 + b)], writes=[tkey])
                            else:
                                ACTF(dtok[:, n_, :], pt[:].bitcast(BF16)[:, 0:128], AF.Copy, [("ps", 4 + b)], [tkey])
                P.barrier()
                if CST <= 1:
                    continue
                S0 = carve(W0, 512); S1 = carve(W0 + 512, 512); S2 = carve(W0 + 1024, 512); S3 = carve(W0 + 1536, 512); S4 = carve(W0 + 2048, 512)
                QKT = carve(W0 + 2560, 256, BF16); bv = carve(W0 + 2816, 128, BF16); bk = carve(W0 + 2944, 128, BF16); kd = carve(W0 + 3072, 128, BF16)
                u = carve(W0 + 3200, 256); wT = carve(W0 + 3456, 128, BF16).rearrange("p (d i) -> p d i", d=2)
                vnew = carve(W0 + 3712, 128, BF16); Sst = carve(W0 + 3840, 128).rearrange("p (d v) -> p d v", d=2)
                Sb = carve(W0 + 3968, 64, BF16).rearrange("p (d v) -> p d v", d=2)
                Rb = carve(W0 + 4032, 256, BF16)
                oacc = carve(W0 + 4288, 2304).rearrange("p (n c) -> p n c", n=NT)
                A("dve", lambda e: e.memset(oacc, 0.0), writes=[("oacc", 0), ("oacc", 1)])
                A("dve", lambda e: e.memset(Sst, 0.0), writes=[("S", 0), ("S", 1)])
                A("dve", lambda e: e.memset(Sb, 0.0), writes=[("Sb", 0), ("Sb", 1)])
                sl = lambda qi: slice(qi * 128, (qi + 1) * 128)
                CSTEP = int(os.environ.get('CSTEP', '99'))
                for s in range(min(NT, CSTEP)):
                    tl_ = [order[0][s], order[1][s]]
                    bfv = lambda off: carve(off, 256, BF16)
                    XB, QKB = bfv(W0 + 2048), bfv(W0 + 2304)
                    YB, TMB = bfv(W0 + 0), bfv(W0 + 256)
                    XcB, YcB = bfv(W0 + 512), bfv(W0 + 768)
                    XnB, YnB = bfv(W0 + 1024), bfv(W0 + 1280)
                    P1B, Q1B = bfv(W0 + 1536), bfv(W0 + 1792)
                    stages = []

                    def st(f):
                        stages.append(f)
                        return f

                    def mk(hh):
                        B = [psl[hh], psl[2 + hh], psl[4 + hh], psl[6 + hh]]
                        BK = [("ps", hh), ("ps", 2 + hh), ("ps", 4 + hh), ("ps", 6 + hh)]
                        hc = slice(hh * 256, (hh + 1) * 256)
                        h2 = slice(hh * 128, (hh + 1) * 128)
                        r64 = slice(hh * 64, (hh + 1) * 64)
                        c64 = r64
                        K = lambda n: (n, hh)
                        dl = lambda d: slice(d * 128, (d + 1) * 128)
                        qs = lambda d: slice((hh * 2 + d) * 128, (hh * 2 + d + 1) * 128)
                        q64 = lambda d: slice((hh * 2 + d) * 64, (hh * 2 + d + 1) * 64)
                        h = hp * 2 + hh

                        def s1():
                            for d in range(2):
                                t = tl_[d]
                                TS("dve", S0[:, qs(d)], cF_("msk4")[:, qs(d)], g_all[:, d, t, h:h + 1], None, ALU.mult, None, ["kF", "g_all"], [K("S0"), K("S0a"), K("S0b")])
                            for d in range(2):
                                MM(B[2][:, dl(d)], cF_("tri_f" if d == 0 else "tri_b"), S0[:, qs(d)], [K("S0"), "kF"], [BK[2]])
                            for d in range(2):
                                t = tl_[d]
                                kTs = KT[r64, t * 128:(t + 1) * 128]
                                MM(B[0][:, dl(d)], kTs, kTs, ["KT"], [BK[0]])
                                MM(B[1][:, dl(d)], QT[r64, t * 128:(t + 1) * 128], kTs, ["KT", "QT"], [BK[1]])

                        def s2():
                            ACTF(S1[:, hc], B[2][:, 0:256], AF.Exp, [BK[2]], [K("S1"), K("S1a"), K("S1b")])

                        def s3():
                            TT("dve", S2[:, hc], S1[:, hc], cF_("strict4")[:, hc], ALU.mult, [K("S1"), "kF"], [K("S2"), K("S2a"), K("S2b")])
                            TT("dve", S3[:, hc], S1[:, hc], cF_("incl4")[:, hc], ALU.mult, [K("S1"), "kF"], [K("S3"), K("S3a"), K("S3b")])
                            for d in range(2):
                                t = tl_[d]
                                STT(XB[:, qs(d)], B[0][:, dl(d)], negb[:, d, t, h:h + 1], S2[:, qs(d)], ALU.mult, ALU.mult, [BK[0], "negb", K("S2")], [K("S4a")])
                            TT("dve", QKB[:, hc], B[1][:, 0:256], S3[:, hc], ALU.mult, [BK[1], K("S3")], [K("S4b")])

                        def s4():
                            for d in range(2):
                                MM(B[3][:, dl(d)], XB[:, qs(d)], cB_("ident"), [K("S4a"), "kB"], [BK[3]])
                                MM(B[2][:, dl(d)], QKB[:, qs(d)], cB_("ident"), [K("S4b"), "kB"], [BK[2]])

                        def s5():
                            A("dve", lambda e: e.tensor_copy(YB[:, hc], B[3][:, 0:256]), reads=[BK[3], K("S0")], writes=[K("S0a")])
                            ACTF(QKT[:, hc], B[2][:, 0:256], AF.Copy, [BK[2]], [K("QKT")])
                            TT("dve", XcB[:, hc], XB[:, hc], cF_("bd32_4")[:, hc], ALU.mult, [K("S4a"), "kF", K("S1")], [K("S1a")])
                            TT("dve", YcB[:, hc], YB[:, hc], cF_("bd32_4")[:, hc], ALU.mult, [K("S0a"), "kF", K("S1")], [K("S1b")])
                            TT("dve", Rb[:, hc], YcB[:, hc], cF_("ident4")[:, hc], ALU.add, [K("S1b"), "kF"], [K("Rb")])
                            TT("dve", TMB[:, hc], XcB[:, hc], cF_("ident4")[:, hc], ALU.add, [K("S1a"), "kF", K("S0")], [K("S0b")])

                        fl = [s1, s2, s3, s4, s5]
                        cur = {"Xc": XcB, "Yc": YcB, "Xk": K("S1a"), "Yk": K("S1b"), "Xn": XnB, "Yn": YnB, "Xnk": K("S2a"), "Ynk": K("S2b")}

                        def base_mm():
                            for d in range(2):
                                MM(B[0][:, dl(d)], cur["Yc"][:, qs(d)], cur["Xc"][:, qs(d)], [cur["Xk"], cur["Yk"]], [BK[0]])
                            for d in range(2):
                                MM(B[1][:, dl(d)], cur["Xc"][:, qs(d)], cur["Yc"][:, qs(d)], [cur["Xk"], cur["Yk"]], [BK[1]])

                        def base_cp():
                            Xn_, Yn_ = cur["Xn"], cur["Yn"]
                            ACTF(Xn_[:, hc], B[0][:, 0:256], AF.Copy, [BK[0], K("S2")], [cur["Xnk"]])
                            A("dve", lambda e, Yn_=Yn_: e.tensor_copy(Yn_[:, hc], B[1][:, 0:256]), reads=[BK[1], K("S2")], writes=[cur["Ynk"]])

                        def base_up():
                            for d in range(2):
                                MM(B[2][:, dl(d)], cur["Xn"][:, qs(d)], Rb[:, qs(d)], [cur["Xnk"], K("Rb")], [BK[2]])
                            for d in range(2):
                                MM(B[3][:, dl(d)], cur["Yn"][:, qs(d)], TMB[:, qs(d)], [cur["Ynk"], K("S0b")], [BK[3]])

                        def base_add():
                            TT("dve", Rb[:, hc], Rb[:, hc], B[2][:, 0:256], ALU.add, [K("Rb"), BK[2]], [K("Rb")])
                            TT("dve", TMB[:, hc], TMB[:, hc], B[3][:, 0:256], ALU.add, [K("S0b"), BK[3]], [K("S0b")])
                            cur["Xc"], cur["Xn"] = cur["Xn"], cur["Xc"]; cur["Xk"], cur["Xnk"] = cur["Xnk"], cur["Xk"]
                            cur["Yc"], cur["Yn"] = cur["Yn"], cur["Yc"]; cur["Yk"], cur["Ynk"] = cur["Ynk"], cur["Yk"]

                        for kk_ in range(4):
                            fl += [base_mm, base_cp, base_up, base_add]

                        def lvl_fns(mname, need_tm):
                            def l1():
                                TT("dve", XcB[:, hc], XB[:, hc], cF_(mname)[:, hc], ALU.mult, [K("S4a"), "kF"], [K("S1a")])
                                if need_tm:
                                    TT("dve", YcB[:, hc], YB[:, hc], cF_(mname)[:, hc], ALU.mult, [K("S0a"), "kF"], [K("S1b")])

                            def l2():
                                for d in range(2):
                                    MM(B[0][:, dl(d)], XcB[:, qs(d)], Rb[:, qs(d)], [K("S1a"), K("Rb")], [BK[0]])
                                if need_tm:
                                    for d in range(2):
                                        MM(B[1][:, dl(d)], YcB[:, qs(d)], TMB[:, qs(d)], [K("S1b"), K("S0b")], [BK[1]])

                            def l3():
                                ACTF(P1B[:, hc], B[0][:, 0:256], AF.Copy, [BK[0], K("S3")], [K("S3a")])
                                if need_tm:
                                    A("dve", lambda e: e.tensor_copy(Q1B[:, hc], B[1][:, 0:256]), reads=[BK[1], K("S3")], writes=[K("S3b")])

                            def l4():
                                for d in range(2):
                                    MM(B[2][:, dl(d)], TMB[:, qs(d)], P1B[:, qs(d)], [K("S0b"), K("S3a")], [BK[2]])
                                if need_tm:
                                    for d in range(2):
                                        MM(B[3][:, dl(d)], Rb[:, qs(d)], Q1B[:, qs(d)], [K("Rb"), K("S3b")], [BK[3]])

                            def l5():
                                TT("dve", Rb[:, hc], Rb[:, hc], B[2][:, 0:256], ALU.add, [K("Rb"), BK[2]], [K("Rb")])
                                if need_tm:
                                    TT("dve", TMB[:, hc], TMB[:, hc], B[3][:, 0:256], ALU.add, [K("S0b"), BK[3]], [K("S0b")])
                            return [l1, l2, l3, l4, l5]
                        fl += lvl_fns("m1_4", True) + lvl_fns("m2_4", False)

                        def t1():
                            for d in range(2):
                                t = tl_[d]
                                TS("dve", bv[:, q64(d)], VTOK[:, t, c64], beta[:, d, t, h:h + 1], None, ALU.mult, None, ["VTOK", "beta"], [K("bv")])
                                ACTF(bk[:, q64(d)], KTOK[:, t, c64], AF.Identity, ["KTOK", "beg"], [K("bk")], scale=beg[:, d, t, h:h + 1])
                                ACTF(kd[:, q64(d)], KTOK[:, t, c64], AF.Identity, ["KTOK", "kds"], [K("kd")], scale=kds[:, d, t, h:h + 1])
                            for d in range(2):
                                MM(B[0][:, d * 64:(d + 1) * 64], Rb[:, qs(d)], bv[:, q64(d)], [K("Rb"), K("bv")], [BK[0]])
                            for d in range(2):
                                MM(B[1][r64, dl(d)], bk[:, q64(d)], Rb[:, qs(d)], [K("Rb"), K("bk")], [BK[1]], tp=(0, hh * 64))

                        def t2():
                            A("dve", lambda e: e.tensor_copy(u[:, h2], B[0][:, 0:128]), reads=[BK[0]], writes=[K("u")])
                            ACTF(wT[r64, :, :], B[1][r64, 0:256].rearrange("p (d i) -> p d i", d=2), AF.Copy, [BK[1]], [K("wT")])

                        def t3():
                            for d in range(2):
                                MM(B[2][:, d * 64:(d + 1) * 64], wT[r64, d, :], Sb[r64, d, :], [K("wT"), K("Sb")], [BK[2]])

                        def t4():
                            TT("dve", vnew[:, h2], u[:, h2], B[2][:, 0:128], ALU.subtract, [K("u"), BK[2]], [K("vnew")])

                        def t5():
                            for d in range(2):
                                t = tl_[d]
                                MM(B[3][:, d * 64:(d + 1) * 64], QT[r64, t * 128:(t + 1) * 128], Sb[r64, d, :], ["QT", K("Sb")], [BK[3]])
                            for d in range(2):
                                MM(B[0][:, d * 64:(d + 1) * 64], QKT[:, qs(d)], vnew[:, q64(d)], [K("QKT"), K("vnew")], [BK[0]])
                            for d in range(2):
                                MM(B[1][r64, d * 64:(d + 1) * 64], kd[:, q64(d)], vnew[:, q64(d)], [K("kd"), K("vnew")], [BK[1]], tp=(0, hh * 64))

                        def t6():
                            for d in range(2):
                                t = tl_[d]
                                STT(oacc[:, t, c64], B[3][:, d * 64:(d + 1) * 64], eG[:, d, t, h:h + 1], oacc[:, t, c64], ALU.mult, ALU.add,
                                    [BK[3], "eG", K("oacc")], [K("oacc")])
                                TT("dve", oacc[:, t, c64], oacc[:, t, c64], B[0][:, d * 64:(d + 1) * 64], ALU.add, [BK[0], K("oacc")], [K("oacc")])
                            for d in range(2):
                                t = tl_[d]
                                STT(Sst[r64, d, :], Sst[r64, d, :], glv[r64, d, t, hp:hp + 1], B[1][r64, d * 64:(d + 1) * 64], ALU.mult, ALU.add,
                                    [K("S"), "glv", BK[1]], [K("S")])
                            ACTF(Sb[r64, :, :], Sst[r64, :, :], AF.Copy, [K("S")], [K("Sb")])
                        fl += [t1, t2, t3, t4, t5, t6]
                        return fl

                    f0, f1 = mk(0), mk(1)
                    for fa, fb in zip(f0, f1):
                        fa()
                        fb()
                P.barrier()
                if CSTEP < 99:
                    return
                G = carve(W0, 2304).rearrange("p (n c) -> p n c", n=NT)
                ssq = carve(W0 + 2304, 36); junk = carve(W0 + 2368, 64); y1 = carve(W0 + 2432, 64)
                ytk = carve(W0 + 2560, 1152, BF16).rearrange("p (n c) -> p n c", n=NT)
                yCT = carve(1512, 1152, BF16).rearrange("p (c n) -> p c n", c=1)
                load_win(l, OFF_G + hp * 128, 128, 1)
                for n_ in range(NT):
                    b = n_ % 2
                    ti = 0 if n_ < 2 else 1 + (n_ - 2) // 4
                    ps = psl[6 + b]
                    for k in range(8):
                        MM(ps[:, 0:128], hn[:, k, n_ * 128:(n_ + 1) * 128], w1b[1][:, k, 0:128], WK1 + ks("hn", k, ti), [("ps", 6 + b)], start=(k == 0), stop=(k == 7))
                    ACTF(G[:, n_, :], ps[:, 0:128], AF.Silu, [("ps", 6 + b)], ["G"])
                for n_ in range(NT):
                    for hh in range(2):
                        c64 = slice(hh * 64, (hh + 1) * 64)
                        ix = n_ * 2 + hh
                        A("act", lambda e, n_=n_, c64=c64, ix=ix: e.activation(out=junk, in_=oacc[:, n_, c64], func=AF.Square, accum_out=ssq[:, ix:ix + 1]),
                          reads=[("oacc", 0), ("oacc", 1)], writes=["ssq", "junk"])
                ACTF(ssq, ssq, AF.Sqrt, ["ssq", "epsb"], ["ssq"], scale=1.0 / 64, bias=epsb[:, 0:1])
                A("dve", lambda e: e.reciprocal(ssq, ssq), reads=["ssq"], writes=["ssq"])
                for n_ in range(NT):
                    for hh in range(2):
                        c64 = slice(hh * 64, (hh + 1) * 64)
                        ix = n_ * 2 + hh
                        STT(y1, oacc[:, n_, c64], ssq[:, ix:ix + 1], pbc_(l, "c_onorm"), ALU.mult, ALU.mult, [("oacc", 0), ("oacc", 1), "ssq", "pb"], ["y1"])
                        TT("dve", ytk[:, n_, c64], y1, G[:, n_, c64], ALU.mult, ["y1", "G"], ["ytk"])
                for n_ in range(NT):
                    b = n_ % 2
                    pt = psl[4 + b]
                    TR(pt[:].bitcast(BF16)[:, 0:128], ytk[:, n_, :], cB_("ident"), ["ytk", "kB"], [("ps", 4 + b)])
                    A("dve", lambda e, pt=pt, n_=n_: e.tensor_copy(yCT[:, 0, n_ * 128:(n_ + 1) * 128], pt[:].bitcast(BF16)[:, 0:128]),
                      reads=[("ps", 4 + b)], writes=["yCT"])
                if not os.environ.get('CSKIPW'):
                    wout_part(l, 640 + hp * 128, 1, yCT, ["yCT"], [0, 1, 2, 3, 4] if need_ctx else [1, 2, 3, 4])
                P.barrier()
                if int(os.environ.get('CHP', '99')) <= hp + 1:
                    return

        stages = []
        ALLT = [0, 1, 2, 3, 4]
        LAT = [1, 2, 3, 4]
        for l in range(DEPTH):
            last = (l == DEPTH - 1)
            ffn(l, 0, f1w1, f1w2, ALLT)
            if upto in ("ffn1", "ffn1_%d" % l):
                break
            norm_mod(l, 3, ALLT)
            P.barrier()
            mixer_A(l, not last)
            if upto in ('mixA', 'mixA_%d' % l):
                break
            mixer_B(l, not last)
            if upto in ('mixAB', 'mixAB_%d' % l):
                break
            mixer_C(l, not last)
            if upto in ('mix', 'mix_%d' % l):
                break
            ffn(l, 6, f2w1, f2w2, LAT if last else ALLT)
            if upto in ('l0', 'ffn2_%d' % l):
                break

        P.barrier()
        if dbg:
            A("sp", lambda e: e.dma_start(out=dbg2, in_=arena[:]), writes=[("out", "d2")], dma="st_d2")
            for c in range(8):
                A("sp", lambda e, c=c: e.dma_start(out=dbgT[c * 128:(c + 1) * 128, :], in_=hT[:, c, :]),
                  reads=ks("hT", c, range(5)), writes=[("out", "d", c)], dma="st_d%d" % c)
        for c in range(8):
            A("sp", lambda e, c=c: e.dma_start(out=outT[c * 128:(c + 1) * 128, :], in_=hT[:, c, NCTX:T]),
              reads=ks("hT", c, range(5)), writes=[("out", c)], dma="st_o%d" % c)
        fin = Op()
        fin.eng = "sp"; fin.fn = None; fin.dma = None; fin.rk = set(); fin.wk = set(); fin.signal = False; fin.count = 0; fin.semkey = "sp"
        fin.deps = [op for op in P.ops if op.dma is not None and op.dma.startswith("st_")]
        for d in fin.deps:
            d.signal = True
        P.ops.append(fin)
        P.emit(stack)
    return nc, cF, cB


def make_inputs(inputs, b, cF, cB):
    f = lambda a: np.ascontiguousarray(np.asarray(a, dtype=np.float32))
    x = f(inputs["x"]); ctx = f(inputs["ctx"]); c = f(inputs["c"]); c_ctx = f(inputs["c_ctx"])
    m = {}
    m["xT"] = np.ascontiguousarray(np.concatenate([ctx[b].T, x[b].T], axis=1))
    cT = np.concatenate([c[b].reshape(8, 128).T, c_ctx.reshape(8, 128).T], axis=1)
    m["cT"] = np.ascontiguousarray(cT)
    m["w_mod"] = f(inputs["w_mod"])
    bm = f(inputs["b_mod"])
    m["b_modT"] = np.ascontiguousarray(bm.reshape(DEPTH, 72, 128).transpose(2, 0, 1).reshape(128, DEPTH * 72))
    for n in ["ffn1_w1", "ffn1_w2", "ffn2_w1", "ffn2_w2", "w_in", "w_out"]:
        m[n] = f(inputs[n])
    rows = []
    for l in range(DEPTH):
        rows.append(np.concatenate([f(inputs[n])[l].reshape(-1) for n in
                                    ["a_qnorm", "a_knorm", "a_lambda", "a_subln", "b_qnorm", "b_knorm", "b_sink", "c_A_log", "c_dt_bias", "c_onorm"]]))
    row = np.concatenate(rows)
    m["pbc"] = np.ascontiguousarray(np.broadcast_to(row[None, :], (128, row.size)))
    ppl = []
    p = np.arange(128)
    for l in range(DEPTH):
        cols = [f(inputs["a_qnorm"])[l][p % 32], f(inputs["a_knorm"])[l][p % 32], f(inputs["b_qnorm"])[l][p % 64], f(inputs["b_knorm"])[l][p % 64]]
        cv = f(inputs["c_conv"])[l]
        for cc in range(9):
            for k in range(5):
                cols.append(cv[k, cc * 128 + p])
        ppl.append(np.stack(cols, axis=1))
    m["ppar"] = np.ascontiguousarray(np.concatenate(ppl, axis=1))
    m["constF"] = cF
    m["constB"] = cB.astype(ml_dtypes.bfloat16)
    return m


_CACHE = {}


def kernel(**inputs):
    if "nc" not in _CACHE:
        _CACHE["nc"] = build()
    nc, cF, cB = _CACHE["nc"]
    in_maps = [make_inputs(inputs, b, cF, cB) for b in range(8)]
    res = run_bass_kernel_spmd(nc, in_maps, core_ids=list(range(8)))
    out = np.stack([np.ascontiguousarray(res.results[b]["outT"].T) for b in range(8)], axis=0)
    return out.astype(np.float32)
```
